# Optimizing a Trainium2 kernel written in Bass

```python
import jax
import jax.numpy as jnp
from jax import lax
import numpy as np

D_MODEL = 2048
BATCH = 4
SEQ = 4096
DEPTH = 1

HEAD_DIM = 128
MOBA_HEADS = D_MODEL // (2 * HEAD_DIM)
RET_HEADS = D_MODEL // (2 * HEAD_DIM)
MOBA_WIDTH = MOBA_HEADS * HEAD_DIM
RET_WIDTH = RET_HEADS * HEAD_DIM
MIX_WIDTH = MOBA_WIDTH + RET_WIDTH
MOBA_BLOCK = 256
MOBA_TOPK = 3
MOBA_QCHUNK = 64
RET_CHUNK = 256
ROPE_THETA = 10000.0
D_FF = 256 * ((8 * D_MODEL // 3 + 255) // 256)
CONV_WIDTH = 3
NORM_EPS = 1e-6
GN_EPS = 1e-5
IN_SPLITS = (MOBA_WIDTH, 2 * MOBA_WIDTH, 3 * MOBA_WIDTH,
             3 * MOBA_WIDTH + RET_WIDTH, 3 * MOBA_WIDTH + 2 * RET_WIDTH,
             3 * MOBA_WIDTH + 3 * RET_WIDTH)
IN_COLS = 3 * MOBA_WIDTH + 4 * RET_WIDTH

kernel_name = "hymba_moba_retnet_convffn_layer"


def rms_norm(x, w):
    xf = x.astype(jnp.float32)
    y = xf * lax.rsqrt(jnp.mean(xf * xf, axis=-1, keepdims=True) + NORM_EPS)
    return (y * w.astype(jnp.float32)).astype(x.dtype)


def rope_tables(seq, dim, dtype):
    pos = jnp.arange(seq, dtype=jnp.float32)
    inv = 1.0 / (ROPE_THETA ** (jnp.arange(0, dim, 2, dtype=jnp.float32) / dim))
    ang = pos[:, None] * inv[None, :]
    return jnp.cos(ang).astype(dtype), jnp.sin(ang).astype(dtype)


def apply_rope(t, cos, sin):
    t1, t2 = jnp.split(t, 2, axis=-1)
    return jnp.concatenate([t1 * cos - t2 * sin, t1 * sin + t2 * cos], axis=-1)


def pad_seq(t, s_pad):
    return jnp.pad(t, ((0, 0), (0, 0), (0, s_pad - t.shape[2]), (0, 0)))


def moba_attention(q, k, v):
    b, h, s, d = q.shape
    nb = -(-s // MOBA_BLOCK)
    s_pad = nb * MOBA_BLOCK
    q, k, v = pad_seq(q, s_pad), pad_seq(k, s_pad), pad_seq(v, s_pad)
    scale = d ** -0.5
    k_blocks = k.reshape(b, h, nb, MOBA_BLOCK, d)
    v_blocks = v.reshape(b, h, nb, MOBA_BLOCK, d)
    k_mean = jnp.mean(k_blocks.astype(jnp.float32), axis=3).astype(q.dtype)
    gate = jnp.einsum('bhsd,bhnd->bhsn', q, k_mean, preferred_element_type=jnp.float32)
    q_blk = jnp.arange(s_pad) // MOBA_BLOCK
    past = jnp.arange(nb)[None, :] < q_blk[:, None]
    gate = jnp.where(past, gate, -jnp.inf)
    n_sel = min(MOBA_TOPK, nb)
    _, sel = lax.top_k(gate, n_sel)

    nc = s_pad // MOBA_QCHUNK
    q_chunks = jnp.moveaxis(q.reshape(b, h, nc, MOBA_QCHUNK, d), 2, 0)
    sel_chunks = jnp.moveaxis(sel.reshape(b, h, nc, MOBA_QCHUNK, n_sel), 2, 0)
    gather = jax.vmap(jax.vmap(lambda blocks, idx: blocks[idx]))

    def chunk(args):
        c, q_c, sel_c = args
        own = (c * MOBA_QCHUNK) // MOBA_BLOCK
        qpos = c * MOBA_QCHUNK + jnp.arange(MOBA_QCHUNK)
        kpos = own * MOBA_BLOCK + jnp.arange(MOBA_BLOCK)
        k_own = lax.dynamic_index_in_dim(k_blocks, own, axis=2, keepdims=False)
        v_own = lax.dynamic_index_in_dim(v_blocks, own, axis=2, keepdims=False)
        s_own = jnp.einsum('bhqd,bhkd->bhqk', q_c, k_own,
                           preferred_element_type=jnp.float32) * scale
        scores = [jnp.where(kpos[None, :] <= qpos[:, None], s_own, -jnp.inf)]
        for slot in range(n_sel):
            k_g = gather(k_blocks, sel_c[..., slot])
            s_g = jnp.einsum('bhqd,bhqkd->bhqk', q_c, k_g,
                             preferred_element_type=jnp.float32) * scale
            scores.append(jnp.where(slot < own, s_g, -jnp.inf))
        p = jax.nn.softmax(jnp.concatenate(scores, axis=-1), axis=-1).astype(v.dtype)
        p = p.reshape(b, h, MOBA_QCHUNK, n_sel + 1, MOBA_BLOCK)
        out = jnp.einsum('bhqk,bhkd->bhqd', p[:, :, :, 0], v_own,
                         preferred_element_type=jnp.float32)
        for slot in range(n_sel):
            v_g = gather(v_blocks, sel_c[..., slot])
            out = out + jnp.einsum('bhqk,bhqkd->bhqd', p[:, :, :, slot + 1], v_g,
                                   preferred_element_type=jnp.float32)
        return out.astype(v.dtype)

    out = lax.map(chunk, (jnp.arange(nc), q_chunks, sel_chunks))
    out = jnp.moveaxis(out, 0, 2).reshape(b, h, s_pad, d)
    return out[:, :, :s]


def retention(q, k, v):
    b, h, s, d = q.shape
    nc = -(-s // RET_CHUNK)
    s_pad = nc * RET_CHUNK
    q, k, v = pad_seq(q, s_pad), pad_seq(k, s_pad), pad_seq(v, s_pad)
    k = k * (d ** -0.5)
    log_g = jnp.log(1.0 - 2.0 ** (-5.0 - jnp.arange(h, dtype=jnp.float32)))
    idx = jnp.arange(RET_CHUNK, dtype=jnp.float32)
    rel = idx[:, None] - idx[None, :]
    decay_in = jnp.where(rel >= 0, jnp.exp(log_g[:, None, None] * jnp.maximum(rel, 0.0)), 0.0)
    xi = jnp.exp(log_g[:, None] * (idx + 1.0))
    zeta = jnp.exp(log_g[:, None] * (RET_CHUNK - 1.0 - idx))
    g_chunk = jnp.exp(log_g * RET_CHUNK)
    qc = q.reshape(b, h, nc, RET_CHUNK, d)
    kc = k.reshape(b, h, nc, RET_CHUNK, d)
    vc = v.reshape(b, h, nc, RET_CHUNK, d)
    inner = jnp.einsum('bhnqd,bhnkd->bhnqk', qc, kc,
                       preferred_element_type=jnp.float32) * decay_in[None, :, None]
    y_in = jnp.einsum('bhnqk,bhnkd->bhnqd', inner.astype(vc.dtype), vc,
                      preferred_element_type=jnp.float32)
    kv = jnp.einsum('bhnkd,bhnke->bhnde', kc.astype(jnp.float32) * zeta[None, :, None, :, None],
                    vc.astype(jnp.float32))

    def step(state, kv_n):
        return state * g_chunk[None, :, None, None] + kv_n, state

    _, prev = lax.scan(step, jnp.zeros((b, h, d, d), jnp.float32), jnp.moveaxis(kv, 2, 0))
    prev = jnp.moveaxis(prev, 0, 2)
    y_cross = jnp.einsum('bhnqd,bhnde->bhnqe', qc.astype(jnp.float32), prev) * xi[None, :, None, :, None]
    y = (y_in + y_cross).reshape(b, h, s_pad, d)
    return y[:, :, :s]


def hybrid_mixer(hn, w_in, ret_gn_w, ret_gn_b, w_out):
    b, s, _ = hn.shape
    proj = hn @ w_in
    q_a, k_a, v_a, q_r, k_r, v_r, g_r = jnp.split(proj, IN_SPLITS, axis=-1)

    def heads(t, n):
        return t.reshape(b, s, n, HEAD_DIM).transpose(0, 2, 1, 3)

    cos, sin = rope_tables(s, HEAD_DIM, hn.dtype)
    a = moba_attention(apply_rope(heads(q_a, MOBA_HEADS), cos, sin),
                       apply_rope(heads(k_a, MOBA_HEADS), cos, sin),
                       heads(v_a, MOBA_HEADS))
    a = a.transpose(0, 2, 1, 3).reshape(b, s, MOBA_WIDTH)
    r = retention(apply_rope(heads(q_r, RET_HEADS), cos, sin),
                  apply_rope(heads(k_r, RET_HEADS), cos, sin),
                  heads(v_r, RET_HEADS))
    mu = jnp.mean(r, axis=-1, keepdims=True)
    var = jnp.mean(jnp.square(r - mu), axis=-1, keepdims=True)
    r = ((r - mu) * lax.rsqrt(var + GN_EPS)).transpose(0, 2, 1, 3).reshape(b, s, RET_WIDTH)
    r = r * ret_gn_w.astype(jnp.float32) + ret_gn_b.astype(jnp.float32)
    r = (jax.nn.silu(g_r.astype(jnp.float32)) * r).astype(hn.dtype)
    return jnp.concatenate([a, r], axis=-1) @ w_out


def conv_glu_ffn(hn, w_up, conv_w, conv_b, w_down):
    u = hn @ w_up
    u_g, u_v = jnp.split(u, 2, axis=-1)
    u_g = lax.conv_general_dilated(
        u_g, conv_w[:, None, :].astype(u_g.dtype), window_strides=(1,),
        padding=[(CONV_WIDTH - 1, 0)], dimension_numbers=('NWC', 'WIO', 'NWC'),
        feature_group_count=D_FF) + conv_b
    return (jax.nn.silu(u_g) * u_v) @ w_down


def setup_inputs(seed: int = 0) -> dict:
    key = jax.random.key(seed)
    ks = jax.random.split(key, 13)
    f32 = jnp.float32

    def nrm(k, shape, scale):
        return jax.random.normal(k, shape, f32) * scale

    return {
        "x": nrm(ks[0], (BATCH, SEQ, D_MODEL), 1.0),
        "norm_mix_pre": 1.0 + nrm(ks[1], (DEPTH, D_MODEL), 0.05),
        "w_in": nrm(ks[2], (DEPTH, D_MODEL, IN_COLS), D_MODEL ** -0.5),
        "ret_gn_w": 1.0 + nrm(ks[3], (DEPTH, RET_WIDTH), 0.05),
        "ret_gn_b": nrm(ks[4], (DEPTH, RET_WIDTH), 0.02),
        "w_out": nrm(ks[5], (DEPTH, MIX_WIDTH, D_MODEL), MIX_WIDTH ** -0.5),
        "norm_mix_post": 1.0 + nrm(ks[6], (DEPTH, D_MODEL), 0.05),
        "norm_ffn_pre": 1.0 + nrm(ks[7], (DEPTH, D_MODEL), 0.05),
        "w_up": nrm(ks[8], (DEPTH, D_MODEL, 2 * D_FF), D_MODEL ** -0.5),
        "conv_w": nrm(ks[9], (DEPTH, CONV_WIDTH, D_FF), CONV_WIDTH ** -0.5),
        "conv_b": nrm(ks[10], (DEPTH, D_FF), 0.02),
        "w_down": nrm(ks[11], (DEPTH, D_FF, D_MODEL), D_FF ** -0.5),
        "norm_ffn_post": 1.0 + nrm(ks[12], (DEPTH, D_MODEL), 0.05),
    }


def reference(x, norm_mix_pre, w_in, ret_gn_w, ret_gn_b, w_out, norm_mix_post,
              norm_ffn_pre, w_up, conv_w, conv_b, w_down, norm_ffn_post):
    for layer in range(DEPTH):
        hn = rms_norm(x, norm_mix_pre[layer])
        mix = hybrid_mixer(hn, w_in[layer], ret_gn_w[layer], ret_gn_b[layer], w_out[layer])
        x = x + rms_norm(mix, norm_mix_post[layer])
        hn = rms_norm(x, norm_ffn_pre[layer])
        ff = conv_glu_ffn(hn, w_up[layer], conv_w[layer], conv_b[layer], w_down[layer])
        x = x + rms_norm(ff, norm_ffn_post[layer])
    return x
```

```python
import contextlib
import numpy as np
import ml_dtypes
import concourse.bass as bass
import concourse.mybir as mybir
from concourse.bass_utils import run_bass_kernel_spmd

F32 = mybir.dt.float32
BF16 = mybir.dt.bfloat16
ALU = mybir.AluOpType
AF = mybir.ActivationFunctionType
AX = mybir.AxisListType

N_DMA_SEMS = 40
D = 2048
NTP = 15
NTO = 17
DFF = 5632
NCT = DFF // 128
BIG = 30000.0
NEG = -1.0e30


class Op:
    __slots__ = ("idx", "eng", "fn", "deps", "is_dma", "signal", "count", "dsem", "dtarget")

    def __init__(self, idx, eng, fn, is_dma):
        self.idx = idx
        self.eng = eng
        self.fn = fn
        self.is_dma = is_dma
        self.deps = set()
        self.signal = False
        self.count = 0
        self.dsem = None
        self.dtarget = 0


class Prog:
    COMPUTE = ("pe", "act", "dve", "pool")

    def __init__(self):
        self.ops = []
        self.last_w = {}
        self.readers = {}

    def add(self, eng, fn, reads=(), writes=(), dma=False):
        op = Op(len(self.ops), eng, fn, dma)
        deps = op.deps
        for k in reads:
            w = self.last_w.get(k)
            if w is not None:
                deps.add(w)
        for k in writes:
            w = self.last_w.get(k)
            if w is not None:
                deps.add(w)
            r = self.readers.get(k)
            if r:
                deps.update(r)
        for k in reads:
            self.readers.setdefault(k, []).append(op.idx)
        for k in writes:
            self.last_w[k] = op.idx
            self.readers[k] = []
        deps.discard(op.idx)
        self.ops.append(op)
        return op

    def pe(self, fn, reads=(), writes=()):
        return self.add("pe", fn, reads, writes)

    def act(self, fn, reads=(), writes=()):
        return self.add("act", fn, reads, writes)

    def dve(self, fn, reads=(), writes=()):
        return self.add("dve", fn, reads, writes)

    def pool(self, fn, reads=(), writes=()):
        return self.add("pool", fn, reads, writes)

    def dma(self, q, out, in_, reads=(), writes=()):
        return self.add(q, lambda e: e.dma_start(out=out, in_=in_), reads, writes, dma=True)

    @staticmethod
    def _needs_wait(x, y):
        if y.is_dma or x.is_dma:
            return True
        if x.eng == y.eng and x.eng == "pe":
            return False
        return True

    def plan(self, G):
        ops = self.ops
        self.base_cnt = dict(G["cnt"])
        self.base_tgt = list(G["tgt"])
        last_on_sem = [None] * N_DMA_SEMS
        tgt = G["tgt"]
        k = G["k"]
        for op in ops:
            if op.is_dma:
                s = k % N_DMA_SEMS
                k += 1
                if last_on_sem[s] is not None:
                    op.deps.add(last_on_sem[s])
                tgt[s] += 16
                op.dsem = s
                op.dtarget = tgt[s]
                last_on_sem[s] = op.idx
        G["k"] = k
        for op in ops:
            for d in op.deps:
                y = ops[d]
                if not y.is_dma and self._needs_wait(op, y):
                    y.signal = True
        last = {}
        for op in ops:
            if not op.is_dma:
                last[op.eng] = op
        for op in last.values():
            op.signal = True
        cnt = G["cnt"]
        for op in ops:
            if not op.is_dma and op.signal:
                cnt[op.eng] += 1
                op.count = cnt[op.eng]

    def emit(self, ename, eng, G):
        ops = self.ops
        csems, dsems = G["c"], G["d"]
        waited = {}
        for e_ in self.COMPUTE:
            if self.base_cnt[e_] > 0:
                eng.wait_ge(csems[e_], self.base_cnt[e_])
                waited[("c", e_)] = self.base_cnt[e_]
        for s_ in range(N_DMA_SEMS):
            if self.base_tgt[s_] > 0:
                eng.wait_ge(dsems[s_], self.base_tgt[s_])
                waited[("d", s_)] = self.base_tgt[s_]
        mydma = {}
        for op in ops:
            if op.eng != ename:
                continue
            for d in sorted(op.deps):
                y = ops[d]
                if not self._needs_wait(op, y):
                    continue
                if y.is_dma:
                    key, val, sem = ("d", y.dsem), y.dtarget, dsems[y.dsem]
                else:
                    key, val, sem = ("c", y.eng), y.count, csems[y.eng]
                if waited.get(key, 0) >= val:
                    continue
                waited[key] = val
                eng.wait_ge(sem, val)
            ins = op.fn(eng)
            if op.is_dma:
                ins.then_inc(dsems[op.dsem], 16)
                mydma[op.dsem] = op.dtarget
            elif op.signal:
                ins.then_inc(csems[op.eng], 1)
        for s, v in sorted(mydma.items()):
            if waited.get(("d", s), 0) < v:
                eng.wait_ge(dsems[s], v)


def make_sems(nc, st):
    return {"c": {e: st.enter_context(nc.semaphore("cs_" + e)) for e in Prog.COMPUTE},
            "d": [st.enter_context(nc.semaphore("ds_%d" % i)) for i in range(N_DMA_SEMS)],
            "cnt": {e: 0 for e in Prog.COMPUTE}, "tgt": [0] * N_DMA_SEMS, "k": 0}


def run_prog(nc, prog, G):
    prog.plan(G)
    with nc.Block() as block:
        @block.tensor
        def _(e):
            prog.emit("pe", e, G)

        @block.scalar
        def _(e):
            prog.emit("act", e, G)

        @block.vector
        def _(e):
            prog.emit("dve", e, G)

        @block.gpsimd
        def _(e):
            prog.emit("pool", e, G)

        @block.sync
        def _(e):
            prog.emit("sp", e, G)


class Ring:
    def __init__(self, name, bufs):
        self.name = name
        self.bufs = bufs
        self.i = -1

    def next(self):
        self.i += 1
        s = self.i % len(self.bufs)
        return self.bufs[s], (self.name, s)


class Prefetch:
    def __init__(self, p, ring, srcs, queue="pool"):
        self.p, self.ring, self.srcs, self.queue = p, ring, srcs, queue
        self.slots = []
        self.R = len(ring.bufs)

    def ensure(self, k):
        k = min(k, len(self.srcs) - 1)
        while len(self.slots) <= k:
            buf, key = self.ring.next()
            self.p.dma(self.queue, buf[:], self.srcs[len(self.slots)], writes=[key])
            self.slots.append((buf, key))

    def get(self, k):
        self.ensure(k + self.R - 1)
        return self.slots[k]


GROUP_KIND = ["qa", "qa", "ka", "ka", "va", "va", "qr", "qr", "kr", "kr", "vr", "vr", "gr", "gr"]


def build_nc(debug=False, stages=5):
    nc = bass.Bass("TRN2", target_bir_lowering=False)
    ext = "ExternalInput"

    def din(name, shape, dt=F32):
        return nc.dram_tensor(name, shape, dt, kind=ext).ap()

    xin = din("xin", [32, 128, D])
    w_in = din("w_in", [D, 7168])
    w_out = din("w_out", [D, D])
    w_up = din("w_up", [D, 2 * DFF])
    w_down = din("w_down", [DFF, D])
    nw = [din("nw%d" % i, [128, D]) for i in range(4)]
    gnw = din("gnw", [128, 1024])
    gnb = din("gnb", [128, 1024])
    convw = din("convw", [128, NCT, 3])
    convb = din("convb", [128, NCT])
    cos_d = din("cos_t", [128, 32, 64])
    sin_d = din("sin_t", [128, 32, 64])
    qfac_d = din("qfac", [128, 2, 8])
    kfac_d = din("kfac", [128, 2, 8])
    gch_d = din("gch", [128, 8])
    tri01_d = din("tri01", [128, 128])
    trib_d = din("trib", [128, 128])
    gbias_d = din("gbias", [128, NTO, 16])
    ident_d = din("ident", [128, 128])
    out = nc.dram_tensor("out", [16, 128, D], F32, kind="ExternalOutput").ap()

    sk = "ExternalOutput" if debug else "Internal"

    def dsc(name, shape, dt):
        return nc.dram_tensor(name, shape, dt, kind=sk).ap()

    KaT_d = dsc("KaT_d", [8, 128, 4096], BF16)
    KrT_d = dsc("KrT_d", [8, 128, 4096], BF16)
    QaT_d = dsc("QaT_d", [8, 128, NTO * 128], BF16)
    QrT_d = dsc("QrT_d", [8, 128, NTO * 128], BF16)
    Va_d = dsc("Va_d", [32, 128, 1024], BF16)
    Vr_d = dsc("Vr_d", [32, 128, 1024], BF16)
    Kr_d = dsc("Kr_d", [32, 128, 1024], BF16)
    Sg_d = dsc("Sg_d", [NTO, 128, 1024], F32)
    mixT_d = dsc("mixT_d", [16, 128, NTO * 128], BF16)
    mo_d = dsc("mo_d", [NTO, 128, D], F32)
    x1_d = dsc("x1_d", [NTO, 128, D], F32)
    dbgB = dsc("dbgB", [128, NTO * 16 + NTO + NTO * 18], F32)
    dbgQ = dsc("dbgQ", [128, NTO * 128], BF16)
    dbgK = dsc("dbgK", [128, 2], BF16)

    w_in_v = w_in.rearrange("(kc p) n -> p kc n", p=128)
    w_out_v = w_out.rearrange("(kc p) n -> p kc n", p=128)
    w_up_v = w_up.rearrange("(kc p) n -> p kc n", p=128)
    w_down_v = w_down.rearrange("(ct p) n -> p ct n", p=128)

    with contextlib.ExitStack() as gst:
        def gsb(name, shape, dt):
            return gst.enter_context(nc.sbuf_tensor("g_" + name, shape, dt))

        ident = gsb("ident", [128, 128], BF16)
        G = make_sems(nc, gst)

        p = Prog()
        p.dma("pool", ident[:], ident_d, writes=["ident"])
        run_prog(nc, p, G)

        for seg in ("P", "O"):
            with contextlib.ExitStack() as st:
                def sb(name, shape, dt):
                    return st.enter_context(nc.sbuf_tensor("s1_" + name + seg, shape, dt))

                def ps(name, shape, dt):
                    return st.enter_context(nc.psum_tensor("p1_" + name + seg, shape, dt))

                NT = NTP if seg == "P" else NTO
                T0 = 0 if seg == "P" else NTP
                groups = [2, 3, 4, 5, 8, 9, 10, 11] if seg == "P" else list(range(14))

                hnT = sb("hnT", [128, 16, NTO * 128], BF16)
                xt_r = Ring("xt", [sb("xt%d" % i, [128, D], F32) for i in range(2)])
                xn_r = Ring("xn", [sb("xn%d" % i, [128, D], BF16) for i in range(2)])
                junk = sb("junk", [128, D], BF16)
                wbc = sb("wbc", [128, D], F32)
                ss = sb("ss", [128, NTO], F32)
                rs = sb("rs", [128, NTO], F32)
                W_r = Ring("W", [sb("W%d" % i, [128, 16, 512], BF16) for i in range(2)])
                cos_t = sb("cos", [128, 32, 64], F32)
                sin_t = sb("sin", [128, 32, 64], F32)
                qfac = sb("qfac", [128, 2, 8], F32)
                kfac = sb("kfac", [128, 2, 8], F32)
                rA = Ring("rA", [sb("rA%d" % i, [128, 4, 64], F32) for i in range(2)])
                rB = Ring("rB", [sb("rB%d" % i, [128, 4, 64], F32) for i in range(2)])
                rC = Ring("rC", [sb("rC%d" % i, [128, 4, 64], F32) for i in range(2)])
                rD = Ring("rD", [sb("rD%d" % i, [128, 4, 64], F32) for i in range(2)])
                rot_r = Ring("rot", [sb("rot%d" % i, [128, 4, 128], F32) for i in range(2)])
                tok_r = Ring("tok", [sb("tok%d" % i, [128, 4, 128], BF16) for i in range(6)])
                ft_r = Ring("ft", [sb("ft%d" % i, [128, 4, 128], BF16) for i in range(3)])
                sg_r = Ring("sg", [sb("sg%d" % i, [128, 512], F32) for i in range(2)])
                pA = ps("pA", [128, D], BF16)
                pM_r = Ring("pM", [ps("pM%d" % i, [128, 512], F32) for i in range(4)])
                pT_r = Ring("pT", [ps("pT%d" % i, [128, 4, 128], BF16) for i in range(2)])

                p = Prog()
                p.dma("sp", wbc[:], nw[0], writes=["wbc"])
                p.dma("sp", cos_t[:], cos_d, writes=["cos"])
                p.dma("sp", sin_t[:], sin_d, writes=["sin"])
                p.dma("sp", qfac[:], qfac_d, writes=["qfac"])
                p.dma("sp", kfac[:], kfac_d, writes=["kfac"])

                def phaseA(j):
                    t = T0 + j
                    xt, kxt = xt_r.next()
                    xn, kxn = xn_r.next()
                    p.dma("sp", xt[:], xin[t], writes=[kxt])
                    p.act(lambda e, xt=xt, j=j: e.activation(junk[:], xt[:], AF.Square, accum_out=ss[:, j:j + 1]),
                          reads=[kxt], writes=["junk", ("ss", j)])
                    p.dve(lambda e, j=j: e.tensor_scalar(rs[:, j:j + 1], ss[:, j:j + 1], 1.0 / D, 1e-6, ALU.mult, ALU.add),
                          reads=[("ss", j)], writes=[("rs", j)])
                    p.act(lambda e, j=j: e.activation(rs[:, j:j + 1], rs[:, j:j + 1], AF.Sqrt),
                          reads=[("rs", j)], writes=[("rs", j)])
                    p.dve(lambda e, j=j: e.reciprocal(rs[:, j:j + 1], rs[:, j:j + 1]),
                          reads=[("rs", j)], writes=[("rs", j)])
                    p.dve(lambda e, xt=xt, xn=xn, j=j: e.scalar_tensor_tensor(
                        xn[:], xt[:], rs[:, j:j + 1], wbc[:], ALU.mult, ALU.mult),
                        reads=[kxt, ("rs", j), "wbc"], writes=[kxn])
                    for kc in range(16):
                        p.pe(lambda e, xn=xn, kc=kc: e.transpose(pA[:, kc * 128:(kc + 1) * 128],
                                                                   xn[:, kc * 128:(kc + 1) * 128], ident[:]),
                             reads=[kxn], writes=["pA"])
                    p.act(lambda e, j=j: e.activation(hnT[:, 0:8, j * 128:(j + 1) * 128],
                                                      pA[:, 0:1024].rearrange("p (k c) -> p k c", k=8), AF.Copy),
                          reads=["pA"], writes=[("hnT", j, 0)])
                    p.dve(lambda e, j=j: e.tensor_copy(hnT[:, 8:16, j * 128:(j + 1) * 128],
                                                       pA[:, 1024:2048].rearrange("p (k c) -> p k c", k=8)),
                          reads=["pA"], writes=[("hnT", j, 1)])

                for j in range(min(3, NT)):
                    phaseA(j)
                doneA = [min(3, NT)]
                deferred = []
                Wpf = Prefetch(p, W_r, [w_in_v[:, :, g * 512:(g + 1) * 512] for g in groups])
                for gi_, g in enumerate(groups):
                    kind = GROUP_KIND[g]
                    g4 = g % 2
                    Wb, kW = Wpf.get(gi_)
                    for j in range(NT):
                        t = T0 + j
                        if doneA[0] < NT:
                            phaseA(doneA[0])
                            doneA[0] += 1
                        pM, kpM = pM_r.next()
                        for kc in range(16):
                            p.pe(lambda e, pM=pM, Wb=Wb, kc=kc, j=j: e.matmul(
                                pM[:], hnT[:, kc, j * 128:(j + 1) * 128], Wb[:, kc, :],
                                start=(kc == 0), stop=(kc == 15)),
                                reads=[("hnT", j, 0), ("hnT", j, 1), kW], writes=[kpM])
                        keep = []
                        for ent in deferred:
                            ent[0] -= 1
                            if ent[0] <= 0:
                                ent[1]()
                            else:
                                keep.append(ent)
                        deferred[:] = keep
                        if kind in ("va", "vr"):
                            tok, ktok = tok_r.next()
                            p.act(lambda e, tok=tok, pM=pM: e.activation(
                                tok[:].rearrange("p h d -> p (h d)"), pM[:], AF.Copy),
                                reads=[kpM], writes=[ktok])
                            dst = Va_d if kind == "va" else Vr_d
                            p.dma("sp", dst[t][:, g4 * 512:(g4 + 1) * 512], tok[:].rearrange("p h d -> p (h d)"),
                                  reads=[ktok], writes=[(kind, t, g4)])
                            continue
                        if kind == "gr":
                            sg, ksg = sg_r.next()
                            p.act(lambda e, sg=sg, pM=pM: e.activation(sg[:], pM[:], AF.Silu),
                                  reads=[kpM], writes=[ksg])
                            p.dma("sp", Sg_d[j][:, g4 * 512:(g4 + 1) * 512], sg[:], reads=[ksg], writes=[("sgd", j, g4)])
                            continue
                        src = pM[:].rearrange("p (h d) -> p h d", h=4)
                        t1 = src[:, :, 0:64]
                        t2 = src[:, :, 64:128]
                        cb = cos_t[:, t:t + 1, :].to_broadcast([128, 4, 64])
                        sbb = sin_t[:, t:t + 1, :].to_broadcast([128, 4, 64])
                        A, kA = rA.next()
                        B, kB = rB.next()
                        C, kC = rC.next()
                        Dd, kD = rD.next()
                        tok, ktok = tok_r.next()
                        is_ret = kind in ("qr", "kr")
                        if is_ret:
                            rot, krot = rot_r.next()
                            dst, kdst = rot, krot
                        else:
                            dst, kdst = tok, ktok
                        p.dve(lambda e, A=A, t1=t1, cb=cb: e.tensor_tensor(A[:], t1, cb, ALU.mult),
                              reads=[kpM, "cos"], writes=[kA])
                        p.dve(lambda e, B=B, t2=t2, sbb=sbb: e.tensor_tensor(B[:], t2, sbb, ALU.mult),
                              reads=[kpM, "sin"], writes=[kB])
                        p.dve(lambda e, C=C, t1=t1, sbb=sbb: e.tensor_tensor(C[:], t1, sbb, ALU.mult),
                              reads=[kpM, "sin"], writes=[kC])
                        p.dve(lambda e, Dd=Dd, t2=t2, cb=cb: e.tensor_tensor(Dd[:], t2, cb, ALU.mult),
                              reads=[kpM, "cos"], writes=[kD])
                        p.pool(lambda e, dst=dst, A=A, B=B: e.tensor_tensor(dst[:, :, 0:64], A[:], B[:], ALU.subtract),
                               reads=[kA, kB], writes=[(kdst, "lo")])
                        p.pool(lambda e, dst=dst, C=C, Dd=Dd: e.tensor_tensor(dst[:, :, 64:128], C[:], Dd[:], ALU.add),
                               reads=[kC, kD], writes=[(kdst, "hi")])
                        if is_ret:
                            fac = qfac if kind == "qr" else kfac
                            par = t % 2
                            for hh in range(4):
                                head = g4 * 4 + hh
                                p.act(lambda e, tok=tok, rot=rot, hh=hh, fac=fac, par=par, head=head: e.activation(
                                    tok[:, hh, :], rot[:, hh, :], AF.Copy, scale=fac[:, par, head:head + 1]),
                                    reads=[(krot, "lo"), (krot, "hi"), "qfac", "kfac"], writes=[(ktok, hh)])
                            tokreads = [(ktok, hh) for hh in range(4)]
                        else:
                            tokreads = [(ktok, "lo"), (ktok, "hi")]
                        if kind == "kr":
                            p.dma("sp", Kr_d[t][:, g4 * 512:(g4 + 1) * 512], tok[:].rearrange("p h d -> p (h d)"),
                                  reads=tokreads, writes=[("krd", t, g4)])
                        need_T = kind in ("ka", "kr") or seg == "O"
                        if not need_T:
                            continue

                        def tail(tok=tok, tokreads=tokreads, kind=kind, g4=g4, t=t, j=j):
                            pT, kpT = pT_r.next()
                            for hh in range(4):
                                p.pe(lambda e, pT=pT, tok=tok, hh=hh: e.transpose(pT[:, hh, :], tok[:, hh, :], ident[:]),
                                     reads=tokreads, writes=[kpT])
                            ft, kft = ft_r.next()
                            p.act(lambda e, ft=ft, pT=pT: e.activation(ft[:], pT[:], AF.Copy), reads=[kpT], writes=[kft])
                            if kind in ("ka", "kr"):
                                dd = KaT_d if kind == "ka" else KrT_d
                                dap = dd[g4 * 4:(g4 + 1) * 4].rearrange("h d t -> d h t")[:, :, t * 128:(t + 1) * 128]
                            else:
                                dd = QaT_d if kind == "qa" else QrT_d
                                dap = dd[g4 * 4:(g4 + 1) * 4].rearrange("h d t -> d h t")[:, :, j * 128:(j + 1) * 128]
                            p.dma("sp", dap, ft[:], reads=[kft], writes=[(kind + "T", t, g4)])
                        deferred.append([2, tail])
                for _, fn_ in deferred:
                    fn_()
                run_prog(nc, p, G)

        if stages < 2:
            return nc
        mst = gst.enter_context(contextlib.ExitStack())
        mixT = mst.enter_context(nc.sbuf_tensor("g_mixT", [128, 16, NTO * 128], BF16))
        SCALE = 128.0 ** -0.5

        with contextlib.ExitStack() as st:
            def sb(name, shape, dt):
                return st.enter_context(nc.sbuf_tensor("s2_" + name, shape, dt))

            def ps(name, shape, dt):
                return st.enter_context(nc.psum_tensor("p2_" + name, shape, dt))

            KT_r = Ring("KT", [sb("KT%d" % i, [128, 4096], BF16) for i in range(2)])
            V_r = Ring("V", [sb("V%d" % i, [128, 32, 128], BF16) for i in range(2)])
            QT_r = Ring("QT", [sb("QT%d" % i, [128, NTO * 128], BF16) for i in range(2)])
            QA_r = Ring("QA", [sb("QA%d" % i, [128, NTO * 128], BF16) for i in range(2)])
            gbias = sb("gbias", [128, NTO, 16], F32)
            trib = sb("trib", [128, 256], BF16)
            ksum = sb("ksum", [128, 16], F32)
            kmb_r = Ring("kmb", [sb("kmb%d" % i, [128, 16], BF16) for i in range(2)])
            kab = sb("kab", [128, 2], F32)
            kabb_r = Ring("kabb", [sb("kabb%d" % i, [128, 2], BF16) for i in range(2)])
            gb_r = Ring("gb", [sb("gb%d" % i, [128, 16], F32) for i in range(2)])
            t8_r = Ring("t8", [sb("t8%d" % i, [128, 8], F32) for i in range(2)])
            thr_r = Ring("thr", [sb("thr%d" % i, [128, 1], F32) for i in range(2)])
            s01_r = Ring("s01", [sb("s01%d" % i, [128, 16], F32) for i in range(2)])
            ball_r = Ring("ball", [sb("ball%d" % i, [128, NTO, 16], F32) for i in range(2)])
            negm_r = Ring("negm", [sb("negm%d" % i, [128, NTO], F32) for i in range(2)])
            rsum_r = Ring("rsum", [sb("rsum%d" % i, [128, NTO, 18], F32) for i in range(2)])
            rtot_r = Ring("rtot", [sb("rtot%d" % i, [128, 1], F32) for i in range(3)])
            Pb_r = Ring("Pb", [sb("Pb%d" % i, [128, 256], BF16) for i in range(6)])
            sm_r = Ring("sm", [sb("sm%d" % i, [128, 128], F32) for i in range(4)])
            PT_r = Ring("PT", [sb("PT%d" % i, [128, 2, 128], BF16) for i in range(5)])
            ab_r = Ring("ab", [sb("ab%d" % i, [128, 128], BF16) for i in range(3)])
            bkf = [ps("bkf%d" % i, [128, 512], F32) for i in range(6)]
            bkb = [ps("bkb%d" % i, [128, 1024], BF16) for i in range(2)]
            pO_r = Ring("pO", [bkf[0][:, 0:128], bkf[1][:, 0:128]])
            pS_r = Ring("pS", [bkf[2][:, 0:256], bkf[3][:, 0:256], bkf[4][:, 0:256]])
            pGB_r = Ring("pGB", [bkf[5][:, 0:32]])
            pPT_r = Ring("pPT", [bkb[i][:, 0:256].rearrange("p (k c) -> p k c", k=2) for i in range(2)])
            pAT_r = Ring("pPT", [bkb[0][:, 512:640]])

            p = Prog()
            p.dma("sp", gbias[:], gbias_d, writes=["gbias"])
            p.dma("pool", trib[:, 128:256], trib_d, writes=["trib"])
            p.pool(lambda e: e.memset(trib[:, 0:128], 0.0), writes=["trib0"])

            def load_head(h):
                KT, kKT = KT_r.next()
                V, kV = V_r.next()
                QT, kQT = QT_r.next()
                p.dma("sp", KT[:], KaT_d[h], writes=[kKT])
                p.dma("sp", V[:], Va_d[:, :, h * 128:(h + 1) * 128].rearrange("t p c -> p t c"), writes=[kV])
                p.dma("sp", QT[:], QaT_d[h], writes=[kQT])
                return (KT, kKT, V, kV, QT, kQT)

            def prologue_pieces(hd):
                KT, kKT, V, kV, QT, kQT = hd
                QA, kQA = QA_r.next()
                kmb, kkmb = kmb_r.next()
                kabb, kkabb = kabb_r.next()
                ball, kball = ball_r.next()
                negm, knegm = negm_r.next()
                rsum, krsum = rsum_r.next()
                bufs = dict(QA=QA, kQA=kQA, ball=ball, kball=kball, negm=negm, knegm=knegm, rsum=rsum, krsum=krsum)
                pieces = []

                def head_ops():
                    p.dve(lambda e: e.tensor_reduce(ksum[:], KT[:].rearrange("p (n k) -> p n k", n=16), AX.X, ALU.add),
                          reads=[kKT], writes=["ksum"])
                    p.dve(lambda e: e.tensor_scalar(kmb[:], ksum[:], 1.0 / 256.0, None, ALU.mult),
                          reads=["ksum"], writes=[kkmb])
                    p.dve(lambda e: e.tensor_reduce(kab[:, 0:1], KT[:], AX.X, ALU.max, apply_absolute_value=True),
                          reads=[kKT], writes=["kab"])
                    p.dve(lambda e: e.tensor_copy(kab[:, 1:2], kab[:, 0:1]), reads=["kab"], writes=["kab1"])
                    p.dve(lambda e: e.tensor_scalar(kabb[:], kab[:], 1.02, None, ALU.mult),
                          reads=["kab", "kab1"], writes=[kkabb])
                    p.act(lambda e: e.activation(QA[:], QT[:], AF.Abs), reads=[kQT], writes=[kQA])
                pieces.append(head_ops)

                def tile_ops(j):
                    pGB, kpGB = pGB_r.next()
                    gb, kgb = gb_r.next()
                    t8, kt8 = t8_r.next()
                    thr, kthr = thr_r.next()
                    s01, ks01 = s01_r.next()
                    qs = slice(j * 128, (j + 1) * 128)
                    p.pe(lambda e: e.matmul(pGB[:, 0:16], QT[:, qs], kmb[:], start=True, stop=True),
                         reads=[kQT, kkmb], writes=[kpGB])
                    p.pe(lambda e: e.matmul(pGB[:, 16:18], QA[:, qs], kabb[:], start=True, stop=True),
                         reads=[kQA, kkabb], writes=[kpGB])
                    p.dve(lambda e: e.tensor_tensor(gb[:], pGB[:, 0:16], gbias[:, j, :], ALU.add),
                          reads=["gbias"], writes=[kgb, kpGB])
                    p.dve(lambda e: e.tensor_scalar(negm[:, j:j + 1], pGB[:, 16:17], -SCALE, None, ALU.mult),
                          reads=[], writes=[(knegm, j), kpGB])
                    p.dve(lambda e: e.max(t8[:], gb[:]), reads=[kgb], writes=[kt8])
                    p.dve(lambda e: e.tensor_scalar(thr[:], t8[:, 2:3], -1.0e29, None, ALU.max), reads=[kt8], writes=[kthr])
                    p.dve(lambda e: e.tensor_scalar(s01[:], gb[:], thr[:, 0:1], None, ALU.is_ge), reads=[kgb, kthr], writes=[ks01])
                    p.dve(lambda e: e.tensor_scalar(s01[:], s01[:], -1.0, BIG, ALU.add, ALU.mult), reads=[ks01], writes=[ks01])
                    p.dve(lambda e: e.tensor_scalar(ball[:, j, :], s01[:], negm[:, j:j + 1], None, ALU.add),
                          reads=[ks01, (knegm, j)], writes=[(kball, j)])
                for j_ in range(NTO):
                    pieces.append(lambda j_=j_: tile_ops(j_))
                return pieces, bufs

            nxt = load_head(0)
            pend, nxt_bufs = prologue_pieces(nxt)
            for pc_ in pend:
                pc_()
            pend = []
            evac_flip = [0]
            for h in range(8):
                KT, kKT, V, kV, QT, kQT = nxt
                cb_ = nxt_bufs
                ball, kball, negm, knegm, rsum, krsum = cb_["ball"], cb_["kball"], cb_["negm"], cb_["knegm"], cb_["rsum"], cb_["krsum"]
                if h + 1 < 8:
                    nxt = load_head(h + 1)
                    pend, nxt_bufs = prologue_pieces(nxt)

                items = []
                for j in range(NTO):
                    t = NTP + j
                    qb, half = t // 2, t % 2
                    blocks = list(range(qb + 1))
                    for bi, n in enumerate(blocks):
                        items.append(dict(j=j, n=n, own=(n == qb), half=half, first=(bi == 0),
                                          last=(bi == len(blocks) - 1), slot=bi))
                NI = len(items)
                pO_cur = {}

                def stage_S(it):
                    j, n = it["j"], it["n"]
                    nk = 128 if (it["own"] and it["half"] == 0) else 256
                    pS, kpS = pS_r.next()
                    it["pS"], it["kpS"], it["nk"] = pS, kpS, nk
                    qs = slice(j * 128, (j + 1) * 128)
                    own = it["own"]
                    p.pe(lambda e, QT=QT, KT=KT, pS=pS, qs=qs, n=n, nk=nk, own=own: e.matmul(
                        pS[:, 0:nk], QT[:, qs], KT[:, n * 256:n * 256 + nk], start=True, stop=(not own)),
                        reads=[kQT, kKT], writes=[kpS])
                    if own:
                        dc = it["half"] * 128
                        p.pe(lambda e, pS=pS, nk=nk: e.matmul(pS[:, 0:nk], ident[:], trib[:, 256 - nk:256], start=False, stop=True),
                             reads=["trib", "trib0"], writes=[kpS])

                def stage_E(it):
                    j, n, nk = it["j"], it["n"], it["nk"]
                    pS, kpS = it["pS"], it["kpS"]
                    Pb, kPb = Pb_r.next()
                    it["Pb"], it["kPb"] = Pb, kPb
                    if it["own"]:
                        bias_ap, bkey = negm[:, j:j + 1], (knegm, j)
                    else:
                        bias_ap, bkey = ball[:, j, n:n + 1], (kball, j)
                    p.act(lambda e, Pb=Pb, pS=pS, nk=nk, bias_ap=bias_ap, rsum=rsum, j=j, sl=it["slot"]: e.activation(
                        Pb[:, 0:nk], pS[:, 0:nk], AF.Exp, bias=bias_ap, scale=SCALE, accum_out=rsum[:, j, sl:sl + 1]),
                        reads=[bkey], writes=[kPb, (krsum, j, it["slot"]), kpS])

                def stage_T(it):
                    nk = it["nk"]
                    Pb, kPb = it["Pb"], it["kPb"]
                    pPT, kpPT = pPT_r.next()
                    it["pPT"], it["kpPT"] = pPT, kpPT
                    rd = it.get("kPb_parts", [kPb])
                    for kk in range(nk // 128):
                        p.pe(lambda e, QT=QT, KT=KT, V=V, ball=ball, negm=negm, rsum=rsum, pPT=pPT, Pb=Pb, kk=kk: e.transpose(pPT[:, kk, :], Pb[:, kk * 128:(kk + 1) * 128], ident[:]),
                             reads=rd, writes=[kpPT])

                def stage_V(it):
                    nkk = it["nk"] // 128
                    pPT, kpPT = it["pPT"], it["kpPT"]
                    PT, kPT = PT_r.next()
                    it["PT"], it["kPT"] = PT, kPT
                    if True:
                        p.dve(lambda e, QT=QT, KT=KT, V=V, ball=ball, negm=negm, rsum=rsum, PT=PT, pPT=pPT, nkk=nkk: e.tensor_copy(PT[:, 0:nkk, :], pPT[:, 0:nkk, :]),
                              writes=[kPT, kpPT])
                    else:
                        p.act(lambda e, QT=QT, KT=KT, V=V, ball=ball, negm=negm, rsum=rsum, PT=PT, pPT=pPT, nkk=nkk: e.activation(PT[:, 0:nkk, :], pPT[:, 0:nkk, :], AF.Copy),
                              writes=[kPT, kpPT])

                def stage_M(it):
                    j, n = it["j"], it["n"]
                    nkk = it["nk"] // 128
                    PT, kPT = it["PT"], it["kPT"]
                    if it["first"]:
                        pO_cur[j] = pO_r.next()
                    pO, kpO = pO_cur[j]
                    for kk in range(nkk):
                        p.pe(lambda e, QT=QT, KT=KT, V=V, ball=ball, negm=negm, rsum=rsum, pO=pO, PT=PT, kk=kk, n=n, st_=(it["first"] and kk == 0),
                             sp_=(it["last"] and kk == nkk - 1): e.matmul(pO[:], PT[:, kk, :], V[:, n * 2 + kk, :], start=st_, stop=sp_),
                             reads=[kPT, kV], writes=[kpO])
                    if it["last"]:
                        cnt = it["slot"] + 1
                        rtot, krtot = rtot_r.next()
                        ab, kab_ = ab_r.next()
                        rd = [(krsum, j, s_) for s_ in range(cnt)]

                        def f1(rtot=rtot, krtot=krtot, j=j, cnt=cnt, rd=rd, rsum=rsum):
                            p.dve(lambda e: e.tensor_reduce(rtot[:], rsum[:, j, 0:cnt], AX.X, ALU.add), reads=rd, writes=[krtot])
                            p.dve(lambda e: e.reciprocal(rtot[:], rtot[:]), reads=[krtot], writes=[krtot])

                        def f2(ab=ab, kab_=kab_, pO=pO, kpO=kpO, rtot=rtot, krtot=krtot):
                            p.act(lambda e: e.activation(ab[:], pO[:], AF.Copy, scale=rtot[:, 0:1]),
                                  reads=[krtot], writes=[kab_, kpO])

                        def f3(ab=ab, kab_=kab_, j=j, h=h):
                            pAT, kpAT = pAT_r.next()
                            p.pe(lambda e: e.transpose(pAT[:], ab[:], ident[:]), reads=[kab_], writes=[kpAT])
                            p.dve(lambda e: e.tensor_copy(mixT[:, h, j * 128:(j + 1) * 128], pAT[:]),
                                  writes=[("mixT", h, j), kpAT])
                        fin.append([1, f1])
                        fin.append([3, f2])
                        fin.append([5, f3])

                fin = []
                for i in range(NI + 4):
                    if pend and i % 10 == 9:
                        pend.pop(0)()
                    if i < NI:
                        stage_S(items[i])
                    if 0 <= i - 1 < NI:
                        stage_E(items[i - 1])
                    if 0 <= i - 3 < NI:
                        stage_T(items[i - 3])
                        stage_V(items[i - 3])
                    if 0 <= i - 4 < NI:
                        stage_M(items[i - 4])
                    keep_ = []
                    for ent in fin:
                        ent[0] -= 1
                        if ent[0] < 0:
                            ent[1]()
                        else:
                            keep_.append(ent)
                    fin[:] = keep_
                for ent in fin:
                    ent[1]()
                fin[:] = []
                while pend:
                    pend.pop(0)()
            if debug:
                for c in range(8):
                    p.dma("sp", mixT_d[c], mixT[:, c, :], reads=[("mixT", c, j) for j in range(NTO)], writes=[("mixTd", c)])
            run_prog(nc, p, G)

        if stages < 3:
            return nc

        with contextlib.ExitStack() as st:
            def sb(name, shape, dt):
                return st.enter_context(nc.sbuf_tensor("s3_" + name, shape, dt))

            def ps(name, shape, dt):
                return st.enter_context(nc.psum_tensor("p3_" + name, shape, dt))

            Kr_r = Ring("Kr", [sb("Kr%d" % i, [128, 2, 1024], BF16) for i in range(2)])
            Vr_r = Ring("Vr", [sb("Vr%d" % i, [128, 2, 1024], BF16) for i in range(2)])
            Qc_r = Ring("Qc", [sb("Qc%d" % i, [128, 8, 256], BF16) for i in range(2)])
            Kc_r = Ring("Kc", [sb("Kc%d" % i, [128, 8, 256], BF16) for i in range(2)])
            Sg_r = Ring("Sg", [sb("Sg%d" % i, [128, 2, 1024], F32) for i in range(2)])
            state = sb("state", [128, 8, 128], F32)
            sbf = [sb("sbf%d" % i, [128, 8, 128], BF16) for i in range(2)]
            st2_r = Ring("st2", [sb("st2%d" % i, [128, 128], F32) for i in range(8)])
            tri01 = sb("tri01", [128, 128], F32)
            gnw_t = sb("gnw", [128, 1024], F32)
            gnb_t = sb("gnb", [128, 1024], F32)
            gch = sb("gch", [128, 8], F32)
            epsT = sb("epsT", [128, 1], F32)
            NI3 = 16
            PTm_r = Ring("PTm", [sb("PTm%d" % i, [128, 256], BF16) for i in range(16)])
            ysb_r = Ring("ysb", [sb("ysb%d" % i, [128, 128], F32) for i in range(NI3)])
            bst_r = Ring("bst", [sb("bst%d" % i, [128, 6], F32) for i in range(NI3)])
            mv_r = Ring("mv", [sb("mv%d" % i, [128, 2], F32) for i in range(NI3)])
            rsd_r = Ring("rsd", [sb("rsd%d" % i, [128, 1], F32) for i in range(NI3)])
            rn_r = Ring("rn", [sb("rn%d" % i, [128, 128], F32) for i in range(NI3)])
            rbf_r = Ring("rbf", [sb("rbf%d" % i, [128, 128], BF16) for i in range(NI3)])
            pST_r = Ring("pST", [ps("pST%d" % i, [128, 256], F32) for i in range(2)])
            pY_r = Ring("pY", [ps("pY%d" % i, [128, 128], F32) for i in range(2)])
            pKV_r = Ring("pKV", [ps("pKV%d" % i, [128, 128], F32) for i in range(2)])
            pRT_r = Ring("pRT", [ps("pRT%d" % i, [128, 128], BF16) for i in range(2)])

            p = Prog()
            p.dma("sp", tri01[:], tri01_d, writes=["tri01"])
            p.dma("sp", gnw_t[:], gnw, writes=["gnw"])
            p.dma("sp", gnb_t[:], gnb, writes=["gnb"])
            p.dma("sp", gch[:], gch_d, writes=["gch"])
            p.pool(lambda e: e.memset(state[:], 0.0), writes=[("state", h) for h in range(8)])
            p.pool(lambda e: e.memset(sbf[0][:], 0.0), writes=[("sbf", 0, h) for h in range(8)])
            p.pool(lambda e: e.memset(epsT[:], 1e-5), writes=["eps"])
            for n in range(16):
                Kr, kKr = Kr_r.next()
                Vr, kVr = Vr_r.next()
                p.dma("sp", Kr[:], Kr_d[2 * n:2 * n + 2].rearrange("t p c -> p t c"), writes=[kKr])
                p.dma("sp", Vr[:], Vr_d[2 * n:2 * n + 2].rearrange("t p c -> p t c"), writes=[kVr])
                iset = []
                if n >= 7:
                    iset = [1] if n == 7 else [0, 1]
                    Qc, kQc = Qc_r.next()
                    Kc, kKc = Kc_r.next()
                    Sg, kSg = Sg_r.next()
                    if n == 7:
                        p.dma("sp", Qc[:, :, 128:256], QrT_d[:, :, 0:128].rearrange("h d t -> d h t"), writes=[kQc])
                        p.dma("sp", Sg[:, 1, :], Sg_d[0], writes=[kSg])
                    else:
                        j0 = 2 * n - NTP
                        p.dma("sp", Qc[:], QrT_d[:, :, j0 * 128:(j0 + 2) * 128].rearrange("h d t -> d h t"), writes=[kQc])
                        p.dma("sp", Sg[:], Sg_d[j0:j0 + 2].rearrange("t p c -> p t c"), writes=[kSg])
                    p.dma("sp", Kc[:], KrT_d[:, :, 2 * n * 128:(2 * n + 2) * 128].rearrange("h d t -> d h t"), writes=[kKc])
                sb_cur = sbf[n % 2]
                sb_nxt = sbf[(n + 1) % 2]
                its = []
                if iset:
                    PTs = {}
                    for h in range(8):
                        for jj in (0, 1):
                            qis = [i for i in iset if i >= jj]
                            q0 = min(qis) * 128
                            pST, kpST = pST_r.next()
                            PTm, kPTm = PTm_r.next()
                            PTs[(h, jj)] = (PTm, kPTm)
                            p.pe(lambda e, pST=pST, Kc=Kc, Qc=Qc, h=h, jj=jj, q0=q0: e.matmul(
                                pST[:, q0:256], Kc[:, h, jj * 128:(jj + 1) * 128], Qc[:, h, q0:256], start=True, stop=True),
                                reads=[kKc, kQc], writes=[kpST])
                            for i in qis:
                                cs = slice(i * 128, (i + 1) * 128)
                                if i == jj:
                                    p.dve(lambda e, PTm=PTm, pST=pST, cs=cs: e.tensor_tensor(PTm[:, cs], pST[:, cs], tri01[:], ALU.mult),
                                          reads=["tri01"], writes=[(kPTm, i), kpST])
                                else:
                                    p.act(lambda e, PTm=PTm, pST=pST, cs=cs: e.activation(PTm[:, cs], pST[:, cs], AF.Copy),
                                          writes=[(kPTm, i), kpST])
                    for h in range(8):
                        hs = slice(h * 128, (h + 1) * 128)
                        for i in iset:
                            j = 2 * n + i - NTP
                            cs = slice(i * 128, (i + 1) * 128)
                            pY, kpY = pY_r.next()
                            first = True
                            for jj in range(i + 1):
                                PTm, kPTm = PTs[(h, jj)]
                                p.pe(lambda e, pY=pY, PTm=PTm, Vr=Vr, cs=cs, jj=jj, hs=hs, first=first: e.matmul(
                                    pY[:], PTm[:, cs], Vr[:, jj, hs], start=first, stop=False),
                                    reads=[(kPTm, i), kVr], writes=[kpY])
                                first = False
                            p.pe(lambda e, pY=pY, Qc=Qc, cs=cs, h=h, sb_cur=sb_cur: e.matmul(
                                pY[:], Qc[:, h, cs], sb_cur[:, h, :], start=False, stop=True),
                                reads=[kQc, ("sbf", n % 2, h)], writes=[kpY])
                            it = dict(h=h, i=i, j=j, hs=hs)
                            it["ysb"], it["kysb"] = ysb_r.next()
                            it["bst"], it["kbst"] = bst_r.next()
                            it["mv"], it["kmv"] = mv_r.next()
                            it["rsd"], it["krsd"] = rsd_r.next()
                            it["rn"], it["krn"] = rn_r.next()
                            it["rbf"], it["krbf"] = rbf_r.next()
                            p.act(lambda e, ysb=it["ysb"], pY=pY: e.activation(ysb[:], pY[:], AF.Copy), writes=[it["kysb"], kpY])
                            its.append(it)
                if n < 15:
                    kvs = []
                    for h in range(8):
                        hs = slice(h * 128, (h + 1) * 128)
                        pKV, kpKV = pKV_r.next()
                        st2, kst2 = st2_r.next()
                        for jj in (0, 1):
                            p.pe(lambda e, pKV=pKV, Kr=Kr, Vr=Vr, jj=jj, hs=hs: e.matmul(
                                pKV[:], Kr[:, jj, hs], Vr[:, jj, hs], start=(jj == 0), stop=(jj == 1)),
                                reads=[kKr, kVr], writes=[kpKV])
                        p.dve(lambda e, st2=st2, pKV=pKV, h=h: e.tensor_tensor(st2[:], state[:, h, :], pKV[:], ALU.add),
                              reads=[("state", h)], writes=[kst2, kpKV])
                        kvs.append((h, st2, kst2))
                    for h, st2, kst2 in kvs:
                        p.act(lambda e, st2=st2, h=h: e.activation(state[:, h, :], st2[:], AF.Copy, scale=gch[:, h:h + 1]),
                              reads=[kst2, "gch"], writes=[("state", h)])
                    for h, st2, kst2 in kvs:
                        p.pool(lambda e, sb_nxt=sb_nxt, h=h: e.tensor_copy(sb_nxt[:, h, :], state[:, h, :]),
                               reads=[("state", h)], writes=[("sbf", (n + 1) % 2, h)])
                for it in its:
                    p.dve(lambda e, bst=it["bst"], ysb=it["ysb"]: e.bn_stats(bst[:], ysb[:]), reads=[it["kysb"]], writes=[it["kbst"]])
                for it in its:
                    p.dve(lambda e, mv=it["mv"], bst=it["bst"]: e.bn_aggr(mv[:], bst[:]), reads=[it["kbst"]], writes=[it["kmv"]])
                for it in its:
                    p.act(lambda e, rsd=it["rsd"], mv=it["mv"]: e.activation(rsd[:], mv[:, 1:2], AF.Sqrt, bias=epsT[:, 0:1]),
                          reads=[it["kmv"], "eps"], writes=[it["krsd"]])
                for it in its:
                    p.dve(lambda e, rsd=it["rsd"]: e.reciprocal(rsd[:], rsd[:]), reads=[it["krsd"]], writes=[it["krsd"]])
                for it in its:
                    p.dve(lambda e, rn=it["rn"], ysb=it["ysb"], mv=it["mv"], rsd=it["rsd"]: e.tensor_scalar(
                        rn[:], ysb[:], mv[:, 0:1], rsd[:, 0:1], ALU.subtract, ALU.mult),
                        reads=[it["kysb"], it["kmv"], it["krsd"]], writes=[it["krn"]])
                for it in its:
                    p.pool(lambda e, rn=it["rn"], hs=it["hs"]: e.tensor_tensor(rn[:], rn[:], gnw_t[:, hs], ALU.mult),
                           reads=[it["krn"], "gnw"], writes=[it["krn"]])
                for it in its:
                    p.pool(lambda e, rn=it["rn"], hs=it["hs"]: e.tensor_tensor(rn[:], rn[:], gnb_t[:, hs], ALU.add),
                           reads=[it["krn"], "gnb"], writes=[it["krn"]])
                for it in its:
                    p.pool(lambda e, rbf=it["rbf"], rn=it["rn"], Sg=Sg, i=it["i"], hs=it["hs"]: e.tensor_tensor(
                        rbf[:], rn[:], Sg[:, i, hs], ALU.mult),
                        reads=[it["krn"], kSg], writes=[it["krbf"]])
                for it in its:
                    pRT, kpRT = pRT_r.next()
                    p.pe(lambda e, pRT=pRT, rbf=it["rbf"]: e.transpose(pRT[:], rbf[:], ident[:]), reads=[it["krbf"]], writes=[kpRT])
                    p.act(lambda e, pRT=pRT, h=it["h"], j=it["j"]: e.activation(
                        mixT[:, 8 + h, j * 128:(j + 1) * 128], pRT[:], AF.Copy),
                        writes=[("mixT", 8 + it["h"], it["j"]), kpRT])
            if debug:
                for c in range(16):
                    p.dma("sp", mixT_d[c], mixT[:, c, :], reads=[("mixT", c, j) for j in range(NTO)], writes=[("mixTd", c)])
            run_prog(nc, p, G)

        if stages < 4:
            return nc

        with contextlib.ExitStack() as st:
            def sb(name, shape, dt):
                return st.enter_context(nc.sbuf_tensor("s4_" + name, shape, dt))

            def ps(name, shape, dt):
                return st.enter_context(nc.psum_tensor("p4_" + name, shape, dt))

            Wo = [sb("Wo%d" % g, [128, 16, 512], BF16) for g in range(4)]
            x1t_r = Ring("x1t", [sb("x1t%d" % i, [128, D], F32) for i in range(2)])
            xt_r = Ring("xt", [sb("xt%d" % i, [128, D], F32) for i in range(2)])
            junk = sb("junk", [128, 512], BF16)
            wbc = sb("wbc", [128, D], F32)
            ss4 = sb("ss4", [128, NTO, 4], F32)
            rs = sb("rs", [128, NTO], F32)
            pM = [ps("pM%d" % i, [128, 512], F32) for i in range(8)]
            p = Prog()
            p.dma("sp", wbc[:], nw[1], writes=["wbc"])
            for g in range(4):
                p.dma("pool", Wo[g][:], w_out_v[:, :, g * 512:(g + 1) * 512], writes=[("Wo", g)])
            for j in range(NTO):
                xt, kxt = xt_r.next()
                x1t, kx1 = x1t_r.next()
                p.dma("sp", xt[:], xin[NTP + j], writes=[kxt])
                banks = [(pM[(j % 2) * 4 + g], ("pM", (j % 2) * 4 + g)) for g in range(4)]
                for g in range(4):
                    pb, kpb = banks[g]
                    for kc in range(16):
                        p.pe(lambda e, pb=pb, g=g, kc=kc, j=j: e.matmul(
                            pb[:], mixT[:, kc, j * 128:(j + 1) * 128], Wo[g][:, kc, :], start=(kc == 0), stop=(kc == 15)),
                            reads=[("Wo", g)], writes=[kpb])
                for g in range(4):
                    pb, kpb = banks[g]
                    p.act(lambda e, pb=pb, j=j, g=g: e.activation(junk[:], pb[:], AF.Square, accum_out=ss4[:, j, g:g + 1]),
                          writes=["junk", ("ss4", j, g), kpb])
                p.dve(lambda e, j=j: e.tensor_reduce(rs[:, j:j + 1], ss4[:, j, :], AX.X, ALU.add),
                      reads=[("ss4", j, g) for g in range(4)], writes=[("rs", j)])
                p.dve(lambda e, j=j: e.tensor_scalar(rs[:, j:j + 1], rs[:, j:j + 1], 1.0 / D, 1e-6, ALU.mult, ALU.add),
                      reads=[("rs", j)], writes=[("rs", j)])
                p.act(lambda e, j=j: e.activation(rs[:, j:j + 1], rs[:, j:j + 1], AF.Sqrt), reads=[("rs", j)], writes=[("rs", j)])
                p.dve(lambda e, j=j: e.reciprocal(rs[:, j:j + 1], rs[:, j:j + 1]), reads=[("rs", j)], writes=[("rs", j)])
                for g in range(4):
                    pb, kpb = banks[g]
                    gs = slice(g * 512, (g + 1) * 512)
                    p.dve(lambda e, pb=pb, x1t=x1t, gs=gs, j=j: e.scalar_tensor_tensor(
                        x1t[:, gs], pb[:], rs[:, j:j + 1], wbc[:, gs], ALU.mult, ALU.mult),
                        reads=[("rs", j), "wbc"], writes=[(kx1, g), kpb])
                p.pool(lambda e, x1t=x1t, xt=xt: e.tensor_tensor(x1t[:], x1t[:], xt[:], ALU.add),
                       reads=[kxt] + [(kx1, g) for g in range(4)], writes=[(kx1, "f")])
                p.dma("sp", x1_d[j], x1t[:], reads=[(kx1, "f")], writes=[("x1d", j)] + [(kx1, g) for g in range(4)])
            run_prog(nc, p, G)

        mst.close()
        if stages < 5:
            return nc

        with contextlib.ExitStack() as st:
            def sb(name, shape, dt):
                return st.enter_context(nc.sbuf_tensor("s5_" + name, shape, dt))

            def ps(name, shape, dt):
                return st.enter_context(nc.psum_tensor("p5_" + name, shape, dt))

            wbc2 = sb("wbc2", [128, D], F32)
            wbc3 = sb("wbc3", [128, D], F32)
            identf = sb("identf", [128, 128], F32)
            x1_r = Ring("x1t", [sb("x1t%d" % i, [128, D], F32) for i in range(2)])
            junk = sb("junk", [128, D], BF16)
            xn = sb("xn", [128, D], F32)
            uT = sb("uT", [128, 16, 512], BF16)
            uTh = sb("uTh", [128, 16, 2], BF16)
            hT = sb("hT", [128, NCT, 512], BF16)
            Wg_r = Ring("Wg", [sb("Wg%d" % i, [128, 16, 256], BF16) for i in range(3)])
            Wv_r = Ring("Wv", [sb("Wv%d" % i, [128, 16, 256], BF16) for i in range(3)])
            Wd_r = Ring("Wd", [sb("Wd%d" % i, [128, 4, 512], BF16) for i in range(3)])
            ug_r = Ring("ug", [sb("ug%d" % i, [128, 514], F32) for i in range(2)])
            tA_r = Ring("tA", [sb("tA%d" % i, [128, 512], F32) for i in range(2)])
            carry = sb("carry", [128, NCT, 2], F32)
            fo = sb("fo", [128, 4, D], F32)
            cw = sb("cw", [128, NCT, 3], F32)
            cbt = sb("cbt", [128, NCT], F32)
            ss = sb("ss", [128, 64], F32)
            rs = sb("rs", [128, 64], F32)
            pD = [ps("pD%d" % i, [128, 512], F32) for i in range(4)]
            pG_r = Ring("pG", [ps("pG%d" % i, [128, 512], F32) for i in range(2)])
            pV_r = Ring("pV", [ps("pV%d" % i, [128, 512], F32) for i in range(2)])

            p = Prog()
            p.dma("sp", wbc2[:], nw[2], writes=["wbc2"])
            p.dma("sp", wbc3[:], nw[3], writes=["wbc3"])
            p.dma("sp", identf[:], ident_d, writes=["identf"])
            p.dma("sp", cw[:], convw, writes=["cw"])
            p.dma("sp", cbt[:], convb, writes=["cbt"])
            sidx = [0]

            def rstd_ops(src_ap, src_keys):
                k = sidx[0]
                sidx[0] += 1
                p.act(lambda e, k=k: e.activation(junk[:], src_ap, AF.Square, accum_out=ss[:, k:k + 1]),
                      reads=src_keys, writes=["junk", ("ss", k)])
                p.dve(lambda e, k=k: e.tensor_scalar(rs[:, k:k + 1], ss[:, k:k + 1], 1.0 / D, 1e-6, ALU.mult, ALU.add),
                      reads=[("ss", k)], writes=[("rs", k)])
                p.act(lambda e, k=k: e.activation(rs[:, k:k + 1], rs[:, k:k + 1], AF.Sqrt), reads=[("rs", k)], writes=[("rs", k)])
                p.dve(lambda e, k=k: e.reciprocal(rs[:, k:k + 1], rs[:, k:k + 1]), reads=[("rs", k)], writes=[("rs", k)])
                return k, ("rs", k)

            def u_prep(j, halo, tt):
                x1t, kx1 = x1_r.next()
                p.dma("sp", x1t[:], x1_d[j], writes=[kx1])
                k, krs = rstd_ops(x1t[:], [kx1])
                p.dve(lambda e, x1t=x1t, k=k: e.scalar_tensor_tensor(xn[:], x1t[:], rs[:, k:k + 1], wbc2[:], ALU.mult, ALU.mult),
                      reads=[kx1, krs, "wbc2"], writes=["xn"])
                for kc in range(16):
                    p.pe(lambda e, kc=kc: e.transpose(pD[kc // 4][:, (kc % 4) * 128:(kc % 4 + 1) * 128],
                                                      xn[:, kc * 128:(kc + 1) * 128], identf[:]),
                         reads=["xn", "identf"], writes=[("pD", kc // 4)])
                for b in range(4):
                    src = pD[b][:].rearrange("p (k c) -> p k c", k=4)
                    if halo:
                        dst = uTh[:, b * 4:(b + 1) * 4, :]
                        srcv = src[:, :, 126:128]
                        wk = ("uTh", b)
                    else:
                        dst = uT[:, b * 4:(b + 1) * 4, tt * 128:(tt + 1) * 128]
                        srcv = src
                        wk = ("uT", tt, b)
                    if b % 2 == 0:
                        p.act(lambda e, dst=dst, srcv=srcv: e.activation(dst, srcv, AF.Copy), writes=[wk, ("pD", b)])
                    else:
                        p.dve(lambda e, dst=dst, srcv=srcv: e.tensor_copy(dst, srcv), writes=[wk, ("pD", b)])

            up_src, dn_src = [], []
            for gi in range(4):
                for cg in range(NCT // 2):
                    up_src.append((w_up_v[:, :, cg * 256:(cg + 1) * 256], w_up_v[:, :, DFF + cg * 256:DFF + (cg + 1) * 256]))
                for ng in range(4):
                    for pc in range(NCT // 4):
                        dn_src.append(w_down_v[:, pc * 4:(pc + 1) * 4, ng * 512:(ng + 1) * 512])
            Gpf = Prefetch(p, Wg_r, [a for a, _ in up_src])
            Vpf = Prefetch(p, Wv_r, [b for _, b in up_src])
            Dpf = Prefetch(p, Wd_r, dn_src)
            for gi in range(4):
                if gi == 0:
                    u_prep(0, True, 0)
                for tt in range(4):
                    u_prep(1 + 4 * gi + tt, False, tt)
                uT_keys = [("uT", tt, b) for tt in range(4) for b in range(4)]
                uTh_keys = [("uTh", b) for b in range(4)]
                Dpf.ensure(gi * 44 + Dpf.R - 1)
                for cg in range(NCT // 2):
                    ku = gi * (NCT // 2) + cg
                    Gpf.ensure(ku + Gpf.R - 1)
                    Vpf.ensure(ku + Vpf.R - 1)
                    Wg, kWg = Gpf.get(ku)
                    Wv, kWv = Vpf.get(ku)
                    for ci in range(2):
                        ct = cg * 2 + ci
                        cs = slice(ci * 128, (ci + 1) * 128)
                        ug, kug = ug_r.next()
                        tA, ktA = tA_r.next()
                        pG, kpG = pG_r.next()
                        pV, kpV = pV_r.next()
                        if gi == 0:
                            for kc in range(16):
                                p.pe(lambda e, pV=pV, Wg=Wg, kc=kc, cs=cs: e.matmul(
                                    pV[:, 0:2], Wg[:, kc, cs], uTh[:, kc, :], start=(kc == 0), stop=(kc == 15)),
                                    reads=[kWg] + uTh_keys, writes=[kpV])
                            p.act(lambda e, ug=ug, pV=pV: e.activation(ug[:, 0:2], pV[:, 0:2], AF.Copy), writes=[(kug, "h"), kpV])
                        else:
                            p.act(lambda e, ug=ug, ct=ct: e.activation(ug[:, 0:2], carry[:, ct, :], AF.Copy),
                                  reads=[("carry", ct)], writes=[(kug, "h")])
                        for kc in range(16):
                            p.pe(lambda e, pG=pG, Wg=Wg, kc=kc, cs=cs: e.matmul(
                                pG[:], Wg[:, kc, cs], uT[:, kc, :], start=(kc == 0), stop=(kc == 15)),
                                reads=[kWg] + uT_keys, writes=[kpG])
                        p.act(lambda e, ug=ug, pG=pG: e.activation(ug[:, 2:514], pG[:], AF.Copy), writes=[(kug, "m"), kpG])
                        if gi < 3:
                            p.act(lambda e, ug=ug, ct=ct: e.activation(carry[:, ct, :], ug[:, 512:514], AF.Copy),
                                  reads=[(kug, "m")], writes=[("carry", ct)])
                        for kc in range(16):
                            p.pe(lambda e, pV=pV, Wv=Wv, kc=kc, cs=cs: e.matmul(
                                pV[:], Wv[:, kc, cs], uT[:, kc, :], start=(kc == 0), stop=(kc == 15)),
                                reads=[kWv] + uT_keys, writes=[kpV])
                        ugk = [(kug, "h"), (kug, "m")]
                        p.dve(lambda e, tA=tA, ug=ug, ct=ct: e.tensor_scalar(tA[:], ug[:, 0:512], cw[:, ct, 0:1], None, ALU.mult),
                              reads=ugk + ["cw"], writes=[ktA])
                        p.dve(lambda e, tA=tA, ug=ug, ct=ct: e.scalar_tensor_tensor(
                            tA[:], ug[:, 1:513], cw[:, ct, 1:2], tA[:], ALU.mult, ALU.add),
                            reads=ugk + ["cw", ktA], writes=[ktA])
                        p.dve(lambda e, tA=tA, ug=ug, ct=ct: e.scalar_tensor_tensor(
                            tA[:], ug[:, 2:514], cw[:, ct, 2:3], tA[:], ALU.mult, ALU.add),
                            reads=ugk + ["cw", ktA], writes=[ktA])
                        p.act(lambda e, tA=tA, ct=ct: e.activation(tA[:], tA[:], AF.Silu, bias=cbt[:, ct:ct + 1]),
                              reads=[ktA, "cbt"], writes=[ktA])
                        p.dve(lambda e, tA=tA, pV=pV, ct=ct: e.tensor_tensor(hT[:, ct, :], tA[:], pV[:], ALU.mult),
                              reads=[ktA], writes=[("hT", ct), kpV])
                hT_keys = [("hT", ct) for ct in range(NCT)]
                if gi < 3:
                    Gpf.ensure((gi + 1) * (NCT // 2) + Gpf.R - 2)
                    Vpf.ensure((gi + 1) * (NCT // 2) + Vpf.R - 2)
                for ng in range(4):
                    for pc in range(NCT // 4):
                        Wd, kWd = Dpf.get(gi * 44 + ng * 11 + pc)
                        for cc in range(4):
                            ct = pc * 4 + cc
                            for tt in range(4):
                                p.pe(lambda e, tt=tt, ct=ct, cc=cc, Wd=Wd: e.matmul(
                                    pD[tt][:], hT[:, ct, tt * 128:(tt + 1) * 128], Wd[:, cc, :],
                                    start=(ct == 0), stop=(ct == NCT - 1)),
                                    reads=[("hT", ct), kWd], writes=[("pD", tt)])
                    for tt in range(4):
                        p.act(lambda e, tt=tt, ng=ng: e.activation(fo[:, tt, ng * 512:(ng + 1) * 512], pD[tt][:], AF.Copy),
                              writes=[("fo", tt, ng), ("pD", tt)])
                for tt in range(4):
                    j = 1 + 4 * gi + tt
                    fok = [("fo", tt, ng) for ng in range(4)]
                    x1t, kx1 = x1_r.next()
                    p.dma("sp", x1t[:], x1_d[j], writes=[kx1])
                    k, krs = rstd_ops(fo[:, tt, :], fok)
                    p.dve(lambda e, tt=tt, k=k: e.scalar_tensor_tensor(
                        fo[:, tt, :], fo[:, tt, :], rs[:, k:k + 1], wbc3[:], ALU.mult, ALU.mult),
                        reads=[krs, "wbc3"], writes=fok)
                    p.dve(lambda e, tt=tt, x1t=x1t: e.tensor_tensor(fo[:, tt, :], fo[:, tt, :], x1t[:], ALU.add),
                          reads=[kx1], writes=fok)
                    p.dma("sp", out[j - 1], fo[:, tt, :], reads=fok, writes=[("out", j)])
            run_prog(nc, p, G)
    return nc


def _tables(half):
    start = half * 2048 - 2048
    pos = (start + np.arange(4096)).astype(np.float64)
    pos = np.maximum(pos, 0.0).astype(np.float32)
    inv = (1.0 / (np.float32(10000.0) ** (np.arange(0, 128, 2, dtype=np.float32) / np.float32(128)))).astype(np.float32)
    ang = pos[:, None] * inv[None, :]
    cos = np.cos(ang).astype(np.float32).reshape(32, 128, 64).transpose(1, 0, 2)
    sin = np.sin(ang).astype(np.float32).reshape(32, 128, 64).transpose(1, 0, 2)
    hidx = np.arange(8, dtype=np.float32)
    log_g = np.log(1.0 - 2.0 ** (-5.0 - hidx)).astype(np.float64)
    pp = (np.arange(256, dtype=np.float64) + 1.0).reshape(2, 128).T
    qfac = np.exp(log_g[None, None, :] * pp[:, :, None]).astype(np.float32)
    kfac = (np.exp(-log_g[None, None, :] * pp[:, :, None]) * (128.0 ** -0.5)).astype(np.float32)
    gch = np.broadcast_to(np.exp(log_g * 256.0).astype(np.float32)[None, :], (128, 8)).copy()
    kk = np.arange(128)
    tri01 = (kk[:, None] <= kk[None, :]).astype(np.float32)
    trib = np.where(kk[None, :] <= kk[:, None], 0.0, -BIG).astype(np.float32)
    gb = np.full((128, NTO, 16), NEG, dtype=np.float32)
    for j in range(NTO):
        qb = (NTP + j) // 2
        for n in range(qb):
            if half == 1 or n >= 8:
                gb[:, j, n] = 0.0
    return dict(cos_t=np.ascontiguousarray(cos), sin_t=np.ascontiguousarray(sin), qfac=qfac, kfac=kfac,
                gch=gch, tri01=tri01, trib=trib, gbias=gb, ident=np.eye(128, dtype=np.float32))


def make_in_maps(x, norm_mix_pre, w_in, ret_gn_w, ret_gn_b, w_out, norm_mix_post, norm_ffn_pre,
                 w_up, conv_w, conv_b, w_down, norm_ffn_post):
    f = np.float32
    x = np.asarray(x, f)
    shared = dict(
        w_in=np.ascontiguousarray(np.asarray(w_in, f)[0]),
        w_out=np.ascontiguousarray(np.asarray(w_out, f)[0]),
        w_up=np.ascontiguousarray(np.asarray(w_up, f)[0]),
        w_down=np.ascontiguousarray(np.asarray(w_down, f)[0]),
        nw0=np.ascontiguousarray(np.broadcast_to(np.asarray(norm_mix_pre, f)[0][None, :], (128, D))),
        nw1=np.ascontiguousarray(np.broadcast_to(np.asarray(norm_mix_post, f)[0][None, :], (128, D))),
        nw2=np.ascontiguousarray(np.broadcast_to(np.asarray(norm_ffn_pre, f)[0][None, :], (128, D))),
        nw3=np.ascontiguousarray(np.broadcast_to(np.asarray(norm_ffn_post, f)[0][None, :], (128, D))),
        gnw=np.ascontiguousarray(np.broadcast_to(np.asarray(ret_gn_w, f)[0][None, :], (128, 1024))),
        gnb=np.ascontiguousarray(np.broadcast_to(np.asarray(ret_gn_b, f)[0][None, :], (128, 1024))),
        convw=np.ascontiguousarray(np.asarray(conv_w, f)[0].reshape(3, NCT, 128).transpose(2, 1, 0)),
        convb=np.ascontiguousarray(np.asarray(conv_b, f)[0].reshape(NCT, 128).T),
    )
    tabs = [_tables(0), _tables(1)]
    maps = []
    for c in range(8):
        b, half = c // 2, c % 2
        xi = np.zeros((4096, D), f)
        if half == 0:
            xi[2048:] = x[b, :2048]
        else:
            xi[:] = x[b]
        m = dict(shared)
        m.update(tabs[half])
        m["xin"] = xi.reshape(32, 128, D)
        maps.append(m)
    return maps


_NC_CACHE = {}


def kernel(**inputs):
    if "nc" not in _NC_CACHE:
        _NC_CACHE["nc"] = build_nc()
    nc = _NC_CACHE["nc"]
    maps = make_in_maps(**inputs)
    res = run_bass_kernel_spmd(nc, maps, core_ids=list(range(8)))
    outp = np.empty((4, 4096, D), np.float32)
    for c in range(8):
        b, half = c // 2, c % 2
        outp[b, half * 2048:(half + 1) * 2048] = np.asarray(res.results[c]["out"]).reshape(2048, D)
    return outp
```

```python
import contextlib
import numpy as np
import ml_dtypes
import concourse.bass as bass
import concourse.mybir as mybir
from concourse.bass_utils import run_bass_kernel_spmd

F32 = mybir.dt.float32
BF16 = mybir.dt.bfloat16
ALU = mybir.AluOpType
AF = mybir.ActivationFunctionType
AX = mybir.AxisListType

N_DMA_SEMS = 40
D = 2048
NTP = 15
NTO = 17
DFF = 5632
NCT = DFF // 128
BIG = 30000.0
NEG = -1.0e30


class Op:
    __slots__ = ("idx", "eng", "fn", "deps", "is_dma", "signal", "count", "dsem", "dtarget")

    def __init__(self, idx, eng, fn, is_dma):
        self.idx = idx
        self.eng = eng
        self.fn = fn
        self.is_dma = is_dma
        self.deps = set()
        self.signal = False
        self.count = 0
        self.dsem = None
        self.dtarget = 0


class Prog:
    COMPUTE = ("pe", "act", "dve", "pool")

    def __init__(self):
        self.ops = []
        self.last_w = {}
        self.readers = {}

    def add(self, eng, fn, reads=(), writes=(), dma=False):
        op = Op(len(self.ops), eng, fn, dma)
        deps = op.deps
        for k in reads:
            w = self.last_w.get(k)
            if w is not None:
                deps.add(w)
        for k in writes:
            w = self.last_w.get(k)
            if w is not None:
                deps.add(w)
            r = self.readers.get(k)
            if r:
                deps.update(r)
        for k in reads:
            self.readers.setdefault(k, []).append(op.idx)
        for k in writes:
            self.last_w[k] = op.idx
            self.readers[k] = []
        deps.discard(op.idx)
        self.ops.append(op)
        return op

    def pe(self, fn, reads=(), writes=()):
        return self.add("pe", fn, reads, writes)

    def act(self, fn, reads=(), writes=()):
        return self.add("act", fn, reads, writes)

    def dve(self, fn, reads=(), writes=()):
        return self.add("dve", fn, reads, writes)

    def pool(self, fn, reads=(), writes=()):
        return self.add("pool", fn, reads, writes)

    def dma(self, q, out, in_, reads=(), writes=()):
        return self.add(q, lambda e: e.dma_start(out=out, in_=in_), reads, writes, dma=True)

    @staticmethod
    def _needs_wait(x, y):
        if y.is_dma or x.is_dma:
            return True
        if x.eng == y.eng and x.eng == "pe":
            return False
        return True

    def plan(self, G):
        ops = self.ops
        self.base_cnt = dict(G["cnt"])
        self.base_tgt = list(G["tgt"])
        last_on_sem = [None] * N_DMA_SEMS
        tgt = G["tgt"]
        k = G["k"]
        NSP = 24
        for op in ops:
            if op.is_dma:
                if op.eng == "sp":
                    s = k["sp"] % NSP
                    k["sp"] += 1
                else:
                    s = NSP + k["pool"] % (N_DMA_SEMS - NSP)
                    k["pool"] += 1
                if last_on_sem[s] is not None:
                    op.deps.add(last_on_sem[s])
                tgt[s] += 16
                op.dsem = s
                op.dtarget = tgt[s]
                last_on_sem[s] = op.idx
        G["k"] = k
        for op in ops:
            for d in op.deps:
                y = ops[d]
                if not y.is_dma and self._needs_wait(op, y):
                    y.signal = True
        last = {}
        for op in ops:
            if not op.is_dma:
                last[op.eng] = op
        for op in last.values():
            op.signal = True
        cnt = G["cnt"]
        for op in ops:
            if not op.is_dma and op.signal:
                cnt[op.eng] += 1
                op.count = cnt[op.eng]

    def emit(self, ename, eng, G):
        ops = self.ops
        csems, dsems = G["c"], G["d"]
        waited = {}
        for e_ in self.COMPUTE:
            if self.base_cnt[e_] > 0:
                eng.wait_ge(csems[e_], self.base_cnt[e_])
                waited[("c", e_)] = self.base_cnt[e_]
        for s_ in range(N_DMA_SEMS):
            if self.base_tgt[s_] > 0:
                eng.wait_ge(dsems[s_], self.base_tgt[s_])
                waited[("d", s_)] = self.base_tgt[s_]
        mydma = {}
        for op in ops:
            if op.eng != ename:
                continue
            for d in sorted(op.deps):
                y = ops[d]
                if not self._needs_wait(op, y):
                    continue
                if y.is_dma:
                    key, val, sem = ("d", y.dsem), y.dtarget, dsems[y.dsem]
                else:
                    key, val, sem = ("c", y.eng), y.count, csems[y.eng]
                if waited.get(key, 0) >= val:
                    continue
                waited[key] = val
                eng.wait_ge(sem, val)
            ins = op.fn(eng)
            if op.is_dma:
                ins.then_inc(dsems[op.dsem], 16)
                mydma[op.dsem] = op.dtarget
            elif op.signal:
                ins.then_inc(csems[op.eng], 1)
        for s, v in sorted(mydma.items()):
            if waited.get(("d", s), 0) < v:
                eng.wait_ge(dsems[s], v)


def make_sems(nc, st):
    return {"c": {e: st.enter_context(nc.semaphore("cs_" + e)) for e in Prog.COMPUTE},
            "d": [st.enter_context(nc.semaphore("ds_%d" % i)) for i in range(N_DMA_SEMS)],
            "cnt": {e: 0 for e in Prog.COMPUTE}, "tgt": [0] * N_DMA_SEMS, "k": {"sp": 0, "pool": 0}}


def run_prog(nc, prog, G):
    prog.plan(G)
    with nc.Block() as block:
        @block.tensor
        def _(e):
            prog.emit("pe", e, G)

        @block.scalar
        def _(e):
            prog.emit("act", e, G)

        @block.vector
        def _(e):
            prog.emit("dve", e, G)

        @block.gpsimd
        def _(e):
            prog.emit("pool", e, G)

        @block.sync
        def _(e):
            prog.emit("sp", e, G)


class Ring:
    def __init__(self, name, bufs):
        self.name = name
        self.bufs = bufs
        self.i = -1

    def next(self):
        self.i += 1
        s = self.i % len(self.bufs)
        return self.bufs[s], (self.name, s)


class Prefetch:
    def __init__(self, p, ring, srcs, queue="pool"):
        self.p, self.ring, self.srcs, self.queue = p, ring, srcs, queue
        self.slots = []
        self.R = len(ring.bufs)

    def ensure(self, k):
        k = min(k, len(self.srcs) - 1)
        while len(self.slots) <= k:
            buf, key = self.ring.next()
            self.p.dma(self.queue, buf[:], self.srcs[len(self.slots)], writes=[key])
            self.slots.append((buf, key))

    def get(self, k):
        self.ensure(k + self.R - 1)
        return self.slots[k]


GROUP_KIND = ["qa", "qa", "ka", "ka", "va", "va", "qr", "qr", "kr", "kr", "vr", "vr", "gr", "gr"]


def build_nc(debug=False, stages=5):
    nc = bass.Bass("TRN2", target_bir_lowering=False)
    ext = "ExternalInput"

    def din(name, shape, dt=F32):
        return nc.dram_tensor(name, shape, dt, kind=ext).ap()

    xin = din("xin", [32, 128, D])
    w_in = din("w_in", [D, 7168])
    w_out = din("w_out", [D, D])
    w_up = din("w_up", [D, 2 * DFF])
    w_down = din("w_down", [DFF, D])
    nw = [din("nw%d" % i, [128, D]) for i in range(4)]
    gnw = din("gnw", [128, 1024])
    gnb = din("gnb", [128, 1024])
    convw = din("convw", [128, NCT, 3])
    convb = din("convb", [128, NCT])
    cos_d = din("cos_t", [128, 32, 64])
    sin_d = din("sin_t", [128, 32, 64])
    qfac_d = din("qfac", [128, 2, 8])
    kfac_d = din("kfac", [128, 2, 8])
    gch_d = din("gch", [128, 8])
    tri01_d = din("tri01", [128, 128])
    trib_d = din("trib", [128, 128])
    gbias_d = din("gbias", [128, NTO, 16])
    ident_d = din("ident", [128, 128])
    out = nc.dram_tensor("out", [16, 128, D], F32, kind="ExternalOutput").ap()

    sk = "ExternalOutput" if debug else "Internal"

    def dsc(name, shape, dt):
        return nc.dram_tensor(name, shape, dt, kind=sk).ap()

    KaT_d = dsc("KaT_d", [8, 128, 4096], BF16)
    KrT_d = dsc("KrT_d", [8, 128, 4096], BF16)
    QaT_d = dsc("QaT_d", [8, 128, NTO * 128], BF16)
    QrT_d = dsc("QrT_d", [8, 128, NTO * 128], BF16)
    Va_d = dsc("Va_d", [32, 128, 1024], BF16)
    Vr_d = dsc("Vr_d", [32, 128, 1024], BF16)
    Kr_d = dsc("Kr_d", [32, 128, 1024], BF16)
    Sg_d = dsc("Sg_d", [NTO, 128, 1024], F32)
    mixT_d = dsc("mixT_d", [16, 128, NTO * 128], BF16)
    mo_d = dsc("mo_d", [NTO, 128, D], F32)
    x1_d = dsc("x1_d", [NTO, 128, D], F32)
    dbgB = dsc("dbgB", [128, NTO * 16 + NTO + NTO * 18], F32)
    dbgQ = dsc("dbgQ", [128, NTO * 128], BF16)
    dbgK = dsc("dbgK", [128, 2], BF16)

    w_in_v = w_in.rearrange("(kc p) n -> p kc n", p=128)
    w_out_v = w_out.rearrange("(kc p) n -> p kc n", p=128)
    w_up_v = w_up.rearrange("(kc p) n -> p kc n", p=128)
    w_down_v = w_down.rearrange("(ct p) n -> p ct n", p=128)

    with contextlib.ExitStack() as gst:
        def gsb(name, shape, dt):
            return gst.enter_context(nc.sbuf_tensor("g_" + name, shape, dt))

        ident = gsb("ident", [128, 128], BF16)
        G = make_sems(nc, gst)

        p = Prog()
        p.dma("pool", ident[:], ident_d, writes=["ident"])
        run_prog(nc, p, G)

        for seg in ("P", "O"):
            with contextlib.ExitStack() as st:
                def sb(name, shape, dt):
                    return st.enter_context(nc.sbuf_tensor("s1_" + name + seg, shape, dt))

                def ps(name, shape, dt):
                    return st.enter_context(nc.psum_tensor("p1_" + name + seg, shape, dt))

                NT = NTP if seg == "P" else NTO
                T0 = 0 if seg == "P" else NTP
                groups = [2, 3, 4, 5, 8, 9, 10, 11] if seg == "P" else list(range(14))

                hnT = sb("hnT", [128, 16, NTO * 128], BF16)
                xt_r = Ring("xt", [sb("xt%d" % i, [128, D], F32) for i in range(2)])
                xn_r = Ring("xn", [sb("xn%d" % i, [128, D], BF16) for i in range(2)])
                junk = sb("junk", [128, D], BF16)
                wbc = sb("wbc", [128, D], F32)
                ss = sb("ss", [128, NTO], F32)
                rs = sb("rs", [128, NTO], F32)
                W_r = Ring("W", [sb("W%d" % i, [128, 16, 512], BF16) for i in range(2)])
                cos_t = sb("cos", [128, 32, 64], F32)
                sin_t = sb("sin", [128, 32, 64], F32)
                qfac = sb("qfac", [128, 2, 8], F32)
                kfac = sb("kfac", [128, 2, 8], F32)
                rA = Ring("rA", [sb("rA%d" % i, [128, 4, 64], F32) for i in range(2)])
                rB = Ring("rB", [sb("rB%d" % i, [128, 4, 64], F32) for i in range(2)])
                rC = Ring("rC", [sb("rC%d" % i, [128, 4, 64], F32) for i in range(2)])
                rD = Ring("rD", [sb("rD%d" % i, [128, 4, 64], F32) for i in range(2)])
                rot_r = Ring("rot", [sb("rot%d" % i, [128, 4, 128], F32) for i in range(2)])
                tok_r = Ring("tok", [sb("tok%d" % i, [128, 4, 128], BF16) for i in range(6)])
                ft_r = Ring("ft", [sb("ft%d" % i, [128, 4, 128], BF16) for i in range(3)])
                sg_r = Ring("sg", [sb("sg%d" % i, [128, 512], F32) for i in range(2)])
                pA = ps("pA", [128, D], BF16)
                pM_r = Ring("pM", [ps("pM%d" % i, [128, 512], F32) for i in range(4)])
                pT_r = Ring("pT", [ps("pT%d" % i, [128, 4, 128], BF16) for i in range(2)])

                p = Prog()
                p.dma("sp", wbc[:], nw[0], writes=["wbc"])
                p.dma("sp", cos_t[:], cos_d, writes=["cos"])
                p.dma("sp", sin_t[:], sin_d, writes=["sin"])
                p.dma("sp", qfac[:], qfac_d, writes=["qfac"])
                p.dma("sp", kfac[:], kfac_d, writes=["kfac"])

                for j in range(NT):
                    t = T0 + j
                    xt, kxt = xt_r.next()
                    xn, kxn = xn_r.next()
                    p.dma("sp", xt[:], xin[t], writes=[kxt])
                    p.act(lambda e, xt=xt, j=j: e.activation(junk[:], xt[:], AF.Square, accum_out=ss[:, j:j + 1]),
                          reads=[kxt], writes=["junk", ("ss", j)])
                    p.dve(lambda e, j=j: e.tensor_scalar(rs[:, j:j + 1], ss[:, j:j + 1], 1.0 / D, 1e-6, ALU.mult, ALU.add),
                          reads=[("ss", j)], writes=[("rs", j)])
                    p.act(lambda e, j=j: e.activation(rs[:, j:j + 1], rs[:, j:j + 1], AF.Sqrt),
                          reads=[("rs", j)], writes=[("rs", j)])
                    p.dve(lambda e, j=j: e.reciprocal(rs[:, j:j + 1], rs[:, j:j + 1]),
                          reads=[("rs", j)], writes=[("rs", j)])
                    p.dve(lambda e, xt=xt, xn=xn, j=j: e.scalar_tensor_tensor(
                        xn[:], xt[:], rs[:, j:j + 1], wbc[:], ALU.mult, ALU.mult),
                        reads=[kxt, ("rs", j), "wbc"], writes=[kxn])
                    for kc in range(16):
                        p.pe(lambda e, xn=xn, kc=kc: e.transpose(pA[:, kc * 128:(kc + 1) * 128],
                                                                   xn[:, kc * 128:(kc + 1) * 128], ident[:]),
                             reads=[kxn], writes=["pA"])
                    p.act(lambda e, j=j: e.activation(hnT[:, 0:8, j * 128:(j + 1) * 128],
                                                      pA[:, 0:1024].rearrange("p (k c) -> p k c", k=8), AF.Copy),
                          reads=["pA"], writes=[("hnT", j, 0)])
                    p.dve(lambda e, j=j: e.tensor_copy(hnT[:, 8:16, j * 128:(j + 1) * 128],
                                                       pA[:, 1024:2048].rearrange("p (k c) -> p k c", k=8)),
                          reads=["pA"], writes=[("hnT", j, 1)])

                deferred = []
                Wpf = Prefetch(p, W_r, [w_in_v[:, :, g * 512:(g + 1) * 512] for g in groups])
                for gi_, g in enumerate(groups):
                    kind = GROUP_KIND[g]
                    g4 = g % 2
                    Wb, kW = Wpf.get(gi_)
                    for j in range(NT):
                        t = T0 + j
                        pM, kpM = pM_r.next()
                        for kc in range(16):
                            p.pe(lambda e, pM=pM, Wb=Wb, kc=kc, j=j: e.matmul(
                                pM[:], hnT[:, kc, j * 128:(j + 1) * 128], Wb[:, kc, :],
                                start=(kc == 0), stop=(kc == 15)),
                                reads=[("hnT", j, 0), ("hnT", j, 1), kW], writes=[kpM])
                        keep = []
                        for ent in deferred:
                            ent[0] -= 1
                            if ent[0] <= 0:
                                ent[1]()
                            else:
                                keep.append(ent)
                        deferred[:] = keep
                        if kind in ("va", "vr"):
                            tok, ktok = tok_r.next()
                            p.act(lambda e, tok=tok, pM=pM: e.activation(
                                tok[:].rearrange("p h d -> p (h d)"), pM[:], AF.Copy),
                                reads=[kpM], writes=[ktok])
                            dst = Va_d if kind == "va" else Vr_d
                            p.dma("sp", dst[t][:, g4 * 512:(g4 + 1) * 512], tok[:].rearrange("p h d -> p (h d)"),
                                  reads=[ktok], writes=[(kind, t, g4)])
                            continue
                        if kind == "gr":
                            sg, ksg = sg_r.next()
                            p.act(lambda e, sg=sg, pM=pM: e.activation(sg[:], pM[:], AF.Silu),
                                  reads=[kpM], writes=[ksg])
                            p.dma("sp", Sg_d[j][:, g4 * 512:(g4 + 1) * 512], sg[:], reads=[ksg], writes=[("sgd", j, g4)])
                            continue
                        src = pM[:].rearrange("p (h d) -> p h d", h=4)
                        t1 = src[:, :, 0:64]
                        t2 = src[:, :, 64:128]
                        cb = cos_t[:, t:t + 1, :].to_broadcast([128, 4, 64])
                        sbb = sin_t[:, t:t + 1, :].to_broadcast([128, 4, 64])
                        A, kA = rA.next()
                        B, kB = rB.next()
                        C, kC = rC.next()
                        Dd, kD = rD.next()
                        tok, ktok = tok_r.next()
                        is_ret = kind in ("qr", "kr")
                        if is_ret:
                            rot, krot = rot_r.next()
                            dst, kdst = rot, krot
                        else:
                            dst, kdst = tok, ktok
                        p.dve(lambda e, A=A, t1=t1, cb=cb: e.tensor_tensor(A[:], t1, cb, ALU.mult),
                              reads=[kpM, "cos"], writes=[kA])
                        p.dve(lambda e, B=B, t2=t2, sbb=sbb: e.tensor_tensor(B[:], t2, sbb, ALU.mult),
                              reads=[kpM, "sin"], writes=[kB])
                        p.dve(lambda e, C=C, t1=t1, sbb=sbb: e.tensor_tensor(C[:], t1, sbb, ALU.mult),
                              reads=[kpM, "sin"], writes=[kC])
                        p.dve(lambda e, Dd=Dd, t2=t2, cb=cb: e.tensor_tensor(Dd[:], t2, cb, ALU.mult),
                              reads=[kpM, "cos"], writes=[kD])
                        p.pool(lambda e, dst=dst, A=A, B=B: e.tensor_tensor(dst[:, :, 0:64], A[:], B[:], ALU.subtract),
                               reads=[kA, kB], writes=[(kdst, "lo")])
                        p.pool(lambda e, dst=dst, C=C, Dd=Dd: e.tensor_tensor(dst[:, :, 64:128], C[:], Dd[:], ALU.add),
                               reads=[kC, kD], writes=[(kdst, "hi")])
                        if is_ret:
                            fac = qfac if kind == "qr" else kfac
                            par = t % 2
                            for hh in range(4):
                                head = g4 * 4 + hh
                                p.act(lambda e, tok=tok, rot=rot, hh=hh, fac=fac, par=par, head=head: e.activation(
                                    tok[:, hh, :], rot[:, hh, :], AF.Copy, scale=fac[:, par, head:head + 1]),
                                    reads=[(krot, "lo"), (krot, "hi"), "qfac", "kfac"], writes=[(ktok, hh)])
                            tokreads = [(ktok, hh) for hh in range(4)]
                        else:
                            tokreads = [(ktok, "lo"), (ktok, "hi")]
                        if kind == "kr":
                            p.dma("sp", Kr_d[t][:, g4 * 512:(g4 + 1) * 512], tok[:].rearrange("p h d -> p (h d)"),
                                  reads=tokreads, writes=[("krd", t, g4)])
                        need_T = kind in ("ka", "kr") or seg == "O"
                        if not need_T:
                            continue

                        def tail(tok=tok, tokreads=tokreads, kind=kind, g4=g4, t=t, j=j):
                            pT, kpT = pT_r.next()
                            for hh in range(4):
                                p.pe(lambda e, pT=pT, tok=tok, hh=hh: e.transpose(pT[:, hh, :], tok[:, hh, :], ident[:]),
                                     reads=tokreads, writes=[kpT])
                            ft, kft = ft_r.next()
                            p.act(lambda e, ft=ft, pT=pT: e.activation(ft[:], pT[:], AF.Copy), reads=[kpT], writes=[kft])
                            if kind in ("ka", "kr"):
                                dd = KaT_d if kind == "ka" else KrT_d
                                dap = dd[g4 * 4:(g4 + 1) * 4].rearrange("h d t -> d h t")[:, :, t * 128:(t + 1) * 128]
                            else:
                                dd = QaT_d if kind == "qa" else QrT_d
                                dap = dd[g4 * 4:(g4 + 1) * 4].rearrange("h d t -> d h t")[:, :, j * 128:(j + 1) * 128]
                            p.dma("sp", dap, ft[:], reads=[kft], writes=[(kind + "T", t, g4)])
                        deferred.append([2, tail])
                for _, fn_ in deferred:
                    fn_()
                run_prog(nc, p, G)

        if stages < 2:
            return nc
        mst = gst.enter_context(contextlib.ExitStack())
        mixT = mst.enter_context(nc.sbuf_tensor("g_mixT", [128, 16, NTO * 128], BF16))
        SCALE = 128.0 ** -0.5

        with contextlib.ExitStack() as st:
            def sb(name, shape, dt):
                return st.enter_context(nc.sbuf_tensor("s2_" + name, shape, dt))

            def ps(name, shape, dt):
                return st.enter_context(nc.psum_tensor("p2_" + name, shape, dt))

            KT_r = Ring("KT", [sb("KT%d" % i, [128, 4096], BF16) for i in range(2)])
            V_r = Ring("V", [sb("V%d" % i, [128, 32, 128], BF16) for i in range(2)])
            QT_r = Ring("QT", [sb("QT%d" % i, [128, NTO * 128], BF16) for i in range(2)])
            QA_r = Ring("QA", [sb("QA%d" % i, [128, NTO * 128], BF16) for i in range(2)])
            gbias = sb("gbias", [128, NTO, 16], F32)
            trib = sb("trib", [128, 256], BF16)
            ksum = sb("ksum", [128, 16], F32)
            kmb_r = Ring("kmb", [sb("kmb%d" % i, [128, 16], BF16) for i in range(2)])
            kab = sb("kab", [128, 2], F32)
            kabb_r = Ring("kabb", [sb("kabb%d" % i, [128, 2], BF16) for i in range(2)])
            gb_r = Ring("gb", [sb("gb%d" % i, [128, 16], F32) for i in range(2)])
            t8_r = Ring("t8", [sb("t8%d" % i, [128, 8], F32) for i in range(2)])
            thr_r = Ring("thr", [sb("thr%d" % i, [128, 1], F32) for i in range(2)])
            s01_r = Ring("s01", [sb("s01%d" % i, [128, 16], F32) for i in range(2)])
            ball_r = Ring("ball", [sb("ball%d" % i, [128, NTO, 16], F32) for i in range(2)])
            negm_r = Ring("negm", [sb("negm%d" % i, [128, NTO], F32) for i in range(2)])
            rsum_r = Ring("rsum", [sb("rsum%d" % i, [128, NTO, 18], F32) for i in range(2)])
            rtot_r = Ring("rtot", [sb("rtot%d" % i, [128, 1], F32) for i in range(3)])
            Pb_r = Ring("Pb", [sb("Pb%d" % i, [128, 256], BF16) for i in range(6)])
            sm_r = Ring("sm", [sb("sm%d" % i, [128, 128], F32) for i in range(4)])
            PT_r = Ring("PT", [sb("PT%d" % i, [128, 2, 128], BF16) for i in range(5)])
            ab_r = Ring("ab", [sb("ab%d" % i, [128, 128], BF16) for i in range(3)])
            bkf = [ps("bkf%d" % i, [128, 512], F32) for i in range(6)]
            bkb = [ps("bkb%d" % i, [128, 1024], BF16) for i in range(2)]
            pO_r = Ring("pO", [bkf[0][:, 0:128], bkf[1][:, 0:128]])
            pS_r = Ring("pS", [bkf[2][:, 0:256], bkf[3][:, 0:256], bkf[4][:, 0:256]])
            pGB_r = Ring("pGB", [bkf[5][:, 0:32]])
            pPT_r = Ring("pPT", [bkb[i][:, 0:256].rearrange("p (k c) -> p k c", k=2) for i in range(2)])
            pAT_r = Ring("pPT", [bkb[0][:, 512:640]])

            p = Prog()
            p.dma("sp", gbias[:], gbias_d, writes=["gbias"])
            p.dma("pool", trib[:, 128:256], trib_d, writes=["trib"])
            p.pool(lambda e: e.memset(trib[:, 0:128], 0.0), writes=["trib0"])

            def load_head(h):
                KT, kKT = KT_r.next()
                V, kV = V_r.next()
                QT, kQT = QT_r.next()
                p.dma("sp", KT[:], KaT_d[h], writes=[kKT])
                p.dma("sp", V[:], Va_d[:, :, h * 128:(h + 1) * 128].rearrange("t p c -> p t c"), writes=[kV])
                p.dma("sp", QT[:], QaT_d[h], writes=[kQT])
                return (KT, kKT, V, kV, QT, kQT)

            def prologue_pieces(hd):
                KT, kKT, V, kV, QT, kQT = hd
                QA, kQA = QA_r.next()
                kmb, kkmb = kmb_r.next()
                kabb, kkabb = kabb_r.next()
                ball, kball = ball_r.next()
                negm, knegm = negm_r.next()
                rsum, krsum = rsum_r.next()
                bufs = dict(QA=QA, kQA=kQA, ball=ball, kball=kball, negm=negm, knegm=knegm, rsum=rsum, krsum=krsum)
                pieces = []

                def head_ops():
                    p.dve(lambda e: e.tensor_reduce(ksum[:], KT[:].rearrange("p (n k) -> p n k", n=16), AX.X, ALU.add),
                          reads=[kKT], writes=["ksum"])
                    p.dve(lambda e: e.tensor_scalar(kmb[:], ksum[:], 1.0 / 256.0, None, ALU.mult),
                          reads=["ksum"], writes=[kkmb])
                    p.dve(lambda e: e.tensor_reduce(kab[:, 0:1], KT[:], AX.X, ALU.max, apply_absolute_value=True),
                          reads=[kKT], writes=["kab"])
                    p.dve(lambda e: e.tensor_copy(kab[:, 1:2], kab[:, 0:1]), reads=["kab"], writes=["kab1"])
                    p.dve(lambda e: e.tensor_scalar(kabb[:], kab[:], 1.02, None, ALU.mult),
                          reads=["kab", "kab1"], writes=[kkabb])
                    p.act(lambda e: e.activation(QA[:], QT[:], AF.Abs), reads=[kQT], writes=[kQA])
                pieces.append(head_ops)

                def tile_ops(j):
                    pGB, kpGB = pGB_r.next()
                    gb, kgb = gb_r.next()
                    t8, kt8 = t8_r.next()
                    thr, kthr = thr_r.next()
                    s01, ks01 = s01_r.next()
                    qs = slice(j * 128, (j + 1) * 128)
                    p.pe(lambda e: e.matmul(pGB[:, 0:16], QT[:, qs], kmb[:], start=True, stop=True),
                         reads=[kQT, kkmb], writes=[kpGB])
                    p.pe(lambda e: e.matmul(pGB[:, 16:18], QA[:, qs], kabb[:], start=True, stop=True),
                         reads=[kQA, kkabb], writes=[kpGB])
                    p.dve(lambda e: e.tensor_tensor(gb[:], pGB[:, 0:16], gbias[:, j, :], ALU.add),
                          reads=["gbias"], writes=[kgb, kpGB])
                    p.dve(lambda e: e.tensor_scalar(negm[:, j:j + 1], pGB[:, 16:17], -SCALE, None, ALU.mult),
                          reads=[], writes=[(knegm, j), kpGB])
                    p.dve(lambda e: e.max(t8[:], gb[:]), reads=[kgb], writes=[kt8])
                    p.dve(lambda e: e.tensor_scalar(thr[:], t8[:, 2:3], -1.0e29, None, ALU.max), reads=[kt8], writes=[kthr])
                    p.dve(lambda e: e.tensor_scalar(s01[:], gb[:], thr[:, 0:1], None, ALU.is_ge), reads=[kgb, kthr], writes=[ks01])
                    p.dve(lambda e: e.tensor_scalar(s01[:], s01[:], -1.0, BIG, ALU.add, ALU.mult), reads=[ks01], writes=[ks01])
                    p.dve(lambda e: e.tensor_scalar(ball[:, j, :], s01[:], negm[:, j:j + 1], None, ALU.add),
                          reads=[ks01, (knegm, j)], writes=[(kball, j)])
                for j_ in range(NTO):
                    pieces.append(lambda j_=j_: tile_ops(j_))
                return pieces, bufs

            nxt = load_head(0)
            pend, nxt_bufs = prologue_pieces(nxt)
            for pc_ in pend:
                pc_()
            pend = []
            evac_flip = [0]
            for h in range(8):
                KT, kKT, V, kV, QT, kQT = nxt
                cb_ = nxt_bufs
                ball, kball, negm, knegm, rsum, krsum = cb_["ball"], cb_["kball"], cb_["negm"], cb_["knegm"], cb_["rsum"], cb_["krsum"]
                if h + 1 < 8:
                    nxt = load_head(h + 1)
                    pend, nxt_bufs = prologue_pieces(nxt)

                items = []
                for j in range(NTO):
                    t = NTP + j
                    qb, half = t // 2, t % 2
                    blocks = list(range(qb + 1))
                    for bi, n in enumerate(blocks):
                        items.append(dict(j=j, n=n, own=(n == qb), half=half, first=(bi == 0),
                                          last=(bi == len(blocks) - 1), slot=bi))
                NI = len(items)
                pO_cur = {}

                def stage_S(it):
                    j, n = it["j"], it["n"]
                    nk = 128 if (it["own"] and it["half"] == 0) else 256
                    pS, kpS = pS_r.next()
                    it["pS"], it["kpS"], it["nk"] = pS, kpS, nk
                    qs = slice(j * 128, (j + 1) * 128)
                    own = it["own"]
                    p.pe(lambda e, QT=QT, KT=KT, pS=pS, qs=qs, n=n, nk=nk, own=own: e.matmul(
                        pS[:, 0:nk], QT[:, qs], KT[:, n * 256:n * 256 + nk], start=True, stop=(not own)),
                        reads=[kQT, kKT], writes=[kpS])
                    if own:
                        dc = it["half"] * 128
                        p.pe(lambda e, pS=pS, nk=nk: e.matmul(pS[:, 0:nk], ident[:], trib[:, 256 - nk:256], start=False, stop=True),
                             reads=["trib", "trib0"], writes=[kpS])

                def stage_E(it):
                    j, n, nk = it["j"], it["n"], it["nk"]
                    pS, kpS = it["pS"], it["kpS"]
                    Pb, kPb = Pb_r.next()
                    it["Pb"], it["kPb"] = Pb, kPb
                    if it["own"]:
                        bias_ap, bkey = negm[:, j:j + 1], (knegm, j)
                    else:
                        bias_ap, bkey = ball[:, j, n:n + 1], (kball, j)
                    p.act(lambda e, Pb=Pb, pS=pS, nk=nk, bias_ap=bias_ap, rsum=rsum, j=j, sl=it["slot"]: e.activation(
                        Pb[:, 0:nk], pS[:, 0:nk], AF.Exp, bias=bias_ap, scale=SCALE, accum_out=rsum[:, j, sl:sl + 1]),
                        reads=[bkey], writes=[kPb, (krsum, j, it["slot"]), kpS])

                def stage_T(it):
                    nk = it["nk"]
                    Pb, kPb = it["Pb"], it["kPb"]
                    pPT, kpPT = pPT_r.next()
                    it["pPT"], it["kpPT"] = pPT, kpPT
                    rd = it.get("kPb_parts", [kPb])
                    for kk in range(nk // 128):
                        p.pe(lambda e, QT=QT, KT=KT, V=V, ball=ball, negm=negm, rsum=rsum, pPT=pPT, Pb=Pb, kk=kk: e.transpose(pPT[:, kk, :], Pb[:, kk * 128:(kk + 1) * 128], ident[:]),
                             reads=rd, writes=[kpPT])

                def stage_V(it):
                    nkk = it["nk"] // 128
                    pPT, kpPT = it["pPT"], it["kpPT"]
                    PT, kPT = PT_r.next()
                    it["PT"], it["kPT"] = PT, kPT
                    if True:
                        p.dve(lambda e, QT=QT, KT=KT, V=V, ball=ball, negm=negm, rsum=rsum, PT=PT, pPT=pPT, nkk=nkk: e.tensor_copy(PT[:, 0:nkk, :], pPT[:, 0:nkk, :]),
                              writes=[kPT, kpPT])
                    else:
                        p.act(lambda e, QT=QT, KT=KT, V=V, ball=ball, negm=negm, rsum=rsum, PT=PT, pPT=pPT, nkk=nkk: e.activation(PT[:, 0:nkk, :], pPT[:, 0:nkk, :], AF.Copy),
                              writes=[kPT, kpPT])

                def stage_M(it):
                    j, n = it["j"], it["n"]
                    nkk = it["nk"] // 128
                    PT, kPT = it["PT"], it["kPT"]
                    if it["first"]:
                        pO_cur[j] = pO_r.next()
                    pO, kpO = pO_cur[j]
                    for kk in range(nkk):
                        p.pe(lambda e, QT=QT, KT=KT, V=V, ball=ball, negm=negm, rsum=rsum, pO=pO, PT=PT, kk=kk, n=n, st_=(it["first"] and kk == 0),
                             sp_=(it["last"] and kk == nkk - 1): e.matmul(pO[:], PT[:, kk, :], V[:, n * 2 + kk, :], start=st_, stop=sp_),
                             reads=[kPT, kV], writes=[kpO])
                    if it["last"]:
                        cnt = it["slot"] + 1
                        rtot, krtot = rtot_r.next()
                        ab, kab_ = ab_r.next()
                        rd = [(krsum, j, s_) for s_ in range(cnt)]

                        def f1(rtot=rtot, krtot=krtot, j=j, cnt=cnt, rd=rd, rsum=rsum):
                            p.dve(lambda e: e.tensor_reduce(rtot[:], rsum[:, j, 0:cnt], AX.X, ALU.add), reads=rd, writes=[krtot])
                            p.dve(lambda e: e.reciprocal(rtot[:], rtot[:]), reads=[krtot], writes=[krtot])

                        def f2(ab=ab, kab_=kab_, pO=pO, kpO=kpO, rtot=rtot, krtot=krtot):
                            p.act(lambda e: e.activation(ab[:], pO[:], AF.Copy, scale=rtot[:, 0:1]),
                                  reads=[krtot], writes=[kab_, kpO])

                        def f3(ab=ab, kab_=kab_, j=j, h=h):
                            pAT, kpAT = pAT_r.next()
                            p.pe(lambda e: e.transpose(pAT[:], ab[:], ident[:]), reads=[kab_], writes=[kpAT])
                            p.dve(lambda e: e.tensor_copy(mixT[:, h, j * 128:(j + 1) * 128], pAT[:]),
                                  writes=[("mixT", h, j), kpAT])
                        fin.append([1, f1])
                        fin.append([3, f2])
                        fin.append([5, f3])

                fin = []
                for i in range(NI + 4):
                    if pend and i % 10 == 9:
                        pend.pop(0)()
                    if i < NI:
                        stage_S(items[i])
                    if 0 <= i - 1 < NI:
                        stage_E(items[i - 1])
                    if 0 <= i - 3 < NI:
                        stage_T(items[i - 3])
                        stage_V(items[i - 3])
                    if 0 <= i - 4 < NI:
                        stage_M(items[i - 4])
                    keep_ = []
                    for ent in fin:
                        ent[0] -= 1
                        if ent[0] < 0:
                            ent[1]()
                        else:
                            keep_.append(ent)
                    fin[:] = keep_
                for ent in fin:
                    ent[1]()
                fin[:] = []
                while pend:
                    pend.pop(0)()
            if debug:
                for c in range(8):
                    p.dma("sp", mixT_d[c], mixT[:, c, :], reads=[("mixT", c, j) for j in range(NTO)], writes=[("mixTd", c)])
            run_prog(nc, p, G)

        if stages < 3:
            return nc

        with contextlib.ExitStack() as st:
            def sb(name, shape, dt):
                return st.enter_context(nc.sbuf_tensor("s3_" + name, shape, dt))

            def ps(name, shape, dt):
                return st.enter_context(nc.psum_tensor("p3_" + name, shape, dt))

            Kr_r = Ring("Kr", [sb("Kr%d" % i, [128, 2, 1024], BF16) for i in range(2)])
            Vr_r = Ring("Vr", [sb("Vr%d" % i, [128, 2, 1024], BF16) for i in range(2)])
            Qc_r = Ring("Qc", [sb("Qc%d" % i, [128, 8, 256], BF16) for i in range(2)])
            Kc_r = Ring("Kc", [sb("Kc%d" % i, [128, 8, 256], BF16) for i in range(2)])
            Sg_r = Ring("Sg", [sb("Sg%d" % i, [128, 2, 1024], F32) for i in range(2)])
            state = sb("state", [128, 8, 128], F32)
            sbf = [sb("sbf%d" % i, [128, 8, 128], BF16) for i in range(2)]
            st2_r = Ring("st2", [sb("st2%d" % i, [128, 128], F32) for i in range(8)])
            tri01 = sb("tri01", [128, 128], F32)
            gnw_t = sb("gnw", [128, 1024], F32)
            gnb_t = sb("gnb", [128, 1024], F32)
            gch = sb("gch", [128, 8], F32)
            epsT = sb("epsT", [128, 1], F32)
            NI3 = 16
            PTm_r = Ring("PTm", [sb("PTm%d" % i, [128, 256], BF16) for i in range(16)])
            ysb_r = Ring("ysb", [sb("ysb%d" % i, [128, 128], F32) for i in range(NI3)])
            bst_r = Ring("bst", [sb("bst%d" % i, [128, 6], F32) for i in range(NI3)])
            mv_r = Ring("mv", [sb("mv%d" % i, [128, 2], F32) for i in range(NI3)])
            rsd_r = Ring("rsd", [sb("rsd%d" % i, [128, 1], F32) for i in range(NI3)])
            rn_r = Ring("rn", [sb("rn%d" % i, [128, 128], F32) for i in range(NI3)])
            rbf_r = Ring("rbf", [sb("rbf%d" % i, [128, 128], BF16) for i in range(NI3)])
            pST_r = Ring("pST", [ps("pST%d" % i, [128, 256], F32) for i in range(2)])
            pY_r = Ring("pY", [ps("pY%d" % i, [128, 128], F32) for i in range(2)])
            pKV_r = Ring("pKV", [ps("pKV%d" % i, [128, 128], F32) for i in range(2)])
            pRT_r = Ring("pRT", [ps("pRT%d" % i, [128, 128], BF16) for i in range(2)])

            p = Prog()
            p.dma("sp", tri01[:], tri01_d, writes=["tri01"])
            p.dma("sp", gnw_t[:], gnw, writes=["gnw"])
            p.dma("sp", gnb_t[:], gnb, writes=["gnb"])
            p.dma("sp", gch[:], gch_d, writes=["gch"])
            p.pool(lambda e: e.memset(state[:], 0.0), writes=[("state", h) for h in range(8)])
            p.pool(lambda e: e.memset(sbf[0][:], 0.0), writes=[("sbf", 0, h) for h in range(8)])
            p.pool(lambda e: e.memset(epsT[:], 1e-5), writes=["eps"])
            for n in range(16):
                Kr, kKr = Kr_r.next()
                Vr, kVr = Vr_r.next()
                p.dma("sp", Kr[:], Kr_d[2 * n:2 * n + 2].rearrange("t p c -> p t c"), writes=[kKr])
                p.dma("sp", Vr[:], Vr_d[2 * n:2 * n + 2].rearrange("t p c -> p t c"), writes=[kVr])
                iset = []
                if n >= 7:
                    iset = [1] if n == 7 else [0, 1]
                    Qc, kQc = Qc_r.next()
                    Kc, kKc = Kc_r.next()
                    Sg, kSg = Sg_r.next()
                    if n == 7:
                        p.dma("sp", Qc[:, :, 128:256], QrT_d[:, :, 0:128].rearrange("h d t -> d h t"), writes=[kQc])
                        p.dma("sp", Sg[:, 1, :], Sg_d[0], writes=[kSg])
                    else:
                        j0 = 2 * n - NTP
                        p.dma("sp", Qc[:], QrT_d[:, :, j0 * 128:(j0 + 2) * 128].rearrange("h d t -> d h t"), writes=[kQc])
                        p.dma("sp", Sg[:], Sg_d[j0:j0 + 2].rearrange("t p c -> p t c"), writes=[kSg])
                    p.dma("sp", Kc[:], KrT_d[:, :, 2 * n * 128:(2 * n + 2) * 128].rearrange("h d t -> d h t"), writes=[kKc])
                sb_cur = sbf[n % 2]
                sb_nxt = sbf[(n + 1) % 2]
                its = []
                if iset:
                    PTs = {}
                    for h in range(8):
                        for jj in (0, 1):
                            qis = [i for i in iset if i >= jj]
                            q0 = min(qis) * 128
                            pST, kpST = pST_r.next()
                            PTm, kPTm = PTm_r.next()
                            PTs[(h, jj)] = (PTm, kPTm)
                            p.pe(lambda e, pST=pST, Kc=Kc, Qc=Qc, h=h, jj=jj, q0=q0: e.matmul(
                                pST[:, q0:256], Kc[:, h, jj * 128:(jj + 1) * 128], Qc[:, h, q0:256], start=True, stop=True),
                                reads=[kKc, kQc], writes=[kpST])
                            for i in qis:
                                cs = slice(i * 128, (i + 1) * 128)
                                if i == jj:
                                    p.dve(lambda e, PTm=PTm, pST=pST, cs=cs: e.tensor_tensor(PTm[:, cs], pST[:, cs], tri01[:], ALU.mult),
                                          reads=["tri01"], writes=[(kPTm, i), kpST])
                                else:
                                    p.act(lambda e, PTm=PTm, pST=pST, cs=cs: e.activation(PTm[:, cs], pST[:, cs], AF.Copy),
                                          writes=[(kPTm, i), kpST])
                    for h in range(8):
                        hs = slice(h * 128, (h + 1) * 128)
                        for i in iset:
                            j = 2 * n + i - NTP
                            cs = slice(i * 128, (i + 1) * 128)
                            pY, kpY = pY_r.next()
                            first = True
                            for jj in range(i + 1):
                                PTm, kPTm = PTs[(h, jj)]
                                p.pe(lambda e, pY=pY, PTm=PTm, Vr=Vr, cs=cs, jj=jj, hs=hs, first=first: e.matmul(
                                    pY[:], PTm[:, cs], Vr[:, jj, hs], start=first, stop=False),
                                    reads=[(kPTm, i), kVr], writes=[kpY])
                                first = False
                            p.pe(lambda e, pY=pY, Qc=Qc, cs=cs, h=h, sb_cur=sb_cur: e.matmul(
                                pY[:], Qc[:, h, cs], sb_cur[:, h, :], start=False, stop=True),
                                reads=[kQc, ("sbf", n % 2, h)], writes=[kpY])
                            it = dict(h=h, i=i, j=j, hs=hs)
                            it["ysb"], it["kysb"] = ysb_r.next()
                            it["bst"], it["kbst"] = bst_r.next()
                            it["mv"], it["kmv"] = mv_r.next()
                            it["rsd"], it["krsd"] = rsd_r.next()
                            it["rn"], it["krn"] = rn_r.next()
                            it["rbf"], it["krbf"] = rbf_r.next()
                            p.act(lambda e, ysb=it["ysb"], pY=pY: e.activation(ysb[:], pY[:], AF.Copy), writes=[it["kysb"], kpY])
                            its.append(it)
                if n < 15:
                    kvs = []
                    for h in range(8):
                        hs = slice(h * 128, (h + 1) * 128)
                        pKV, kpKV = pKV_r.next()
                        st2, kst2 = st2_r.next()
                        for jj in (0, 1):
                            p.pe(lambda e, pKV=pKV, Kr=Kr, Vr=Vr, jj=jj, hs=hs: e.matmul(
                                pKV[:], Kr[:, jj, hs], Vr[:, jj, hs], start=(jj == 0), stop=(jj == 1)),
                                reads=[kKr, kVr], writes=[kpKV])
                        p.dve(lambda e, st2=st2, pKV=pKV, h=h: e.tensor_tensor(st2[:], state[:, h, :], pKV[:], ALU.add),
                              reads=[("state", h)], writes=[kst2, kpKV])
                        kvs.append((h, st2, kst2))
                    for h, st2, kst2 in kvs:
                        p.act(lambda e, st2=st2, h=h: e.activation(state[:, h, :], st2[:], AF.Copy, scale=gch[:, h:h + 1]),
                              reads=[kst2, "gch"], writes=[("state", h)])
                    for h, st2, kst2 in kvs:
                        p.pool(lambda e, sb_nxt=sb_nxt, h=h: e.tensor_copy(sb_nxt[:, h, :], state[:, h, :]),
                               reads=[("state", h)], writes=[("sbf", (n + 1) % 2, h)])
                for it in its:
                    p.dve(lambda e, bst=it["bst"], ysb=it["ysb"]: e.bn_stats(bst[:], ysb[:]), reads=[it["kysb"]], writes=[it["kbst"]])
                for it in its:
                    p.dve(lambda e, mv=it["mv"], bst=it["bst"]: e.bn_aggr(mv[:], bst[:]), reads=[it["kbst"]], writes=[it["kmv"]])
                for it in its:
                    p.act(lambda e, rsd=it["rsd"], mv=it["mv"]: e.activation(rsd[:], mv[:, 1:2], AF.Sqrt, bias=epsT[:, 0:1]),
                          reads=[it["kmv"], "eps"], writes=[it["krsd"]])
                for it in its:
                    p.dve(lambda e, rsd=it["rsd"]: e.reciprocal(rsd[:], rsd[:]), reads=[it["krsd"]], writes=[it["krsd"]])
                for it in its:
                    p.dve(lambda e, rn=it["rn"], ysb=it["ysb"], mv=it["mv"], rsd=it["rsd"]: e.tensor_scalar(
                        rn[:], ysb[:], mv[:, 0:1], rsd[:, 0:1], ALU.subtract, ALU.mult),
                        reads=[it["kysb"], it["kmv"], it["krsd"]], writes=[it["krn"]])
                for it in its:
                    p.pool(lambda e, rn=it["rn"], hs=it["hs"]: e.tensor_tensor(rn[:], rn[:], gnw_t[:, hs], ALU.mult),
                           reads=[it["krn"], "gnw"], writes=[it["krn"]])
                for it in its:
                    p.pool(lambda e, rn=it["rn"], hs=it["hs"]: e.tensor_tensor(rn[:], rn[:], gnb_t[:, hs], ALU.add),
                           reads=[it["krn"], "gnb"], writes=[it["krn"]])
                for it in its:
                    p.pool(lambda e, rbf=it["rbf"], rn=it["rn"], Sg=Sg, i=it["i"], hs=it["hs"]: e.tensor_tensor(
                        rbf[:], rn[:], Sg[:, i, hs], ALU.mult),
                        reads=[it["krn"], kSg], writes=[it["krbf"]])
                for it in its:
                    pRT, kpRT = pRT_r.next()
                    p.pe(lambda e, pRT=pRT, rbf=it["rbf"]: e.transpose(pRT[:], rbf[:], ident[:]), reads=[it["krbf"]], writes=[kpRT])
                    p.act(lambda e, pRT=pRT, h=it["h"], j=it["j"]: e.activation(
                        mixT[:, 8 + h, j * 128:(j + 1) * 128], pRT[:], AF.Copy),
                        writes=[("mixT", 8 + it["h"], it["j"]), kpRT])
            if debug:
                for c in range(16):
                    p.dma("sp", mixT_d[c], mixT[:, c, :], reads=[("mixT", c, j) for j in range(NTO)], writes=[("mixTd", c)])
            run_prog(nc, p, G)

        if stages < 4:
            return nc

        with contextlib.ExitStack() as st:
            def sb(name, shape, dt):
                return st.enter_context(nc.sbuf_tensor("s4_" + name, shape, dt))

            def ps(name, shape, dt):
                return st.enter_context(nc.psum_tensor("p4_" + name, shape, dt))

            Wo = [sb("Wo%d" % g, [128, 16, 512], BF16) for g in range(4)]
            x1t_r = Ring("x1t", [sb("x1t%d" % i, [128, D], F32) for i in range(2)])
            xt_r = Ring("xt", [sb("xt%d" % i, [128, D], F32) for i in range(2)])
            junk = sb("junk", [128, 512], BF16)
            wbc = sb("wbc", [128, D], F32)
            ss4 = sb("ss4", [128, NTO, 4], F32)
            rs = sb("rs", [128, NTO], F32)
            pM = [ps("pM%d" % i, [128, 512], F32) for i in range(8)]
            p = Prog()
            p.dma("sp", wbc[:], nw[1], writes=["wbc"])
            for g in range(4):
                p.dma("pool", Wo[g][:], w_out_v[:, :, g * 512:(g + 1) * 512], writes=[("Wo", g)])
            for j in range(NTO):
                xt, kxt = xt_r.next()
                x1t, kx1 = x1t_r.next()
                p.dma("sp", xt[:], xin[NTP + j], writes=[kxt])
                banks = [(pM[(j % 2) * 4 + g], ("pM", (j % 2) * 4 + g)) for g in range(4)]
                for g in range(4):
                    pb, kpb = banks[g]
                    for kc in range(16):
                        p.pe(lambda e, pb=pb, g=g, kc=kc, j=j: e.matmul(
                            pb[:], mixT[:, kc, j * 128:(j + 1) * 128], Wo[g][:, kc, :], start=(kc == 0), stop=(kc == 15)),
                            reads=[("Wo", g)], writes=[kpb])
                for g in range(4):
                    pb, kpb = banks[g]
                    p.act(lambda e, pb=pb, j=j, g=g: e.activation(junk[:], pb[:], AF.Square, accum_out=ss4[:, j, g:g + 1]),
                          writes=["junk", ("ss4", j, g), kpb])
                p.dve(lambda e, j=j: e.tensor_reduce(rs[:, j:j + 1], ss4[:, j, :], AX.X, ALU.add),
                      reads=[("ss4", j, g) for g in range(4)], writes=[("rs", j)])
                p.dve(lambda e, j=j: e.tensor_scalar(rs[:, j:j + 1], rs[:, j:j + 1], 1.0 / D, 1e-6, ALU.mult, ALU.add),
                      reads=[("rs", j)], writes=[("rs", j)])
                p.act(lambda e, j=j: e.activation(rs[:, j:j + 1], rs[:, j:j + 1], AF.Sqrt), reads=[("rs", j)], writes=[("rs", j)])
                p.dve(lambda e, j=j: e.reciprocal(rs[:, j:j + 1], rs[:, j:j + 1]), reads=[("rs", j)], writes=[("rs", j)])
                for g in range(4):
                    pb, kpb = banks[g]
                    gs = slice(g * 512, (g + 1) * 512)
                    p.dve(lambda e, pb=pb, x1t=x1t, gs=gs, j=j: e.scalar_tensor_tensor(
                        x1t[:, gs], pb[:], rs[:, j:j + 1], wbc[:, gs], ALU.mult, ALU.mult),
                        reads=[("rs", j), "wbc"], writes=[(kx1, g), kpb])
                p.pool(lambda e, x1t=x1t, xt=xt: e.tensor_tensor(x1t[:], x1t[:], xt[:], ALU.add),
                       reads=[kxt] + [(kx1, g) for g in range(4)], writes=[(kx1, "f")])
                p.dma("sp", x1_d[j], x1t[:], reads=[(kx1, "f")], writes=[("x1d", j)] + [(kx1, g) for g in range(4)])
            run_prog(nc, p, G)

        mst.close()
        if stages < 5:
            return nc

        with contextlib.ExitStack() as st:
            def sb(name, shape, dt):
                return st.enter_context(nc.sbuf_tensor("s5_" + name, shape, dt))

            def ps(name, shape, dt):
                return st.enter_context(nc.psum_tensor("p5_" + name, shape, dt))

            wbc2 = sb("wbc2", [128, D], F32)
            wbc3 = sb("wbc3", [128, D], F32)
            identf = sb("identf", [128, 128], F32)
            x1_r = Ring("x1t", [sb("x1t%d" % i, [128, D], F32) for i in range(2)])
            junk = sb("junk", [128, D], BF16)
            xn = sb("xn", [128, D], F32)
            uT = sb("uT", [128, 16, 512], BF16)
            uTh = sb("uTh", [128, 16, 2], BF16)
            hT = sb("hT", [128, NCT, 512], BF16)
            Wg_r = Ring("Wg", [sb("Wg%d" % i, [128, 16, 256], BF16) for i in range(3)])
            Wv_r = Ring("Wv", [sb("Wv%d" % i, [128, 16, 256], BF16) for i in range(3)])
            Wd_r = Ring("Wd", [sb("Wd%d" % i, [128, 4, 512], BF16) for i in range(3)])
            ug_r = Ring("ug", [sb("ug%d" % i, [128, 514], F32) for i in range(2)])
            tA_r = Ring("tA", [sb("tA%d" % i, [128, 512], F32) for i in range(2)])
            carry = sb("carry", [128, NCT, 2], F32)
            fo = sb("fo", [128, 4, D], F32)
            cw = sb("cw", [128, NCT, 3], F32)
            cbt = sb("cbt", [128, NCT], F32)
            ss = sb("ss", [128, 64], F32)
            rs = sb("rs", [128, 64], F32)
            pD = [ps("pD%d" % i, [128, 512], F32) for i in range(4)]
            pG_r = Ring("pG", [ps("pG%d" % i, [128, 512], F32) for i in range(2)])
            pV_r = Ring("pV", [ps("pV%d" % i, [128, 512], F32) for i in range(2)])

            p = Prog()
            p.dma("sp", wbc2[:], nw[2], writes=["wbc2"])
            p.dma("sp", wbc3[:], nw[3], writes=["wbc3"])
            p.dma("sp", identf[:], ident_d, writes=["identf"])
            p.dma("sp", cw[:], convw, writes=["cw"])
            p.dma("sp", cbt[:], convb, writes=["cbt"])
            sidx = [0]

            def rstd_ops(src_ap, src_keys):
                k = sidx[0]
                sidx[0] += 1
                p.act(lambda e, k=k: e.activation(junk[:], src_ap, AF.Square, accum_out=ss[:, k:k + 1]),
                      reads=src_keys, writes=["junk", ("ss", k)])
                p.dve(lambda e, k=k: e.tensor_scalar(rs[:, k:k + 1], ss[:, k:k + 1], 1.0 / D, 1e-6, ALU.mult, ALU.add),
                      reads=[("ss", k)], writes=[("rs", k)])
                p.act(lambda e, k=k: e.activation(rs[:, k:k + 1], rs[:, k:k + 1], AF.Sqrt), reads=[("rs", k)], writes=[("rs", k)])
                p.dve(lambda e, k=k: e.reciprocal(rs[:, k:k + 1], rs[:, k:k + 1]), reads=[("rs", k)], writes=[("rs", k)])
                return k, ("rs", k)

            def u_prep(j, halo, tt):
                x1t, kx1 = x1_r.next()
                p.dma("sp", x1t[:], x1_d[j], writes=[kx1])
                k, krs = rstd_ops(x1t[:], [kx1])
                p.dve(lambda e, x1t=x1t, k=k: e.scalar_tensor_tensor(xn[:], x1t[:], rs[:, k:k + 1], wbc2[:], ALU.mult, ALU.mult),
                      reads=[kx1, krs, "wbc2"], writes=["xn"])
                for kc in range(16):
                    p.pe(lambda e, kc=kc: e.transpose(pD[kc // 4][:, (kc % 4) * 128:(kc % 4 + 1) * 128],
                                                      xn[:, kc * 128:(kc + 1) * 128], identf[:]),
                         reads=["xn", "identf"], writes=[("pD", kc // 4)])
                for b in range(4):
                    src = pD[b][:].rearrange("p (k c) -> p k c", k=4)
                    if halo:
                        dst = uTh[:, b * 4:(b + 1) * 4, :]
                        srcv = src[:, :, 126:128]
                        wk = ("uTh", b)
                    else:
                        dst = uT[:, b * 4:(b + 1) * 4, tt * 128:(tt + 1) * 128]
                        srcv = src
                        wk = ("uT", tt, b)
                    if b % 2 == 0:
                        p.act(lambda e, dst=dst, srcv=srcv: e.activation(dst, srcv, AF.Copy), writes=[wk, ("pD", b)])
                    else:
                        p.dve(lambda e, dst=dst, srcv=srcv: e.tensor_copy(dst, srcv), writes=[wk, ("pD", b)])

            up_src, dn_src = [], []
            for gi in range(4):
                for cg in range(NCT // 2):
                    up_src.append((w_up_v[:, :, cg * 256:(cg + 1) * 256], w_up_v[:, :, DFF + cg * 256:DFF + (cg + 1) * 256]))
                for ng in range(4):
                    for pc in range(NCT // 4):
                        dn_src.append(w_down_v[:, pc * 4:(pc + 1) * 4, ng * 512:(ng + 1) * 512])
            Gpf = Prefetch(p, Wg_r, [a for a, _ in up_src])
            Vpf = Prefetch(p, Wv_r, [b for _, b in up_src])
            Dpf = Prefetch(p, Wd_r, dn_src)
            for gi in range(4):
                if gi == 0:
                    u_prep(0, True, 0)
                for tt in range(4):
                    u_prep(1 + 4 * gi + tt, False, tt)
                uT_keys = [("uT", tt, b) for tt in range(4) for b in range(4)]
                uTh_keys = [("uTh", b) for b in range(4)]
                Dpf.ensure(gi * 44 + Dpf.R - 1)
                for cg in range(NCT // 2):
                    ku = gi * (NCT // 2) + cg
                    Gpf.ensure(ku + Gpf.R - 1)
                    Vpf.ensure(ku + Vpf.R - 1)
                    Wg, kWg = Gpf.get(ku)
                    Wv, kWv = Vpf.get(ku)
                    for ci in range(2):
                        ct = cg * 2 + ci
                        cs = slice(ci * 128, (ci + 1) * 128)
                        ug, kug = ug_r.next()
                        tA, ktA = tA_r.next()
                        pG, kpG = pG_r.next()
                        pV, kpV = pV_r.next()
                        if gi == 0:
                            for kc in range(16):
                                p.pe(lambda e, pV=pV, Wg=Wg, kc=kc, cs=cs: e.matmul(
                                    pV[:, 0:2], Wg[:, kc, cs], uTh[:, kc, :], start=(kc == 0), stop=(kc == 15)),
                                    reads=[kWg] + uTh_keys, writes=[kpV])
                            p.act(lambda e, ug=ug, pV=pV: e.activation(ug[:, 0:2], pV[:, 0:2], AF.Copy), writes=[(kug, "h"), kpV])
                        else:
                            p.act(lambda e, ug=ug, ct=ct: e.activation(ug[:, 0:2], carry[:, ct, :], AF.Copy),
                                  reads=[("carry", ct)], writes=[(kug, "h")])
                        for kc in range(16):
                            p.pe(lambda e, pG=pG, Wg=Wg, kc=kc, cs=cs: e.matmul(
                                pG[:], Wg[:, kc, cs], uT[:, kc, :], start=(kc == 0), stop=(kc == 15)),
                                reads=[kWg] + uT_keys, writes=[kpG])
                        p.act(lambda e, ug=ug, pG=pG: e.activation(ug[:, 2:514], pG[:], AF.Copy), writes=[(kug, "m"), kpG])
                        if gi < 3:
                            p.act(lambda e, ug=ug, ct=ct: e.activation(carry[:, ct, :], ug[:, 512:514], AF.Copy),
                                  reads=[(kug, "m")], writes=[("carry", ct)])
                        for kc in range(16):
                            p.pe(lambda e, pV=pV, Wv=Wv, kc=kc, cs=cs: e.matmul(
                                pV[:], Wv[:, kc, cs], uT[:, kc, :], start=(kc == 0), stop=(kc == 15)),
                                reads=[kWv] + uT_keys, writes=[kpV])
                        ugk = [(kug, "h"), (kug, "m")]
                        p.dve(lambda e, tA=tA, ug=ug, ct=ct: e.tensor_scalar(tA[:], ug[:, 0:512], cw[:, ct, 0:1], None, ALU.mult),
                              reads=ugk + ["cw"], writes=[ktA])
                        p.dve(lambda e, tA=tA, ug=ug, ct=ct: e.scalar_tensor_tensor(
                            tA[:], ug[:, 1:513], cw[:, ct, 1:2], tA[:], ALU.mult, ALU.add),
                            reads=ugk + ["cw", ktA], writes=[ktA])
                        p.dve(lambda e, tA=tA, ug=ug, ct=ct: e.scalar_tensor_tensor(
                            tA[:], ug[:, 2:514], cw[:, ct, 2:3], tA[:], ALU.mult, ALU.add),
                            reads=ugk + ["cw", ktA], writes=[ktA])
                        p.act(lambda e, tA=tA, ct=ct: e.activation(tA[:], tA[:], AF.Silu, bias=cbt[:, ct:ct + 1]),
                              reads=[ktA, "cbt"], writes=[ktA])
                        p.dve(lambda e, tA=tA, pV=pV, ct=ct: e.tensor_tensor(hT[:, ct, :], tA[:], pV[:], ALU.mult),
                              reads=[ktA], writes=[("hT", ct), kpV])
                hT_keys = [("hT", ct) for ct in range(NCT)]
                if gi < 3:
                    Gpf.ensure((gi + 1) * (NCT // 2) + Gpf.R - 2)
                    Vpf.ensure((gi + 1) * (NCT // 2) + Vpf.R - 2)
                for ng in range(4):
                    for pc in range(NCT // 4):
                        Wd, kWd = Dpf.get(gi * 44 + ng * 11 + pc)
                        for cc in range(4):
                            ct = pc * 4 + cc
                            for tt in range(4):
                                p.pe(lambda e, tt=tt, ct=ct, cc=cc, Wd=Wd: e.matmul(
                                    pD[tt][:], hT[:, ct, tt * 128:(tt + 1) * 128], Wd[:, cc, :],
                                    start=(ct == 0), stop=(ct == NCT - 1)),
                                    reads=[("hT", ct), kWd], writes=[("pD", tt)])
                    for tt in range(4):
                        p.act(lambda e, tt=tt, ng=ng: e.activation(fo[:, tt, ng * 512:(ng + 1) * 512], pD[tt][:], AF.Copy),
                              writes=[("fo", tt, ng), ("pD", tt)])
                for tt in range(4):
                    j = 1 + 4 * gi + tt
                    fok = [("fo", tt, ng) for ng in range(4)]
                    x1t, kx1 = x1_r.next()
                    p.dma("sp", x1t[:], x1_d[j], writes=[kx1])
                    k, krs = rstd_ops(fo[:, tt, :], fok)
                    p.dve(lambda e, tt=tt, k=k: e.scalar_tensor_tensor(
                        fo[:, tt, :], fo[:, tt, :], rs[:, k:k + 1], wbc3[:], ALU.mult, ALU.mult),
                        reads=[krs, "wbc3"], writes=fok)
                    p.dve(lambda e, tt=tt, x1t=x1t: e.tensor_tensor(fo[:, tt, :], fo[:, tt, :], x1t[:], ALU.add),
                          reads=[kx1], writes=fok)
                    p.dma("sp", out[j - 1], fo[:, tt, :], reads=fok, writes=[("out", j)])
            run_prog(nc, p, G)
    return nc


def _tables(half):
    start = half * 2048 - 2048
    pos = (start + np.arange(4096)).astype(np.float64)
    pos = np.maximum(pos, 0.0).astype(np.float32)
    inv = (1.0 / (np.float32(10000.0) ** (np.arange(0, 128, 2, dtype=np.float32) / np.float32(128)))).astype(np.float32)
    ang = pos[:, None] * inv[None, :]
    cos = np.cos(ang).astype(np.float32).reshape(32, 128, 64).transpose(1, 0, 2)
    sin = np.sin(ang).astype(np.float32).reshape(32, 128, 64).transpose(1, 0, 2)
    hidx = np.arange(8, dtype=np.float32)
    log_g = np.log(1.0 - 2.0 ** (-5.0 - hidx)).astype(np.float64)
    pp = (np.arange(256, dtype=np.float64) + 1.0).reshape(2, 128).T
    qfac = np.exp(log_g[None, None, :] * pp[:, :, None]).astype(np.float32)
    kfac = (np.exp(-log_g[None, None, :] * pp[:, :, None]) * (128.0 ** -0.5)).astype(np.float32)
    gch = np.broadcast_to(np.exp(log_g * 256.0).astype(np.float32)[None, :], (128, 8)).copy()
    kk = np.arange(128)
    tri01 = (kk[:, None] <= kk[None, :]).astype(np.float32)
    trib = np.where(kk[None, :] <= kk[:, None], 0.0, -BIG).astype(np.float32)
    gb = np.full((128, NTO, 16), NEG, dtype=np.float32)
    for j in range(NTO):
        qb = (NTP + j) // 2
        for n in range(qb):
            if half == 1 or n >= 8:
                gb[:, j, n] = 0.0
    return dict(cos_t=np.ascontiguousarray(cos), sin_t=np.ascontiguousarray(sin), qfac=qfac, kfac=kfac,
                gch=gch, tri01=tri01, trib=trib, gbias=gb, ident=np.eye(128, dtype=np.float32))


def make_in_maps(x, norm_mix_pre, w_in, ret_gn_w, ret_gn_b, w_out, norm_mix_post, norm_ffn_pre,
                 w_up, conv_w, conv_b, w_down, norm_ffn_post):
    f = np.float32
    x = np.asarray(x, f)
    shared = dict(
        w_in=np.ascontiguousarray(np.asarray(w_in, f)[0]),
        w_out=np.ascontiguousarray(np.asarray(w_out, f)[0]),
        w_up=np.ascontiguousarray(np.asarray(w_up, f)[0]),
        w_down=np.ascontiguousarray(np.asarray(w_down, f)[0]),
        nw0=np.ascontiguousarray(np.broadcast_to(np.asarray(norm_mix_pre, f)[0][None, :], (128, D))),
        nw1=np.ascontiguousarray(np.broadcast_to(np.asarray(norm_mix_post, f)[0][None, :], (128, D))),
        nw2=np.ascontiguousarray(np.broadcast_to(np.asarray(norm_ffn_pre, f)[0][None, :], (128, D))),
        nw3=np.ascontiguousarray(np.broadcast_to(np.asarray(norm_ffn_post, f)[0][None, :], (128, D))),
        gnw=np.ascontiguousarray(np.broadcast_to(np.asarray(ret_gn_w, f)[0][None, :], (128, 1024))),
        gnb=np.ascontiguousarray(np.broadcast_to(np.asarray(ret_gn_b, f)[0][None, :], (128, 1024))),
        convw=np.ascontiguousarray(np.asarray(conv_w, f)[0].reshape(3, NCT, 128).transpose(2, 1, 0)),
        convb=np.ascontiguousarray(np.asarray(conv_b, f)[0].reshape(NCT, 128).T),
    )
    tabs = [_tables(0), _tables(1)]
    maps = []
    for c in range(8):
        b, half = c // 2, c % 2
        xi = np.zeros((4096, D), f)
        if half == 0:
            xi[2048:] = x[b, :2048]
        else:
            xi[:] = x[b]
        m = dict(shared)
        m.update(tabs[half])
        m["xin"] = xi.reshape(32, 128, D)
        maps.append(m)
    return maps


_NC_CACHE = {}


def kernel(**inputs):
    if "nc" not in _NC_CACHE:
        _NC_CACHE["nc"] = build_nc()
    nc = _NC_CACHE["nc"]
    maps = make_in_maps(**inputs)
    res = run_bass_kernel_spmd(nc, maps, core_ids=list(range(8)))
    outp = np.empty((4, 4096, D), np.float32)
    for c in range(8):
        b, half = c // 2, c % 2
        outp[b, half * 2048:(half + 1) * 2048] = np.asarray(res.results[c]["out"]).reshape(2048, D)
    return outp
```

```python
import contextlib
import numpy as np
import ml_dtypes
import concourse.bass as bass
import concourse.mybir as mybir
from concourse.bass_utils import run_bass_kernel_spmd

F32 = mybir.dt.float32
BF16 = mybir.dt.bfloat16
ALU = mybir.AluOpType
AF = mybir.ActivationFunctionType
AX = mybir.AxisListType

N_DMA_SEMS = 40
D = 2048
NTP = 15
NTO = 17
DFF = 5632
NCT = DFF // 128
BIG = 30000.0
NEG = -1.0e30


class Op:
    __slots__ = ("idx", "eng", "fn", "deps", "is_dma", "signal", "count", "dsem", "dtarget")

    def __init__(self, idx, eng, fn, is_dma):
        self.idx = idx
        self.eng = eng
        self.fn = fn
        self.is_dma = is_dma
        self.deps = set()
        self.signal = False
        self.count = 0
        self.dsem = None
        self.dtarget = 0


class Prog:
    COMPUTE = ("pe", "act", "dve", "pool")

    def __init__(self):
        self.ops = []
        self.last_w = {}
        self.readers = {}

    def add(self, eng, fn, reads=(), writes=(), dma=False):
        op = Op(len(self.ops), eng, fn, dma)
        deps = op.deps
        for k in reads:
            w = self.last_w.get(k)
            if w is not None:
                deps.add(w)
        for k in writes:
            w = self.last_w.get(k)
            if w is not None:
                deps.add(w)
            r = self.readers.get(k)
            if r:
                deps.update(r)
        for k in reads:
            self.readers.setdefault(k, []).append(op.idx)
        for k in writes:
            self.last_w[k] = op.idx
            self.readers[k] = []
        deps.discard(op.idx)
        self.ops.append(op)
        return op

    def pe(self, fn, reads=(), writes=()):
        return self.add("pe", fn, reads, writes)

    def act(self, fn, reads=(), writes=()):
        return self.add("act", fn, reads, writes)

    def dve(self, fn, reads=(), writes=()):
        return self.add("dve", fn, reads, writes)

    def pool(self, fn, reads=(), writes=()):
        return self.add("pool", fn, reads, writes)

    def dma(self, q, out, in_, reads=(), writes=()):
        return self.add(q, lambda e: e.dma_start(out=out, in_=in_), reads, writes, dma=True)

    @staticmethod
    def _needs_wait(x, y):
        if y.is_dma or x.is_dma:
            return True
        if x.eng == y.eng and x.eng == "pe":
            return False
        return True

    def plan(self, G):
        ops = self.ops
        self.base_cnt = dict(G["cnt"])
        self.base_tgt = list(G["tgt"])
        last_on_sem = [None] * N_DMA_SEMS
        tgt = G["tgt"]
        k = G["k"]
        NSP = 24
        for op in ops:
            if op.is_dma:
                if op.eng == "sp":
                    s = k["sp"] % NSP
                    k["sp"] += 1
                else:
                    s = NSP + k["pool"] % (N_DMA_SEMS - NSP)
                    k["pool"] += 1
                if last_on_sem[s] is not None:
                    op.deps.add(last_on_sem[s])
                tgt[s] += 16
                op.dsem = s
                op.dtarget = tgt[s]
                last_on_sem[s] = op.idx
        G["k"] = k
        for op in ops:
            for d in op.deps:
                y = ops[d]
                if not y.is_dma and self._needs_wait(op, y):
                    y.signal = True
        last = {}
        for op in ops:
            if not op.is_dma:
                last[op.eng] = op
        for op in last.values():
            op.signal = True
        cnt = G["cnt"]
        for op in ops:
            if not op.is_dma and op.signal:
                cnt[op.eng] += 1
                op.count = cnt[op.eng]

    def emit(self, ename, eng, G):
        ops = self.ops
        csems, dsems = G["c"], G["d"]
        waited = {}
        for e_ in self.COMPUTE:
            if self.base_cnt[e_] > 0:
                eng.wait_ge(csems[e_], self.base_cnt[e_])
                waited[("c", e_)] = self.base_cnt[e_]
        for s_ in range(N_DMA_SEMS):
            if self.base_tgt[s_] > 0:
                eng.wait_ge(dsems[s_], self.base_tgt[s_])
                waited[("d", s_)] = self.base_tgt[s_]
        mydma = {}
        for op in ops:
            if op.eng != ename:
                continue
            for d in sorted(op.deps):
                y = ops[d]
                if not self._needs_wait(op, y):
                    continue
                if y.is_dma:
                    key, val, sem = ("d", y.dsem), y.dtarget, dsems[y.dsem]
                else:
                    key, val, sem = ("c", y.eng), y.count, csems[y.eng]
                if waited.get(key, 0) >= val:
                    continue
                waited[key] = val
                eng.wait_ge(sem, val)
            ins = op.fn(eng)
            if op.is_dma:
                ins.then_inc(dsems[op.dsem], 16)
                mydma[op.dsem] = op.dtarget
            elif op.signal:
                ins.then_inc(csems[op.eng], 1)
        for s, v in sorted(mydma.items()):
            if waited.get(("d", s), 0) < v:
                eng.wait_ge(dsems[s], v)


def make_sems(nc, st):
    return {"c": {e: st.enter_context(nc.semaphore("cs_" + e)) for e in Prog.COMPUTE},
            "d": [st.enter_context(nc.semaphore("ds_%d" % i)) for i in range(N_DMA_SEMS)],
            "cnt": {e: 0 for e in Prog.COMPUTE}, "tgt": [0] * N_DMA_SEMS, "k": {"sp": 0, "pool": 0}}


def run_prog(nc, prog, G):
    prog.plan(G)
    with nc.Block() as block:
        @block.tensor
        def _(e):
            prog.emit("pe", e, G)

        @block.scalar
        def _(e):
            prog.emit("act", e, G)

        @block.vector
        def _(e):
            prog.emit("dve", e, G)

        @block.gpsimd
        def _(e):
            prog.emit("pool", e, G)

        @block.sync
        def _(e):
            prog.emit("sp", e, G)


class Ring:
    def __init__(self, name, bufs):
        self.name = name
        self.bufs = bufs
        self.i = -1

    def next(self):
        self.i += 1
        s = self.i % len(self.bufs)
        return self.bufs[s], (self.name, s)


class Prefetch:
    def __init__(self, p, ring, srcs, queue="pool"):
        self.p, self.ring, self.srcs, self.queue = p, ring, srcs, queue
        self.slots = []
        self.R = len(ring.bufs)

    def ensure(self, k):
        k = min(k, len(self.srcs) - 1)
        while len(self.slots) <= k:
            buf, key = self.ring.next()
            self.p.dma(self.queue, buf[:], self.srcs[len(self.slots)], writes=[key])
            self.slots.append((buf, key))

    def get(self, k):
        self.ensure(k + self.R - 1)
        return self.slots[k]


GROUP_KIND = ["qa", "qa", "ka", "ka", "va", "va", "qr", "qr", "kr", "kr", "vr", "vr", "gr", "gr"]


def build_nc(debug=False, stages=5):
    nc = bass.Bass("TRN2", target_bir_lowering=False)
    ext = "ExternalInput"

    def din(name, shape, dt=F32):
        return nc.dram_tensor(name, shape, dt, kind=ext).ap()

    xin = din("xin", [32, 128, D])
    w_in = din("w_in", [D, 7168])
    w_out = din("w_out", [D, D])
    w_up = din("w_up", [D, 2 * DFF])
    w_down = din("w_down", [DFF, D])
    nw = [din("nw%d" % i, [128, D]) for i in range(4)]
    gnw = din("gnw", [128, 1024])
    gnb = din("gnb", [128, 1024])
    convw = din("convw", [128, NCT, 3])
    convb = din("convb", [128, NCT])
    cos_d = din("cos_t", [128, 32, 64])
    sin_d = din("sin_t", [128, 32, 64])
    qfac_d = din("qfac", [128, 2, 8])
    kfac_d = din("kfac", [128, 2, 8])
    gch_d = din("gch", [128, 8])
    tri01_d = din("tri01", [128, 128])
    trib_d = din("trib", [128, 128])
    gbias_d = din("gbias", [128, NTO, 16])
    ident_d = din("ident", [128, 128])
    out = nc.dram_tensor("out", [16, 128, D], F32, kind="ExternalOutput").ap()

    sk = "ExternalOutput" if debug else "Internal"

    def dsc(name, shape, dt):
        return nc.dram_tensor(name, shape, dt, kind=sk).ap()

    KaT_d = dsc("KaT_d", [8, 128, 4096], BF16)
    KrT_d = dsc("KrT_d", [8, 128, 4096], BF16)
    QaT_d = dsc("QaT_d", [8, 128, NTO * 128], BF16)
    QrT_d = dsc("QrT_d", [8, 128, NTO * 128], BF16)
    Va_d = dsc("Va_d", [32, 128, 1024], BF16)
    Vr_d = dsc("Vr_d", [32, 128, 1024], BF16)
    Kr_d = dsc("Kr_d", [32, 128, 1024], BF16)
    Sg_d = dsc("Sg_d", [NTO, 128, 1024], F32)
    mixT_d = dsc("mixT_d", [16, 128, NTO * 128], BF16)
    mo_d = dsc("mo_d", [NTO, 128, D], F32)
    x1_d = dsc("x1_d", [NTO, 128, D], F32)
    dbgB = dsc("dbgB", [128, NTO * 16 + NTO + NTO * 18], F32)
    dbgQ = dsc("dbgQ", [128, NTO * 128], BF16)
    dbgK = dsc("dbgK", [128, 2], BF16)

    w_in_v = w_in.rearrange("(kc p) n -> p kc n", p=128)
    w_out_v = w_out.rearrange("(kc p) n -> p kc n", p=128)
    w_up_v = w_up.rearrange("(kc p) n -> p kc n", p=128)
    w_down_v = w_down.rearrange("(ct p) n -> p ct n", p=128)

    with contextlib.ExitStack() as gst:
        def gsb(name, shape, dt):
            return gst.enter_context(nc.sbuf_tensor("g_" + name, shape, dt))

        ident = gsb("ident", [128, 128], BF16)
        G = make_sems(nc, gst)

        p = Prog()
        p.dma("pool", ident[:], ident_d, writes=["ident"])
        run_prog(nc, p, G)

        for seg in ("P", "O"):
            with contextlib.ExitStack() as st:
                def sb(name, shape, dt):
                    return st.enter_context(nc.sbuf_tensor("s1_" + name + seg, shape, dt))

                def ps(name, shape, dt):
                    return st.enter_context(nc.psum_tensor("p1_" + name + seg, shape, dt))

                NT = NTP if seg == "P" else NTO
                T0 = 0 if seg == "P" else NTP
                groups = [2, 3, 4, 5, 8, 9, 10, 11] if seg == "P" else list(range(14))

                hnT = sb("hnT", [128, 16, NTO * 128], BF16)
                xt_r = Ring("xt", [sb("xt%d" % i, [128, D], F32) for i in range(2)])
                xn_r = Ring("xn", [sb("xn%d" % i, [128, D], BF16) for i in range(2)])
                junk = sb("junk", [128, D], BF16)
                wbc = sb("wbc", [128, D], F32)
                ss = sb("ss", [128, NTO], F32)
                rs = sb("rs", [128, NTO], F32)
                W_r = Ring("W", [sb("W%d" % i, [128, 16, 512], BF16) for i in range(2)])
                cos_t = sb("cos", [128, 32, 64], F32)
                sin_t = sb("sin", [128, 32, 64], F32)
                qfac = sb("qfac", [128, 2, 8], F32)
                kfac = sb("kfac", [128, 2, 8], F32)
                rA = Ring("rA", [sb("rA%d" % i, [128, 4, 64], F32) for i in range(2)])
                rB = Ring("rB", [sb("rB%d" % i, [128, 4, 64], F32) for i in range(2)])
                rC = Ring("rC", [sb("rC%d" % i, [128, 4, 64], F32) for i in range(2)])
                rD = Ring("rD", [sb("rD%d" % i, [128, 4, 64], F32) for i in range(2)])
                rot_r = Ring("rot", [sb("rot%d" % i, [128, 4, 128], F32) for i in range(2)])
                tok_r = Ring("tok", [sb("tok%d" % i, [128, 4, 128], BF16) for i in range(6)])
                ft_r = Ring("ft", [sb("ft%d" % i, [128, 4, 128], BF16) for i in range(3)])
                sg_r = Ring("sg", [sb("sg%d" % i, [128, 512], F32) for i in range(2)])
                pA = ps("pA", [128, D], BF16)
                pM_r = Ring("pM", [ps("pM%d" % i, [128, 512], F32) for i in range(4)])
                pT_r = Ring("pT", [ps("pT%d" % i, [128, 4, 128], BF16) for i in range(2)])

                p = Prog()
                p.dma("sp", wbc[:], nw[0], writes=["wbc"])
                p.dma("sp", cos_t[:], cos_d, writes=["cos"])
                p.dma("sp", sin_t[:], sin_d, writes=["sin"])
                p.dma("sp", qfac[:], qfac_d, writes=["qfac"])
                p.dma("sp", kfac[:], kfac_d, writes=["kfac"])

                for j in range(NT):
                    t = T0 + j
                    xt, kxt = xt_r.next()
                    xn, kxn = xn_r.next()
                    p.dma("sp", xt[:], xin[t], writes=[kxt])
                    p.act(lambda e, xt=xt, j=j: e.activation(junk[:], xt[:], AF.Square, accum_out=ss[:, j:j + 1]),
                          reads=[kxt], writes=["junk", ("ss", j)])
                    p.dve(lambda e, j=j: e.tensor_scalar(rs[:, j:j + 1], ss[:, j:j + 1], 1.0 / D, 1e-6, ALU.mult, ALU.add),
                          reads=[("ss", j)], writes=[("rs", j)])
                    p.act(lambda e, j=j: e.activation(rs[:, j:j + 1], rs[:, j:j + 1], AF.Sqrt),
                          reads=[("rs", j)], writes=[("rs", j)])
                    p.dve(lambda e, j=j: e.reciprocal(rs[:, j:j + 1], rs[:, j:j + 1]),
                          reads=[("rs", j)], writes=[("rs", j)])
                    p.dve(lambda e, xt=xt, xn=xn, j=j: e.scalar_tensor_tensor(
                        xn[:], xt[:], rs[:, j:j + 1], wbc[:], ALU.mult, ALU.mult),
                        reads=[kxt, ("rs", j), "wbc"], writes=[kxn])
                    for kc in range(16):
                        p.pe(lambda e, xn=xn, kc=kc: e.transpose(pA[:, kc * 128:(kc + 1) * 128],
                                                                   xn[:, kc * 128:(kc + 1) * 128], ident[:]),
                             reads=[kxn], writes=["pA"])
                    p.act(lambda e, j=j: e.activation(hnT[:, 0:8, j * 128:(j + 1) * 128],
                                                      pA[:, 0:1024].rearrange("p (k c) -> p k c", k=8), AF.Copy),
                          reads=["pA"], writes=[("hnT", j, 0)])
                    p.dve(lambda e, j=j: e.tensor_copy(hnT[:, 8:16, j * 128:(j + 1) * 128],
                                                       pA[:, 1024:2048].rearrange("p (k c) -> p k c", k=8)),
                          reads=["pA"], writes=[("hnT", j, 1)])

                deferred = []
                Wpf = Prefetch(p, W_r, [w_in_v[:, :, g * 512:(g + 1) * 512] for g in groups])
                for gi_, g in enumerate(groups):
                    kind = GROUP_KIND[g]
                    g4 = g % 2
                    Wb, kW = Wpf.get(gi_)
                    for j in range(NT):
                        t = T0 + j
                        pM, kpM = pM_r.next()
                        for kc in range(16):
                            p.pe(lambda e, pM=pM, Wb=Wb, kc=kc, j=j: e.matmul(
                                pM[:], hnT[:, kc, j * 128:(j + 1) * 128], Wb[:, kc, :],
                                start=(kc == 0), stop=(kc == 15)),
                                reads=[("hnT", j, 0), ("hnT", j, 1), kW], writes=[kpM])
                        keep = []
                        for ent in deferred:
                            ent[0] -= 1
                            if ent[0] <= 0:
                                ent[1]()
                            else:
                                keep.append(ent)
                        deferred[:] = keep
                        if kind in ("va", "vr"):
                            tok, ktok = tok_r.next()
                            p.act(lambda e, tok=tok, pM=pM: e.activation(
                                tok[:].rearrange("p h d -> p (h d)"), pM[:], AF.Copy),
                                reads=[kpM], writes=[ktok])
                            dst = Va_d if kind == "va" else Vr_d
                            p.dma("sp", dst[t][:, g4 * 512:(g4 + 1) * 512], tok[:].rearrange("p h d -> p (h d)"),
                                  reads=[ktok], writes=[(kind, t, g4)])
                            continue
                        if kind == "gr":
                            sg, ksg = sg_r.next()
                            p.act(lambda e, sg=sg, pM=pM: e.activation(sg[:], pM[:], AF.Silu),
                                  reads=[kpM], writes=[ksg])
                            p.dma("sp", Sg_d[j][:, g4 * 512:(g4 + 1) * 512], sg[:], reads=[ksg], writes=[("sgd", j, g4)])
                            continue
                        src = pM[:].rearrange("p (h d) -> p h d", h=4)
                        t1 = src[:, :, 0:64]
                        t2 = src[:, :, 64:128]
                        cb = cos_t[:, t:t + 1, :].to_broadcast([128, 4, 64])
                        sbb = sin_t[:, t:t + 1, :].to_broadcast([128, 4, 64])
                        A, kA = rA.next()
                        B, kB = rB.next()
                        C, kC = rC.next()
                        Dd, kD = rD.next()
                        tok, ktok = tok_r.next()
                        is_ret = kind in ("qr", "kr")
                        if is_ret:
                            rot, krot = rot_r.next()
                            dst, kdst = rot, krot
                        else:
                            dst, kdst = tok, ktok
                        p.dve(lambda e, A=A, t1=t1, cb=cb: e.tensor_tensor(A[:], t1, cb, ALU.mult),
                              reads=[kpM, "cos"], writes=[kA])
                        p.dve(lambda e, B=B, t2=t2, sbb=sbb: e.tensor_tensor(B[:], t2, sbb, ALU.mult),
                              reads=[kpM, "sin"], writes=[kB])
                        p.dve(lambda e, C=C, t1=t1, sbb=sbb: e.tensor_tensor(C[:], t1, sbb, ALU.mult),
                              reads=[kpM, "sin"], writes=[kC])
                        p.dve(lambda e, Dd=Dd, t2=t2, cb=cb: e.tensor_tensor(Dd[:], t2, cb, ALU.mult),
                              reads=[kpM, "cos"], writes=[kD])
                        p.pool(lambda e, dst=dst, A=A, B=B: e.tensor_tensor(dst[:, :, 0:64], A[:], B[:], ALU.subtract),
                               reads=[kA, kB], writes=[kdst])
                        p.pool(lambda e, dst=dst, C=C, Dd=Dd: e.tensor_tensor(dst[:, :, 64:128], C[:], Dd[:], ALU.add),
                               reads=[kC, kD], writes=[kdst])
                        if is_ret:
                            fac = qfac if kind == "qr" else kfac
                            par = t % 2
                            for hh in range(4):
                                head = g4 * 4 + hh
                                p.act(lambda e, tok=tok, rot=rot, hh=hh, fac=fac, par=par, head=head: e.activation(
                                    tok[:, hh, :], rot[:, hh, :], AF.Copy, scale=fac[:, par, head:head + 1]),
                                    reads=[krot, "qfac", "kfac"], writes=[ktok])
                            tokreads = [ktok]
                        else:
                            tokreads = [ktok]
                        if kind == "kr":
                            p.dma("sp", Kr_d[t][:, g4 * 512:(g4 + 1) * 512], tok[:].rearrange("p h d -> p (h d)"),
                                  reads=tokreads, writes=[("krd", t, g4)])
                        need_T = kind in ("ka", "kr") or seg == "O"
                        if not need_T:
                            continue

                        def tail(tok=tok, tokreads=tokreads, kind=kind, g4=g4, t=t, j=j):
                            pT, kpT = pT_r.next()
                            for hh in range(4):
                                p.pe(lambda e, pT=pT, tok=tok, hh=hh: e.transpose(pT[:, hh, :], tok[:, hh, :], ident[:]),
                                     reads=tokreads, writes=[kpT])
                            ft, kft = ft_r.next()
                            p.act(lambda e, ft=ft, pT=pT: e.activation(ft[:], pT[:], AF.Copy), reads=[kpT], writes=[kft])
                            if kind in ("ka", "kr"):
                                dd = KaT_d if kind == "ka" else KrT_d
                                dap = dd[g4 * 4:(g4 + 1) * 4].rearrange("h d t -> d h t")[:, :, t * 128:(t + 1) * 128]
                            else:
                                dd = QaT_d if kind == "qa" else QrT_d
                                dap = dd[g4 * 4:(g4 + 1) * 4].rearrange("h d t -> d h t")[:, :, j * 128:(j + 1) * 128]
                            p.dma("sp", dap, ft[:], reads=[kft], writes=[(kind + "T", t, g4)])
                        deferred.append([2, tail])
                for _, fn_ in deferred:
                    fn_()
                run_prog(nc, p, G)

        if stages < 2:
            return nc
        mst = gst.enter_context(contextlib.ExitStack())
        mixT = mst.enter_context(nc.sbuf_tensor("g_mixT", [128, 16, NTO * 128], BF16))
        SCALE = 128.0 ** -0.5

        with contextlib.ExitStack() as st:
            def sb(name, shape, dt):
                return st.enter_context(nc.sbuf_tensor("s2_" + name, shape, dt))

            def ps(name, shape, dt):
                return st.enter_context(nc.psum_tensor("p2_" + name, shape, dt))

            KT_r = Ring("KT", [sb("KT%d" % i, [128, 4096], BF16) for i in range(2)])
            V_r = Ring("V", [sb("V%d" % i, [128, 32, 128], BF16) for i in range(2)])
            QT_r = Ring("QT", [sb("QT%d" % i, [128, NTO * 128], BF16) for i in range(2)])
            QA_r = Ring("QA", [sb("QA%d" % i, [128, NTO * 128], BF16) for i in range(2)])
            gbias = sb("gbias", [128, NTO, 16], F32)
            trib = sb("trib", [128, 256], BF16)
            ksum = sb("ksum", [128, 16], F32)
            kmb_r = Ring("kmb", [sb("kmb%d" % i, [128, 16], BF16) for i in range(2)])
            kab = sb("kab", [128, 2], F32)
            kabb_r = Ring("kabb", [sb("kabb%d" % i, [128, 2], BF16) for i in range(2)])
            gb_r = Ring("gb", [sb("gb%d" % i, [128, 16], F32) for i in range(2)])
            t8_r = Ring("t8", [sb("t8%d" % i, [128, 8], F32) for i in range(2)])
            thr_r = Ring("thr", [sb("thr%d" % i, [128, 1], F32) for i in range(2)])
            s01_r = Ring("s01", [sb("s01%d" % i, [128, 16], F32) for i in range(2)])
            ball_r = Ring("ball", [sb("ball%d" % i, [128, NTO, 16], F32) for i in range(2)])
            negm_r = Ring("negm", [sb("negm%d" % i, [128, NTO], F32) for i in range(2)])
            rsum_r = Ring("rsum", [sb("rsum%d" % i, [128, NTO, 18], F32) for i in range(2)])
            rtot_r = Ring("rtot", [sb("rtot%d" % i, [128, 1], F32) for i in range(3)])
            Pb_r = Ring("Pb", [sb("Pb%d" % i, [128, 256], BF16) for i in range(6)])
            sm_r = Ring("sm", [sb("sm%d" % i, [128, 128], F32) for i in range(4)])
            PT_r = Ring("PT", [sb("PT%d" % i, [128, 2, 128], BF16) for i in range(5)])
            ab_r = Ring("ab", [sb("ab%d" % i, [128, 128], BF16) for i in range(3)])
            bkf = [ps("bkf%d" % i, [128, 512], F32) for i in range(6)]
            bkb = [ps("bkb%d" % i, [128, 1024], BF16) for i in range(2)]
            pO_r = Ring("pO", [bkf[0][:, 0:128], bkf[1][:, 0:128]])
            pS_r = Ring("pS", [bkf[2][:, 0:256], bkf[3][:, 0:256], bkf[4][:, 0:256]])
            pGB_r = Ring("pGB", [bkf[5][:, 0:32]])
            pPT_r = Ring("pPT", [bkb[i][:, 0:256].rearrange("p (k c) -> p k c", k=2) for i in range(2)])
            pAT_r = Ring("pPT", [bkb[0][:, 512:640]])

            p = Prog()
            p.dma("sp", gbias[:], gbias_d, writes=["gbias"])
            p.dma("pool", trib[:, 128:256], trib_d, writes=["trib"])
            p.pool(lambda e: e.memset(trib[:, 0:128], 0.0), writes=["trib0"])

            def load_head(h):
                KT, kKT = KT_r.next()
                V, kV = V_r.next()
                QT, kQT = QT_r.next()
                p.dma("sp", KT[:], KaT_d[h], writes=[kKT])
                p.dma("sp", V[:], Va_d[:, :, h * 128:(h + 1) * 128].rearrange("t p c -> p t c"), writes=[kV])
                p.dma("sp", QT[:], QaT_d[h], writes=[kQT])
                return (KT, kKT, V, kV, QT, kQT)

            def prologue_pieces(hd):
                KT, kKT, V, kV, QT, kQT = hd
                QA, kQA = QA_r.next()
                kmb, kkmb = kmb_r.next()
                kabb, kkabb = kabb_r.next()
                ball, kball = ball_r.next()
                negm, knegm = negm_r.next()
                rsum, krsum = rsum_r.next()
                bufs = dict(QA=QA, kQA=kQA, ball=ball, kball=kball, negm=negm, knegm=knegm, rsum=rsum, krsum=krsum)
                pieces = []

                def head_ops():
                    p.dve(lambda e: e.tensor_reduce(ksum[:], KT[:].rearrange("p (n k) -> p n k", n=16), AX.X, ALU.add),
                          reads=[kKT], writes=["ksum"])
                    p.dve(lambda e: e.tensor_scalar(kmb[:], ksum[:], 1.0 / 256.0, None, ALU.mult),
                          reads=["ksum"], writes=[kkmb])
                    p.dve(lambda e: e.tensor_reduce(kab[:, 0:1], KT[:], AX.X, ALU.max, apply_absolute_value=True),
                          reads=[kKT], writes=["kab"])
                    p.dve(lambda e: e.tensor_copy(kab[:, 1:2], kab[:, 0:1]), reads=["kab"], writes=["kab1"])
                    p.dve(lambda e: e.tensor_scalar(kabb[:], kab[:], 1.02, None, ALU.mult),
                          reads=["kab", "kab1"], writes=[kkabb])
                    p.act(lambda e: e.activation(QA[:], QT[:], AF.Abs), reads=[kQT], writes=[kQA])
                pieces.append(head_ops)

                def tile_ops(j):
                    pGB, kpGB = pGB_r.next()
                    gb, kgb = gb_r.next()
                    t8, kt8 = t8_r.next()
                    thr, kthr = thr_r.next()
                    s01, ks01 = s01_r.next()
                    qs = slice(j * 128, (j + 1) * 128)
                    p.pe(lambda e: e.matmul(pGB[:, 0:16], QT[:, qs], kmb[:], start=True, stop=True),
                         reads=[kQT, kkmb], writes=[kpGB])
                    p.pe(lambda e: e.matmul(pGB[:, 16:18], QA[:, qs], kabb[:], start=True, stop=True),
                         reads=[kQA, kkabb], writes=[kpGB])
                    p.dve(lambda e: e.tensor_tensor(gb[:], pGB[:, 0:16], gbias[:, j, :], ALU.add),
                          reads=["gbias"], writes=[kgb, kpGB])
                    p.dve(lambda e: e.tensor_scalar(negm[:, j:j + 1], pGB[:, 16:17], -SCALE, None, ALU.mult),
                          reads=[], writes=[(knegm, j), kpGB])
                    p.dve(lambda e: e.max(t8[:], gb[:]), reads=[kgb], writes=[kt8])
                    p.dve(lambda e: e.tensor_scalar(thr[:], t8[:, 2:3], -1.0e29, None, ALU.max), reads=[kt8], writes=[kthr])
                    p.dve(lambda e: e.tensor_scalar(s01[:], gb[:], thr[:, 0:1], None, ALU.is_ge), reads=[kgb, kthr], writes=[ks01])
                    p.dve(lambda e: e.tensor_scalar(s01[:], s01[:], -1.0, BIG, ALU.add, ALU.mult), reads=[ks01], writes=[ks01])
                    p.dve(lambda e: e.tensor_scalar(ball[:, j, :], s01[:], negm[:, j:j + 1], None, ALU.add),
                          reads=[ks01, (knegm, j)], writes=[(kball, j)])
                for j_ in range(NTO):
                    pieces.append(lambda j_=j_: tile_ops(j_))
                return pieces, bufs

            nxt = load_head(0)
            pend, nxt_bufs = prologue_pieces(nxt)
            for pc_ in pend:
                pc_()
            pend = []
            evac_flip = [0]
            for h in range(8):
                KT, kKT, V, kV, QT, kQT = nxt
                cb_ = nxt_bufs
                ball, kball, negm, knegm, rsum, krsum = cb_["ball"], cb_["kball"], cb_["negm"], cb_["knegm"], cb_["rsum"], cb_["krsum"]
                if h + 1 < 8:
                    nxt = load_head(h + 1)
                    pend, nxt_bufs = prologue_pieces(nxt)

                items = []
                for j in range(NTO):
                    t = NTP + j
                    qb, half = t // 2, t % 2
                    blocks = list(range(qb + 1))
                    for bi, n in enumerate(blocks):
                        items.append(dict(j=j, n=n, own=(n == qb), half=half, first=(bi == 0),
                                          last=(bi == len(blocks) - 1), slot=bi))
                NI = len(items)
                pO_cur = {}

                def stage_S(it):
                    j, n = it["j"], it["n"]
                    nk = 128 if (it["own"] and it["half"] == 0) else 256
                    pS, kpS = pS_r.next()
                    it["pS"], it["kpS"], it["nk"] = pS, kpS, nk
                    qs = slice(j * 128, (j + 1) * 128)
                    own = it["own"]
                    p.pe(lambda e, QT=QT, KT=KT, pS=pS, qs=qs, n=n, nk=nk, own=own: e.matmul(
                        pS[:, 0:nk], QT[:, qs], KT[:, n * 256:n * 256 + nk], start=True, stop=(not own)),
                        reads=[kQT, kKT], writes=[kpS])
                    if own:
                        dc = it["half"] * 128
                        p.pe(lambda e, pS=pS, nk=nk: e.matmul(pS[:, 0:nk], ident[:], trib[:, 256 - nk:256], start=False, stop=True),
                             reads=["trib", "trib0"], writes=[kpS])

                def stage_E(it):
                    j, n, nk = it["j"], it["n"], it["nk"]
                    pS, kpS = it["pS"], it["kpS"]
                    Pb, kPb = Pb_r.next()
                    it["Pb"], it["kPb"] = Pb, kPb
                    if it["own"]:
                        bias_ap, bkey = negm[:, j:j + 1], (knegm, j)
                    else:
                        bias_ap, bkey = ball[:, j, n:n + 1], (kball, j)
                    p.act(lambda e, Pb=Pb, pS=pS, nk=nk, bias_ap=bias_ap, rsum=rsum, j=j, sl=it["slot"]: e.activation(
                        Pb[:, 0:nk], pS[:, 0:nk], AF.Exp, bias=bias_ap, scale=SCALE, accum_out=rsum[:, j, sl:sl + 1]),
                        reads=[bkey], writes=[kPb, (krsum, j, it["slot"]), kpS])

                def stage_T(it):
                    nk = it["nk"]
                    Pb, kPb = it["Pb"], it["kPb"]
                    pPT, kpPT = pPT_r.next()
                    it["pPT"], it["kpPT"] = pPT, kpPT
                    rd = it.get("kPb_parts", [kPb])
                    for kk in range(nk // 128):
                        p.pe(lambda e, QT=QT, KT=KT, V=V, ball=ball, negm=negm, rsum=rsum, pPT=pPT, Pb=Pb, kk=kk: e.transpose(pPT[:, kk, :], Pb[:, kk * 128:(kk + 1) * 128], ident[:]),
                             reads=rd, writes=[kpPT])

                def stage_V(it):
                    nkk = it["nk"] // 128
                    pPT, kpPT = it["pPT"], it["kpPT"]
                    PT, kPT = PT_r.next()
                    it["PT"], it["kPT"] = PT, kPT
                    if True:
                        p.dve(lambda e, QT=QT, KT=KT, V=V, ball=ball, negm=negm, rsum=rsum, PT=PT, pPT=pPT, nkk=nkk: e.tensor_copy(PT[:, 0:nkk, :], pPT[:, 0:nkk, :]),
                              writes=[kPT, kpPT])
                    else:
                        p.act(lambda e, QT=QT, KT=KT, V=V, ball=ball, negm=negm, rsum=rsum, PT=PT, pPT=pPT, nkk=nkk: e.activation(PT[:, 0:nkk, :], pPT[:, 0:nkk, :], AF.Copy),
                              writes=[kPT, kpPT])

                def stage_M(it):
                    j, n = it["j"], it["n"]
                    nkk = it["nk"] // 128
                    PT, kPT = it["PT"], it["kPT"]
                    if it["first"]:
                        pO_cur[j] = pO_r.next()
                    pO, kpO = pO_cur[j]
                    for kk in range(nkk):
                        p.pe(lambda e, QT=QT, KT=KT, V=V, ball=ball, negm=negm, rsum=rsum, pO=pO, PT=PT, kk=kk, n=n, st_=(it["first"] and kk == 0),
                             sp_=(it["last"] and kk == nkk - 1): e.matmul(pO[:], PT[:, kk, :], V[:, n * 2 + kk, :], start=st_, stop=sp_),
                             reads=[kPT, kV], writes=[kpO])
                    if it["last"]:
                        cnt = it["slot"] + 1
                        rtot, krtot = rtot_r.next()
                        ab, kab_ = ab_r.next()
                        rd = [(krsum, j, s_) for s_ in range(cnt)]

                        def f1(rtot=rtot, krtot=krtot, j=j, cnt=cnt, rd=rd, rsum=rsum):
                            p.dve(lambda e: e.tensor_reduce(rtot[:], rsum[:, j, 0:cnt], AX.X, ALU.add), reads=rd, writes=[krtot])
                            p.dve(lambda e: e.reciprocal(rtot[:], rtot[:]), reads=[krtot], writes=[krtot])

                        def f2(ab=ab, kab_=kab_, pO=pO, kpO=kpO, rtot=rtot, krtot=krtot):
                            p.act(lambda e: e.activation(ab[:], pO[:], AF.Copy, scale=rtot[:, 0:1]),
                                  reads=[krtot], writes=[kab_, kpO])

                        def f3(ab=ab, kab_=kab_, j=j, h=h):
                            pAT, kpAT = pAT_r.next()
                            p.pe(lambda e: e.transpose(pAT[:], ab[:], ident[:]), reads=[kab_], writes=[kpAT])
                            p.dve(lambda e: e.tensor_copy(mixT[:, h, j * 128:(j + 1) * 128], pAT[:]),
                                  writes=[("mixT", h, j), kpAT])
                        fin.append([1, f1])
                        fin.append([3, f2])
                        fin.append([5, f3])

                fin = []
                for i in range(NI + 4):
                    if pend and i % 10 == 9:
                        pend.pop(0)()
                    if i < NI:
                        stage_S(items[i])
                    if 0 <= i - 1 < NI:
                        stage_E(items[i - 1])
                    if 0 <= i - 3 < NI:
                        stage_T(items[i - 3])
                        stage_V(items[i - 3])
                    if 0 <= i - 4 < NI:
                        stage_M(items[i - 4])
                    keep_ = []
                    for ent in fin:
                        ent[0] -= 1
                        if ent[0] < 0:
                            ent[1]()
                        else:
                            keep_.append(ent)
                    fin[:] = keep_
                for ent in fin:
                    ent[1]()
                fin[:] = []
                while pend:
                    pend.pop(0)()
            if debug:
                for c in range(8):
                    p.dma("sp", mixT_d[c], mixT[:, c, :], reads=[("mixT", c, j) for j in range(NTO)], writes=[("mixTd", c)])
            run_prog(nc, p, G)

        if stages < 3:
            return nc

        with contextlib.ExitStack() as st:
            def sb(name, shape, dt):
                return st.enter_context(nc.sbuf_tensor("s3_" + name, shape, dt))

            def ps(name, shape, dt):
                return st.enter_context(nc.psum_tensor("p3_" + name, shape, dt))

            Kr_r = Ring("Kr", [sb("Kr%d" % i, [128, 2, 1024], BF16) for i in range(2)])
            Vr_r = Ring("Vr", [sb("Vr%d" % i, [128, 2, 1024], BF16) for i in range(2)])
            Qc_r = Ring("Qc", [sb("Qc%d" % i, [128, 8, 256], BF16) for i in range(2)])
            Kc_r = Ring("Kc", [sb("Kc%d" % i, [128, 8, 256], BF16) for i in range(2)])
            Sg_r = Ring("Sg", [sb("Sg%d" % i, [128, 2, 1024], F32) for i in range(2)])
            state = sb("state", [128, 8, 128], F32)
            sbf = [sb("sbf%d" % i, [128, 8, 128], BF16) for i in range(2)]
            st2_r = Ring("st2", [sb("st2%d" % i, [128, 128], F32) for i in range(8)])
            tri01 = sb("tri01", [128, 128], F32)
            gnw_t = sb("gnw", [128, 1024], F32)
            gnb_t = sb("gnb", [128, 1024], F32)
            gch = sb("gch", [128, 8], F32)
            epsT = sb("epsT", [128, 1], F32)
            NI3 = 16
            PTm_r = Ring("PTm", [sb("PTm%d" % i, [128, 256], BF16) for i in range(16)])
            ysb_r = Ring("ysb", [sb("ysb%d" % i, [128, 128], F32) for i in range(NI3)])
            bst_r = Ring("bst", [sb("bst%d" % i, [128, 6], F32) for i in range(NI3)])
            mv_r = Ring("mv", [sb("mv%d" % i, [128, 2], F32) for i in range(NI3)])
            rsd_r = Ring("rsd", [sb("rsd%d" % i, [128, 1], F32) for i in range(NI3)])
            rn_r = Ring("rn", [sb("rn%d" % i, [128, 128], F32) for i in range(NI3)])
            rbf_r = Ring("rbf", [sb("rbf%d" % i, [128, 128], BF16) for i in range(NI3)])
            pST_r = Ring("pST", [ps("pST%d" % i, [128, 256], F32) for i in range(2)])
            pY_r = Ring("pY", [ps("pY%d" % i, [128, 128], F32) for i in range(2)])
            pKV_r = Ring("pKV", [ps("pKV%d" % i, [128, 128], F32) for i in range(2)])
            pRT_r = Ring("pRT", [ps("pRT%d" % i, [128, 128], BF16) for i in range(2)])

            p = Prog()
            p.dma("sp", tri01[:], tri01_d, writes=["tri01"])
            p.dma("sp", gnw_t[:], gnw, writes=["gnw"])
            p.dma("sp", gnb_t[:], gnb, writes=["gnb"])
            p.dma("sp", gch[:], gch_d, writes=["gch"])
            p.pool(lambda e: e.memset(state[:], 0.0), writes=[("state", h) for h in range(8)])
            p.pool(lambda e: e.memset(sbf[0][:], 0.0), writes=[("sbf", 0, h) for h in range(8)])
            p.pool(lambda e: e.memset(epsT[:], 1e-5), writes=["eps"])
            for n in range(16):
                Kr, kKr = Kr_r.next()
                Vr, kVr = Vr_r.next()
                p.dma("sp", Kr[:], Kr_d[2 * n:2 * n + 2].rearrange("t p c -> p t c"), writes=[kKr])
                p.dma("sp", Vr[:], Vr_d[2 * n:2 * n + 2].rearrange("t p c -> p t c"), writes=[kVr])
                iset = []
                if n >= 7:
                    iset = [1] if n == 7 else [0, 1]
                    Qc, kQc = Qc_r.next()
                    Kc, kKc = Kc_r.next()
                    Sg, kSg = Sg_r.next()
                    if n == 7:
                        p.dma("sp", Qc[:, :, 128:256], QrT_d[:, :, 0:128].rearrange("h d t -> d h t"), writes=[kQc])
                        p.dma("sp", Sg[:, 1, :], Sg_d[0], writes=[kSg])
                    else:
                        j0 = 2 * n - NTP
                        p.dma("sp", Qc[:], QrT_d[:, :, j0 * 128:(j0 + 2) * 128].rearrange("h d t -> d h t"), writes=[kQc])
                        p.dma("sp", Sg[:], Sg_d[j0:j0 + 2].rearrange("t p c -> p t c"), writes=[kSg])
                    p.dma("sp", Kc[:], KrT_d[:, :, 2 * n * 128:(2 * n + 2) * 128].rearrange("h d t -> d h t"), writes=[kKc])
                sb_cur = sbf[n % 2]
                sb_nxt = sbf[(n + 1) % 2]
                its = []
                if iset:
                    PTs = {}
                    for h in range(8):
                        for jj in (0, 1):
                            qis = [i for i in iset if i >= jj]
                            q0 = min(qis) * 128
                            pST, kpST = pST_r.next()
                            PTm, kPTm = PTm_r.next()
                            PTs[(h, jj)] = (PTm, kPTm)
                            p.pe(lambda e, pST=pST, Kc=Kc, Qc=Qc, h=h, jj=jj, q0=q0: e.matmul(
                                pST[:, q0:256], Kc[:, h, jj * 128:(jj + 1) * 128], Qc[:, h, q0:256], start=True, stop=True),
                                reads=[kKc, kQc], writes=[kpST])
                            for i in qis:
                                cs = slice(i * 128, (i + 1) * 128)
                                if i == jj:
                                    p.dve(lambda e, PTm=PTm, pST=pST, cs=cs: e.tensor_tensor(PTm[:, cs], pST[:, cs], tri01[:], ALU.mult),
                                          reads=["tri01"], writes=[(kPTm, i), kpST])
                                else:
                                    p.act(lambda e, PTm=PTm, pST=pST, cs=cs: e.activation(PTm[:, cs], pST[:, cs], AF.Copy),
                                          writes=[(kPTm, i), kpST])
                    for h in range(8):
                        hs = slice(h * 128, (h + 1) * 128)
                        for i in iset:
                            j = 2 * n + i - NTP
                            cs = slice(i * 128, (i + 1) * 128)
                            pY, kpY = pY_r.next()
                            first = True
                            for jj in range(i + 1):
                                PTm, kPTm = PTs[(h, jj)]
                                p.pe(lambda e, pY=pY, PTm=PTm, Vr=Vr, cs=cs, jj=jj, hs=hs, first=first: e.matmul(
                                    pY[:], PTm[:, cs], Vr[:, jj, hs], start=first, stop=False),
                                    reads=[(kPTm, i), kVr], writes=[kpY])
                                first = False
                            p.pe(lambda e, pY=pY, Qc=Qc, cs=cs, h=h, sb_cur=sb_cur: e.matmul(
                                pY[:], Qc[:, h, cs], sb_cur[:, h, :], start=False, stop=True),
                                reads=[kQc, ("sbf", n % 2, h)], writes=[kpY])
                            it = dict(h=h, i=i, j=j, hs=hs)
                            it["ysb"], it["kysb"] = ysb_r.next()
                            it["bst"], it["kbst"] = bst_r.next()
                            it["mv"], it["kmv"] = mv_r.next()
                            it["rsd"], it["krsd"] = rsd_r.next()
                            it["rn"], it["krn"] = rn_r.next()
                            it["rbf"], it["krbf"] = rbf_r.next()
                            p.act(lambda e, ysb=it["ysb"], pY=pY: e.activation(ysb[:], pY[:], AF.Copy), writes=[it["kysb"], kpY])
                            its.append(it)
                if n < 15:
                    kvs = []
                    for h in range(8):
                        hs = slice(h * 128, (h + 1) * 128)
                        pKV, kpKV = pKV_r.next()
                        st2, kst2 = st2_r.next()
                        for jj in (0, 1):
                            p.pe(lambda e, pKV=pKV, Kr=Kr, Vr=Vr, jj=jj, hs=hs: e.matmul(
                                pKV[:], Kr[:, jj, hs], Vr[:, jj, hs], start=(jj == 0), stop=(jj == 1)),
                                reads=[kKr, kVr], writes=[kpKV])
                        p.dve(lambda e, st2=st2, pKV=pKV, h=h: e.tensor_tensor(st2[:], state[:, h, :], pKV[:], ALU.add),
                              reads=[("state", h)], writes=[kst2, kpKV])
                        kvs.append((h, st2, kst2))
                    for h, st2, kst2 in kvs:
                        p.act(lambda e, st2=st2, h=h: e.activation(state[:, h, :], st2[:], AF.Copy, scale=gch[:, h:h + 1]),
                              reads=[kst2, "gch"], writes=[("state", h)])
                    for h, st2, kst2 in kvs:
                        p.pool(lambda e, sb_nxt=sb_nxt, h=h: e.tensor_copy(sb_nxt[:, h, :], state[:, h, :]),
                               reads=[("state", h)], writes=[("sbf", (n + 1) % 2, h)])
                for it in its:
                    p.dve(lambda e, bst=it["bst"], ysb=it["ysb"]: e.bn_stats(bst[:], ysb[:]), reads=[it["kysb"]], writes=[it["kbst"]])
                for it in its:
                    p.dve(lambda e, mv=it["mv"], bst=it["bst"]: e.bn_aggr(mv[:], bst[:]), reads=[it["kbst"]], writes=[it["kmv"]])
                for it in its:
                    p.act(lambda e, rsd=it["rsd"], mv=it["mv"]: e.activation(rsd[:], mv[:, 1:2], AF.Sqrt, bias=epsT[:, 0:1]),
                          reads=[it["kmv"], "eps"], writes=[it["krsd"]])
                for it in its:
                    p.dve(lambda e, rsd=it["rsd"]: e.reciprocal(rsd[:], rsd[:]), reads=[it["krsd"]], writes=[it["krsd"]])
                for it in its:
                    p.dve(lambda e, rn=it["rn"], ysb=it["ysb"], mv=it["mv"], rsd=it["rsd"]: e.tensor_scalar(
                        rn[:], ysb[:], mv[:, 0:1], rsd[:, 0:1], ALU.subtract, ALU.mult),
                        reads=[it["kysb"], it["kmv"], it["krsd"]], writes=[it["krn"]])
                for it in its:
                    p.pool(lambda e, rn=it["rn"], hs=it["hs"]: e.tensor_tensor(rn[:], rn[:], gnw_t[:, hs], ALU.mult),
                           reads=[it["krn"], "gnw"], writes=[it["krn"]])
                for it in its:
                    p.pool(lambda e, rn=it["rn"], hs=it["hs"]: e.tensor_tensor(rn[:], rn[:], gnb_t[:, hs], ALU.add),
                           reads=[it["krn"], "gnb"], writes=[it["krn"]])
                for it in its:
                    p.pool(lambda e, rbf=it["rbf"], rn=it["rn"], Sg=Sg, i=it["i"], hs=it["hs"]: e.tensor_tensor(
                        rbf[:], rn[:], Sg[:, i, hs], ALU.mult),
                        reads=[it["krn"], kSg], writes=[it["krbf"]])
                for it in its:
                    pRT, kpRT = pRT_r.next()
                    p.pe(lambda e, pRT=pRT, rbf=it["rbf"]: e.transpose(pRT[:], rbf[:], ident[:]), reads=[it["krbf"]], writes=[kpRT])
                    p.act(lambda e, pRT=pRT, h=it["h"], j=it["j"]: e.activation(
                        mixT[:, 8 + h, j * 128:(j + 1) * 128], pRT[:], AF.Copy),
                        writes=[("mixT", 8 + it["h"], it["j"]), kpRT])
            if debug:
                for c in range(16):
                    p.dma("sp", mixT_d[c], mixT[:, c, :], reads=[("mixT", c, j) for j in range(NTO)], writes=[("mixTd", c)])
            run_prog(nc, p, G)

        if stages < 4:
            return nc

        with contextlib.ExitStack() as st:
            def sb(name, shape, dt):
                return st.enter_context(nc.sbuf_tensor("s4_" + name, shape, dt))

            def ps(name, shape, dt):
                return st.enter_context(nc.psum_tensor("p4_" + name, shape, dt))

            Wo = [sb("Wo%d" % g, [128, 16, 512], BF16) for g in range(4)]
            x1t_r = Ring("x1t", [sb("x1t%d" % i, [128, D], F32) for i in range(2)])
            xt_r = Ring("xt", [sb("xt%d" % i, [128, D], F32) for i in range(2)])
            junk = sb("junk", [128, 512], BF16)
            wbc = sb("wbc", [128, D], F32)
            ss4 = sb("ss4", [128, NTO, 4], F32)
            rs = sb("rs", [128, NTO], F32)
            pM = [ps("pM%d" % i, [128, 512], F32) for i in range(8)]
            p = Prog()
            p.dma("sp", wbc[:], nw[1], writes=["wbc"])
            for g in range(4):
                p.dma("pool", Wo[g][:], w_out_v[:, :, g * 512:(g + 1) * 512], writes=[("Wo", g)])
            for j in range(NTO):
                xt, kxt = xt_r.next()
                x1t, kx1 = x1t_r.next()
                p.dma("sp", xt[:], xin[NTP + j], writes=[kxt])
                banks = [(pM[(j % 2) * 4 + g], ("pM", (j % 2) * 4 + g)) for g in range(4)]
                for g in range(4):
                    pb, kpb = banks[g]
                    for kc in range(16):
                        p.pe(lambda e, pb=pb, g=g, kc=kc, j=j: e.matmul(
                            pb[:], mixT[:, kc, j * 128:(j + 1) * 128], Wo[g][:, kc, :], start=(kc == 0), stop=(kc == 15)),
                            reads=[("Wo", g)], writes=[kpb])
                for g in range(4):
                    pb, kpb = banks[g]
                    p.act(lambda e, pb=pb, j=j, g=g: e.activation(junk[:], pb[:], AF.Square, accum_out=ss4[:, j, g:g + 1]),
                          writes=["junk", ("ss4", j, g), kpb])
                p.dve(lambda e, j=j: e.tensor_reduce(rs[:, j:j + 1], ss4[:, j, :], AX.X, ALU.add),
                      reads=[("ss4", j, g) for g in range(4)], writes=[("rs", j)])
                p.dve(lambda e, j=j: e.tensor_scalar(rs[:, j:j + 1], rs[:, j:j + 1], 1.0 / D, 1e-6, ALU.mult, ALU.add),
                      reads=[("rs", j)], writes=[("rs", j)])
                p.act(lambda e, j=j: e.activation(rs[:, j:j + 1], rs[:, j:j + 1], AF.Sqrt), reads=[("rs", j)], writes=[("rs", j)])
                p.dve(lambda e, j=j: e.reciprocal(rs[:, j:j + 1], rs[:, j:j + 1]), reads=[("rs", j)], writes=[("rs", j)])
                for g in range(4):
                    pb, kpb = banks[g]
                    gs = slice(g * 512, (g + 1) * 512)
                    p.dve(lambda e, pb=pb, x1t=x1t, gs=gs, j=j: e.scalar_tensor_tensor(
                        x1t[:, gs], pb[:], rs[:, j:j + 1], wbc[:, gs], ALU.mult, ALU.mult),
                        reads=[("rs", j), "wbc"], writes=[(kx1, g), kpb])
                p.pool(lambda e, x1t=x1t, xt=xt: e.tensor_tensor(x1t[:], x1t[:], xt[:], ALU.add),
                       reads=[kxt] + [(kx1, g) for g in range(4)], writes=[(kx1, "f")])
                p.dma("sp", x1_d[j], x1t[:], reads=[(kx1, "f")], writes=[("x1d", j)] + [(kx1, g) for g in range(4)])
            run_prog(nc, p, G)

        mst.close()
        if stages < 5:
            return nc

        with contextlib.ExitStack() as st:
            def sb(name, shape, dt):
                return st.enter_context(nc.sbuf_tensor("s5_" + name, shape, dt))

            def ps(name, shape, dt):
                return st.enter_context(nc.psum_tensor("p5_" + name, shape, dt))

            wbc2 = sb("wbc2", [128, D], F32)
            wbc3 = sb("wbc3", [128, D], F32)
            identf = sb("identf", [128, 128], F32)
            x1_r = Ring("x1t", [sb("x1t%d" % i, [128, D], F32) for i in range(2)])
            junk = sb("junk", [128, D], BF16)
            xn = sb("xn", [128, D], F32)
            uT = sb("uT", [128, 16, 512], BF16)
            uTh = sb("uTh", [128, 16, 2], BF16)
            hT = sb("hT", [128, NCT, 512], BF16)
            Wg_r = Ring("Wg", [sb("Wg%d" % i, [128, 16, 256], BF16) for i in range(3)])
            Wv_r = Ring("Wv", [sb("Wv%d" % i, [128, 16, 256], BF16) for i in range(3)])
            Wd_r = Ring("Wd", [sb("Wd%d" % i, [128, 4, 512], BF16) for i in range(3)])
            ug_r = Ring("ug", [sb("ug%d" % i, [128, 514], F32) for i in range(2)])
            tA_r = Ring("tA", [sb("tA%d" % i, [128, 512], F32) for i in range(2)])
            carry = sb("carry", [128, NCT, 2], F32)
            fo = sb("fo", [128, 4, D], F32)
            cw = sb("cw", [128, NCT, 3], F32)
            cbt = sb("cbt", [128, NCT], F32)
            ss = sb("ss", [128, 64], F32)
            rs = sb("rs", [128, 64], F32)
            pD = [ps("pD%d" % i, [128, 512], F32) for i in range(4)]
            pG_r = Ring("pG", [ps("pG%d" % i, [128, 512], F32) for i in range(2)])
            pV_r = Ring("pV", [ps("pV%d" % i, [128, 512], F32) for i in range(2)])

            p = Prog()
            p.dma("sp", wbc2[:], nw[2], writes=["wbc2"])
            p.dma("sp", wbc3[:], nw[3], writes=["wbc3"])
            p.dma("sp", identf[:], ident_d, writes=["identf"])
            p.dma("sp", cw[:], convw, writes=["cw"])
            p.dma("sp", cbt[:], convb, writes=["cbt"])
            sidx = [0]

            def rstd_ops(src_ap, src_keys):
                k = sidx[0]
                sidx[0] += 1
                p.act(lambda e, k=k: e.activation(junk[:], src_ap, AF.Square, accum_out=ss[:, k:k + 1]),
                      reads=src_keys, writes=["junk", ("ss", k)])
                p.dve(lambda e, k=k: e.tensor_scalar(rs[:, k:k + 1], ss[:, k:k + 1], 1.0 / D, 1e-6, ALU.mult, ALU.add),
                      reads=[("ss", k)], writes=[("rs", k)])
                p.act(lambda e, k=k: e.activation(rs[:, k:k + 1], rs[:, k:k + 1], AF.Sqrt), reads=[("rs", k)], writes=[("rs", k)])
                p.dve(lambda e, k=k: e.reciprocal(rs[:, k:k + 1], rs[:, k:k + 1]), reads=[("rs", k)], writes=[("rs", k)])
                return k, ("rs", k)

            def u_prep(j, halo, tt):
                x1t, kx1 = x1_r.next()
                p.dma("sp", x1t[:], x1_d[j], writes=[kx1])
                k, krs = rstd_ops(x1t[:], [kx1])
                p.dve(lambda e, x1t=x1t, k=k: e.scalar_tensor_tensor(xn[:], x1t[:], rs[:, k:k + 1], wbc2[:], ALU.mult, ALU.mult),
                      reads=[kx1, krs, "wbc2"], writes=["xn"])
                for kc in range(16):
                    p.pe(lambda e, kc=kc: e.transpose(pD[kc // 4][:, (kc % 4) * 128:(kc % 4 + 1) * 128],
                                                      xn[:, kc * 128:(kc + 1) * 128], identf[:]),
                         reads=["xn", "identf"], writes=[("pD", kc // 4)])
                for b in range(4):
                    src = pD[b][:].rearrange("p (k c) -> p k c", k=4)
                    if halo:
                        dst = uTh[:, b * 4:(b + 1) * 4, :]
                        srcv = src[:, :, 126:128]
                        wk = ("uTh", b)
                    else:
                        dst = uT[:, b * 4:(b + 1) * 4, tt * 128:(tt + 1) * 128]
                        srcv = src
                        wk = ("uT", tt, b)
                    if b % 2 == 0:
                        p.act(lambda e, dst=dst, srcv=srcv: e.activation(dst, srcv, AF.Copy), writes=[wk, ("pD", b)])
                    else:
                        p.dve(lambda e, dst=dst, srcv=srcv: e.tensor_copy(dst, srcv), writes=[wk, ("pD", b)])

            up_src, dn_src = [], []
            for gi in range(4):
                for cg in range(NCT // 2):
                    up_src.append((w_up_v[:, :, cg * 256:(cg + 1) * 256], w_up_v[:, :, DFF + cg * 256:DFF + (cg + 1) * 256]))
                for ng in range(4):
                    for pc in range(NCT // 4):
                        dn_src.append(w_down_v[:, pc * 4:(pc + 1) * 4, ng * 512:(ng + 1) * 512])
            Gpf = Prefetch(p, Wg_r, [a for a, _ in up_src])
            Vpf = Prefetch(p, Wv_r, [b for _, b in up_src])
            Dpf = Prefetch(p, Wd_r, dn_src)
            for gi in range(4):
                if gi == 0:
                    u_prep(0, True, 0)
                for tt in range(4):
                    u_prep(1 + 4 * gi + tt, False, tt)
                uT_keys = [("uT", tt, b) for tt in range(4) for b in range(4)]
                uTh_keys = [("uTh", b) for b in range(4)]
                Dpf.ensure(gi * 44 + Dpf.R - 1)
                for cg in range(NCT // 2):
                    ku = gi * (NCT // 2) + cg
                    Gpf.ensure(ku + Gpf.R - 1)
                    Vpf.ensure(ku + Vpf.R - 1)
                    Wg, kWg = Gpf.get(ku)
                    Wv, kWv = Vpf.get(ku)
                    for ci in range(2):
                        ct = cg * 2 + ci
                        cs = slice(ci * 128, (ci + 1) * 128)
                        ug, kug = ug_r.next()
                        tA, ktA = tA_r.next()
                        pG, kpG = pG_r.next()
                        pV, kpV = pV_r.next()
                        if gi == 0:
                            for kc in range(16):
                                p.pe(lambda e, pV=pV, Wg=Wg, kc=kc, cs=cs: e.matmul(
                                    pV[:, 0:2], Wg[:, kc, cs], uTh[:, kc, :], start=(kc == 0), stop=(kc == 15)),
                                    reads=[kWg] + uTh_keys, writes=[kpV])
                            p.act(lambda e, ug=ug, pV=pV: e.activation(ug[:, 0:2], pV[:, 0:2], AF.Copy), writes=[(kug, "h"), kpV])
                        else:
                            p.act(lambda e, ug=ug, ct=ct: e.activation(ug[:, 0:2], carry[:, ct, :], AF.Copy),
                                  reads=[("carry", ct)], writes=[(kug, "h")])
                        for kc in range(16):
                            p.pe(lambda e, pG=pG, Wg=Wg, kc=kc, cs=cs: e.matmul(
                                pG[:], Wg[:, kc, cs], uT[:, kc, :], start=(kc == 0), stop=(kc == 15)),
                                reads=[kWg] + uT_keys, writes=[kpG])
                        p.act(lambda e, ug=ug, pG=pG: e.activation(ug[:, 2:514], pG[:], AF.Copy), writes=[(kug, "m"), kpG])
                        if gi < 3:
                            p.act(lambda e, ug=ug, ct=ct: e.activation(carry[:, ct, :], ug[:, 512:514], AF.Copy),
                                  reads=[(kug, "m")], writes=[("carry", ct)])
                        for kc in range(16):
                            p.pe(lambda e, pV=pV, Wv=Wv, kc=kc, cs=cs: e.matmul(
                                pV[:], Wv[:, kc, cs], uT[:, kc, :], start=(kc == 0), stop=(kc == 15)),
                                reads=[kWv] + uT_keys, writes=[kpV])
                        ugk = [(kug, "h"), (kug, "m")]
                        p.dve(lambda e, tA=tA, ug=ug, ct=ct: e.tensor_scalar(tA[:], ug[:, 0:512], cw[:, ct, 0:1], None, ALU.mult),
                              reads=ugk + ["cw"], writes=[ktA])
                        p.dve(lambda e, tA=tA, ug=ug, ct=ct: e.scalar_tensor_tensor(
                            tA[:], ug[:, 1:513], cw[:, ct, 1:2], tA[:], ALU.mult, ALU.add),
                            reads=ugk + ["cw", ktA], writes=[ktA])
                        p.dve(lambda e, tA=tA, ug=ug, ct=ct: e.scalar_tensor_tensor(
                            tA[:], ug[:, 2:514], cw[:, ct, 2:3], tA[:], ALU.mult, ALU.add),
                            reads=ugk + ["cw", ktA], writes=[ktA])
                        p.act(lambda e, tA=tA, ct=ct: e.activation(tA[:], tA[:], AF.Silu, bias=cbt[:, ct:ct + 1]),
                              reads=[ktA, "cbt"], writes=[ktA])
                        p.dve(lambda e, tA=tA, pV=pV, ct=ct: e.tensor_tensor(hT[:, ct, :], tA[:], pV[:], ALU.mult),
                              reads=[ktA], writes=[("hT", ct), kpV])
                hT_keys = [("hT", ct) for ct in range(NCT)]
                if gi < 3:
                    Gpf.ensure((gi + 1) * (NCT // 2) + Gpf.R - 2)
                    Vpf.ensure((gi + 1) * (NCT // 2) + Vpf.R - 2)
                for ng in range(4):
                    for pc in range(NCT // 4):
                        Wd, kWd = Dpf.get(gi * 44 + ng * 11 + pc)
                        for cc in range(4):
                            ct = pc * 4 + cc
                            for tt in range(4):
                                p.pe(lambda e, tt=tt, ct=ct, cc=cc, Wd=Wd: e.matmul(
                                    pD[tt][:], hT[:, ct, tt * 128:(tt + 1) * 128], Wd[:, cc, :],
                                    start=(ct == 0), stop=(ct == NCT - 1)),
                                    reads=[("hT", ct), kWd], writes=[("pD", tt)])
                    for tt in range(4):
                        p.act(lambda e, tt=tt, ng=ng: e.activation(fo[:, tt, ng * 512:(ng + 1) * 512], pD[tt][:], AF.Copy),
                              writes=[("fo", tt, ng), ("pD", tt)])
                for tt in range(4):
                    j = 1 + 4 * gi + tt
                    fok = [("fo", tt, ng) for ng in range(4)]
                    x1t, kx1 = x1_r.next()
                    p.dma("sp", x1t[:], x1_d[j], writes=[kx1])
                    k, krs = rstd_ops(fo[:, tt, :], fok)
                    p.dve(lambda e, tt=tt, k=k: e.scalar_tensor_tensor(
                        fo[:, tt, :], fo[:, tt, :], rs[:, k:k + 1], wbc3[:], ALU.mult, ALU.mult),
                        reads=[krs, "wbc3"], writes=fok)
                    p.dve(lambda e, tt=tt, x1t=x1t: e.tensor_tensor(fo[:, tt, :], fo[:, tt, :], x1t[:], ALU.add),
                          reads=[kx1], writes=fok)
                    p.dma("sp", out[j - 1], fo[:, tt, :], reads=fok, writes=[("out", j)])
            run_prog(nc, p, G)
    return nc


def _tables(half):
    start = half * 2048 - 2048
    pos = (start + np.arange(4096)).astype(np.float64)
    pos = np.maximum(pos, 0.0).astype(np.float32)
    inv = (1.0 / (np.float32(10000.0) ** (np.arange(0, 128, 2, dtype=np.float32) / np.float32(128)))).astype(np.float32)
    ang = pos[:, None] * inv[None, :]
    cos = np.cos(ang).astype(np.float32).reshape(32, 128, 64).transpose(1, 0, 2)
    sin = np.sin(ang).astype(np.float32).reshape(32, 128, 64).transpose(1, 0, 2)
    hidx = np.arange(8, dtype=np.float32)
    log_g = np.log(1.0 - 2.0 ** (-5.0 - hidx)).astype(np.float64)
    pp = (np.arange(256, dtype=np.float64) + 1.0).reshape(2, 128).T
    qfac = np.exp(log_g[None, None, :] * pp[:, :, None]).astype(np.float32)
    kfac = (np.exp(-log_g[None, None, :] * pp[:, :, None]) * (128.0 ** -0.5)).astype(np.float32)
    gch = np.broadcast_to(np.exp(log_g * 256.0).astype(np.float32)[None, :], (128, 8)).copy()
    kk = np.arange(128)
    tri01 = (kk[:, None] <= kk[None, :]).astype(np.float32)
    trib = np.where(kk[None, :] <= kk[:, None], 0.0, -BIG).astype(np.float32)
    gb = np.full((128, NTO, 16), NEG, dtype=np.float32)
    for j in range(NTO):
        qb = (NTP + j) // 2
        for n in range(qb):
            if half == 1 or n >= 8:
                gb[:, j, n] = 0.0
    return dict(cos_t=np.ascontiguousarray(cos), sin_t=np.ascontiguousarray(sin), qfac=qfac, kfac=kfac,
                gch=gch, tri01=tri01, trib=trib, gbias=gb, ident=np.eye(128, dtype=np.float32))


def make_in_maps(x, norm_mix_pre, w_in, ret_gn_w, ret_gn_b, w_out, norm_mix_post, norm_ffn_pre,
                 w_up, conv_w, conv_b, w_down, norm_ffn_post):
    f = np.float32
    x = np.asarray(x, f)
    shared = dict(
        w_in=np.ascontiguousarray(np.asarray(w_in, f)[0]),
        w_out=np.ascontiguousarray(np.asarray(w_out, f)[0]),
        w_up=np.ascontiguousarray(np.asarray(w_up, f)[0]),
        w_down=np.ascontiguousarray(np.asarray(w_down, f)[0]),
        nw0=np.ascontiguousarray(np.broadcast_to(np.asarray(norm_mix_pre, f)[0][None, :], (128, D))),
        nw1=np.ascontiguousarray(np.broadcast_to(np.asarray(norm_mix_post, f)[0][None, :], (128, D))),
        nw2=np.ascontiguousarray(np.broadcast_to(np.asarray(norm_ffn_pre, f)[0][None, :], (128, D))),
        nw3=np.ascontiguousarray(np.broadcast_to(np.asarray(norm_ffn_post, f)[0][None, :], (128, D))),
        gnw=np.ascontiguousarray(np.broadcast_to(np.asarray(ret_gn_w, f)[0][None, :], (128, 1024))),
        gnb=np.ascontiguousarray(np.broadcast_to(np.asarray(ret_gn_b, f)[0][None, :], (128, 1024))),
        convw=np.ascontiguousarray(np.asarray(conv_w, f)[0].reshape(3, NCT, 128).transpose(2, 1, 0)),
        convb=np.ascontiguousarray(np.asarray(conv_b, f)[0].reshape(NCT, 128).T),
    )
    tabs = [_tables(0), _tables(1)]
    maps = []
    for c in range(8):
        b, half = c // 2, c % 2
        xi = np.zeros((4096, D), f)
        if half == 0:
            xi[2048:] = x[b, :2048]
        else:
            xi[:] = x[b]
        m = dict(shared)
        m.update(tabs[half])
        m["xin"] = xi.reshape(32, 128, D)
        maps.append(m)
    return maps


_NC_CACHE = {}


def kernel(**inputs):
    if "nc" not in _NC_CACHE:
        _NC_CACHE["nc"] = build_nc()
    nc = _NC_CACHE["nc"]
    maps = make_in_maps(**inputs)
    res = run_bass_kernel_spmd(nc, maps, core_ids=list(range(8)))
    outp = np.empty((4, 4096, D), np.float32)
    for c in range(8):
        b, half = c // 2, c % 2
        outp[b, half * 2048:(half + 1) * 2048] = np.asarray(res.results[c]["out"]).reshape(2048, D)
    return outp
```
